# Optimizing a Trainium2 kernel written in Bass

```python
import math
import jax
import jax.numpy as jnp
from jax import lax
import numpy as np

D_MODEL = 1024
BATCH = 4
SEQ = 4096
DEPTH = 4

N_MIXERS = 2
EPS = 1e-6

GDN_HEADS = 8
GDN_DK = D_MODEL // GDN_HEADS
GDN_DV = D_MODEL // GDN_HEADS
GDN_CONV = 4
GDN_CHUNK = 64

DSW_PATTERNS = ((128, 1), (512, 4), (2048, 16))
DSW_GROUPS = len(DSW_PATTERNS)
DSW_HEADS = 8
DSW_HEAD_DIM = D_MODEL // DSW_HEADS
DSW_BLOCK = 128
ROPE_THETA = 500000.0
ROPE_DIM = DSW_HEAD_DIM // 4

FFN_HIDDEN = ((8 * D_MODEL + 2) // 3 + 255) // 256 * 256

kernel_name = 'hybrid_gdn_dilated_swa_adaln_block'


def rms_norm(x, g):
    xf = x.astype(jnp.float32)
    y = xf * lax.rsqrt(jnp.mean(xf * xf, axis=-1, keepdims=True) + EPS)
    return (y * g.astype(jnp.float32)).astype(x.dtype)


def l2_norm(x):
    xf = x.astype(jnp.float32)
    return xf * lax.rsqrt(jnp.sum(xf * xf, axis=-1, keepdims=True) + EPS)


def partial_rope(x, pos):
    half = ROPE_DIM // 2
    inv = jnp.exp(-math.log(ROPE_THETA) * (2.0 * jnp.arange(half, dtype=jnp.float32) / ROPE_DIM))
    ang = pos.astype(jnp.float32)[:, None] * inv[None, :]
    cos = jnp.cos(ang)[None, :, None, :]
    sin = jnp.sin(ang)[None, :, None, :]
    x1 = x[..., :half]
    x2 = x[..., half:ROPE_DIM]
    return jnp.concatenate([x1 * cos - x2 * sin, x2 * cos + x1 * sin, x[..., ROPE_DIM:]], axis=-1)


def causal_dwconv(x, w):
    K, C = w.shape
    return lax.conv_general_dilated(
        x, w[:, None, :].astype(x.dtype), window_strides=(1,), padding=((K - 1, 0),),
        dimension_numbers=('NWC', 'WIO', 'NWC'), feature_group_count=C)


def chunk_gated_delta_rule(q, k, v, g, beta):
    B, T, H, dk = q.shape
    dv = v.shape[-1]
    C = GDN_CHUNK
    N = T // C

    def to_chunks(t):
        return jnp.moveaxis(t.reshape(B, N, C, H, t.shape[-1]), 3, 1)

    q, k, v = to_chunks(q), to_chunks(k), to_chunks(v)
    g = jnp.moveaxis(g.reshape(B, N, C, H), 3, 1)
    beta = jnp.moveaxis(beta.reshape(B, N, C, H), 3, 1)
    g = jnp.cumsum(g, axis=-1)
    causal = jnp.tril(jnp.ones((C, C), dtype=bool))
    strict = jnp.tril(jnp.ones((C, C), dtype=bool), -1)
    decay = jnp.exp(jnp.where(causal, g[..., :, None] - g[..., None, :], -jnp.inf))
    k_beta = k * beta[..., None]
    v_beta = v * beta[..., None]
    L = jnp.einsum('bhnid,bhnjd->bhnij', k_beta, k) * decay
    A = jnp.eye(C, dtype=jnp.float32) + jnp.where(strict, L, 0.0)
    rhs = jnp.concatenate([v_beta, k_beta * jnp.exp(g)[..., None]], axis=-1)
    sol = lax.linalg.triangular_solve(A, rhs, left_side=True, lower=True, unit_diagonal=True)
    u = sol[..., :dv]
    w = sol[..., dv:]
    attn_intra = jnp.where(causal, jnp.einsum('bhnid,bhnjd->bhnij', q, k) * decay, 0.0)
    q_decay = q * jnp.exp(g)[..., None]
    k_tail = k * jnp.exp(g[..., -1:] - g)[..., None]
    g_last = jnp.exp(g[..., -1])

    def step(S, xs):
        qd, kt, a_in, u_c, w_c, gl = xs
        v_new = u_c - jnp.einsum('bhcd,bhde->bhce', w_c, S)
        o = jnp.einsum('bhcd,bhde->bhce', qd, S) + jnp.einsum('bhij,bhje->bhie', a_in, v_new)
        S = S * gl[..., None, None] + jnp.einsum('bhcd,bhce->bhde', kt, v_new)
        return S, o

    xs = tuple(jnp.moveaxis(t, 2, 0) for t in (q_decay, k_tail, attn_intra, u, w, g_last))
    S0 = jnp.zeros((B, H, dk, dv), jnp.float32)
    _, o = lax.scan(step, S0, xs)
    o = jnp.moveaxis(o, 0, 2)
    return jnp.moveaxis(o, 1, 3).reshape(B, T, H, dv)


def gated_deltanet(h, w_in, conv_w, A_log, dt_bias, norm_g, w_out):
    B, T, _ = h.shape
    H, dk, dv = GDN_HEADS, GDN_DK, GDN_DV
    nq, nv = H * dk, H * dv
    proj = h @ w_in
    qkv = jax.nn.silu(causal_dwconv(proj[..., :2 * nq + nv], conv_w))
    z = proj[..., 2 * nq + nv:2 * nq + 2 * nv].reshape(B, T, H, dv)
    b = proj[..., 2 * nq + 2 * nv:2 * nq + 2 * nv + H].astype(jnp.float32)
    a = proj[..., 2 * nq + 2 * nv + H:].astype(jnp.float32)
    q = l2_norm(qkv[..., :nq].reshape(B, T, H, dk)) * (dk ** -0.5)
    k = l2_norm(qkv[..., nq:2 * nq].reshape(B, T, H, dk))
    v = qkv[..., 2 * nq:].reshape(B, T, H, dv).astype(jnp.float32)
    beta = jax.nn.sigmoid(b)
    g = -jnp.exp(A_log.astype(jnp.float32)) * jax.nn.softplus(a + dt_bias.astype(jnp.float32))
    o = chunk_gated_delta_rule(q, k, v, g, beta)
    o = rms_norm(o, norm_g) * jax.nn.silu(z.astype(jnp.float32))
    return o.reshape(B, T, nv).astype(h.dtype) @ w_out


def dilated_group_attention(q, k, v, dilation, span):
    B, T, H, dh = q.shape
    d = dilation
    Ts = T // d
    P = DSW_BLOCK
    nb = -(-Ts // P)
    Lp = nb * P

    def streams(t):
        t = jnp.swapaxes(t.reshape(B, Ts, d, H, dh), 1, 2)
        t = jnp.pad(t, ((0, 0), (0, 0), (0, Lp - Ts), (0, 0), (0, 0)))
        return t.reshape(B, d, nb, P, H, dh)

    def with_prev(t):
        prev = jnp.pad(t, ((0, 0), (0, 0), (1, 0), (0, 0), (0, 0), (0, 0)))[:, :, :-1]
        return jnp.concatenate([prev, t], axis=3)

    qs = streams(q)
    kb = with_prev(streams(k))
    vb = with_prev(streams(v))
    s = jnp.einsum('bgnqhe,bgnkhe->bgnhqk', qs, kb) * (dh ** -0.5)
    qi = jnp.arange(P)[:, None] + P
    ki = jnp.arange(2 * P)[None, :]
    rel = qi - ki
    band = (rel >= 0) & (rel <= span)
    blk = jnp.arange(nb)[:, None, None]
    valid = band[None] & ((blk > 0) | (ki[None] >= P))
    s = jnp.where(valid[:, None], s, -jnp.inf)
    lse = jax.nn.logsumexp(s, axis=-1)
    p = jnp.exp(s - lse[..., None])
    o = jnp.einsum('bgnhqk,bgnkhe->bgnqhe', p, vb)
    o = o.reshape(B, d, Lp, H, dh)[:, :, :Ts]
    o = jnp.swapaxes(o, 1, 2).reshape(B, T, H, dh)
    lse = jnp.swapaxes(lse, 3, 4).reshape(B, d, Lp, H)[:, :, :Ts]
    lse = jnp.swapaxes(lse, 1, 2).reshape(B, T, H)
    return o, lse


def dilated_window_attention(h, w_in, q_norm_g, k_norm_g, w_out):
    B, T, _ = h.shape
    G, H, dh = DSW_GROUPS, DSW_HEADS, DSW_HEAD_DIM
    proj = (h @ w_in).reshape(B, T, G, 3, H, dh)
    pos = jnp.arange(T)
    outs = []
    lses = []
    for gi, (window, dilation) in enumerate(DSW_PATTERNS):
        q = partial_rope(rms_norm(proj[:, :, gi, 0].astype(jnp.float32), q_norm_g[gi]), pos)
        k = partial_rope(rms_norm(proj[:, :, gi, 1].astype(jnp.float32), k_norm_g[gi]), pos)
        v = proj[:, :, gi, 2].astype(jnp.float32)
        o, lse = dilated_group_attention(q, k, v, dilation, window // dilation)
        outs.append(o)
        lses.append(lse)
    wts = jax.nn.softmax(jnp.stack(lses, axis=0), axis=0)
    o = jnp.sum(wts[..., None] * jnp.stack(outs, axis=0), axis=0)
    return o.reshape(B, T, H * dh).astype(h.dtype) @ w_out


def swiglu(h, w_gate_up, w_down):
    gate, up = jnp.split(h @ w_gate_up, 2, axis=-1)
    return (jax.nn.silu(gate) * up) @ w_down


def setup_inputs(seed: int = 0) -> dict:
    key = jax.random.key(seed)
    ks = jax.random.split(key, 20)
    D = D_MODEL
    f32 = jnp.float32
    nA = (DEPTH + N_MIXERS - 1) // N_MIXERS
    nB = DEPTH // N_MIXERS
    H = GDN_HEADS
    gdn_in = 2 * H * GDN_DK + 2 * H * GDN_DV + 2 * H
    gdn_conv_ch = 2 * H * GDN_DK + H * GDN_DV
    dsw_in = DSW_GROUPS * 3 * DSW_HEADS * DSW_HEAD_DIM

    def nrm(k, shape, fan_in):
        return jax.random.normal(k, shape, f32) * (fan_in ** -0.5)

    def gain(k, shape):
        return 1.0 + 0.02 * jax.random.normal(k, shape, f32)

    dt = jnp.exp(jax.random.uniform(ks[9], (nA, H), f32, math.log(1e-3), math.log(1e-1)))
    return {
        'x': jax.random.normal(ks[0], (BATCH, SEQ, D), f32),
        'c': jax.random.normal(ks[1], (BATCH, D), f32),
        'mod_w': nrm(ks[2], (DEPTH, D, 6 * D), D),
        'mod_b': 0.02 * jax.random.normal(ks[3], (DEPTH, 6 * D), f32),
        'mix_norm_g': gain(ks[4], (DEPTH, D)),
        'ffn_norm_g': gain(ks[5], (DEPTH, D)),
        'gdn_w_in': nrm(ks[6], (nA, D, gdn_in), D),
        'gdn_conv_w': nrm(ks[7], (nA, GDN_CONV, gdn_conv_ch), GDN_CONV),
        'gdn_A_log': jnp.log(jax.random.uniform(ks[8], (nA, H), f32, 1.0, 16.0)),
        'gdn_dt_bias': dt + jnp.log(-jnp.expm1(-dt)),
        'gdn_norm_g': gain(ks[10], (nA, GDN_DV)),
        'gdn_w_out': nrm(ks[11], (nA, H * GDN_DV, D), H * GDN_DV),
        'dsw_w_in': nrm(ks[12], (nB, D, dsw_in), D),
        'dsw_q_norm_g': gain(ks[13], (nB, DSW_GROUPS, DSW_HEAD_DIM)),
        'dsw_k_norm_g': gain(ks[14], (nB, DSW_GROUPS, DSW_HEAD_DIM)),
        'dsw_w_out': nrm(ks[15], (nB, DSW_HEADS * DSW_HEAD_DIM, D), DSW_HEADS * DSW_HEAD_DIM),
        'ffn_w_gate_up': nrm(ks[16], (DEPTH, D, 2 * FFN_HIDDEN), D),
        'ffn_w_down': nrm(ks[17], (DEPTH, FFN_HIDDEN, D), FFN_HIDDEN),
    }


def reference(x, c, mod_w, mod_b, mix_norm_g, ffn_norm_g, gdn_w_in, gdn_conv_w, gdn_A_log,
              gdn_dt_bias, gdn_norm_g, gdn_w_out, dsw_w_in, dsw_q_norm_g, dsw_k_norm_g,
              dsw_w_out, ffn_w_gate_up, ffn_w_down):
    cond = jax.nn.silu(c)
    for layer in range(DEPTH):
        mod = cond @ mod_w[layer] + mod_b[layer]
        sh1, sc1, g1, sh2, sc2, g2 = [m[:, None, :] for m in jnp.split(mod, 6, axis=-1)]
        h = rms_norm(x, mix_norm_g[layer]) * (1 + sc1) + sh1
        j = layer // N_MIXERS
        if layer % N_MIXERS == 0:
            y = gated_deltanet(h, gdn_w_in[j], gdn_conv_w[j], gdn_A_log[j], gdn_dt_bias[j],
                               gdn_norm_g[j], gdn_w_out[j])
        else:
            y = dilated_window_attention(h, dsw_w_in[j], dsw_q_norm_g[j], dsw_k_norm_g[j],
                                         dsw_w_out[j])
        x = x + g1 * y
        h = rms_norm(x, ffn_norm_g[layer]) * (1 + sc2) + sh2
        x = x + g2 * swiglu(h, ffn_w_gate_up[layer], ffn_w_down[layer])
    return x
```

```python
import contextlib
import math
import numpy as np
import concourse.bass as bass
import concourse.mybir as mybir
from concourse.alu_op_type import AluOpType as ALU
from concourse.bass_utils import run_bass_kernel_spmd

AF = mybir.ActivationFunctionType
F32 = mybir.dt.float32
BF16 = mybir.dt.bfloat16

D = 1024
T = 4096
DEPTH = 4
KC = 8
FH = 2816
NJ = FH // 128
EPS = 1e-6
TW = 256
NT = T // TW
GDN_IN = 4112
DSW_IN = 9216
DSW_DIL = (1, 4, 16)
ROPE_THETA = 500000.0


class Buf:
    __slots__ = ("w", "r", "name", "excl")

    def __init__(self, name="", excl=False):
        self.w = None
        self.r = []
        self.name = name
        self.excl = excl


def PB():
    return Buf(excl=True)


class Sched:
    def __init__(self, nc, stack):
        self.nc = nc
        self.eng = {"pe": nc.tensor, "dve": nc.vector, "act": nc.scalar,
                    "pool": nc.gpsimd, "sp": nc.sync}
        self.sems = {}
        self.cnt = {}
        self.stack = stack
        for e in self.eng:
            self.sems[e] = stack.enter_context(nc.semaphore("s_" + e))
            self.cnt[e] = 0
        self.waited = {e: {} for e in self.eng}
        self.nslot = 0
        self.ninst = 0
        self.free_slots = []
        self.free_sw = []
        self.live_slots = []

    def new_slot(self, sw=False):
        fl = self.free_sw if sw else self.free_slots
        if fl:
            k = fl.pop()
        else:
            k = ("w%d" if sw else "d%d") % self.nslot
            self.nslot += 1
            self.sems[k] = self.stack.enter_context(self.nc.semaphore("s_" + k))
            self.cnt[k] = 0
        self.live_slots.append(k)
        return k

    def _wait(self, e, deps):
        best = {}
        for d in deps:
            if d is None:
                continue
            k, v = d
            if best.get(k, 0) < v:
                best[k] = v
        w = self.waited[e]
        for k, v in best.items():
            if k == e and e == "pe":
                continue
            if w.get(k, 0) < v:
                self.eng[e].wait_ge(self.sems[k], v)
                w[k] = v
                self.ninst += 1

    @staticmethod
    def _deps(reads, writes):
        deps = []
        for b in reads:
            deps.append(b.w)
            if b.excl:
                deps.extend(b.r)
        for b in writes:
            deps.append(b.w)
            deps.extend(b.r)
        return deps

    @staticmethod
    def _stamp(st, reads, writes):
        for b in reads:
            if b.excl:
                b.w = st
                b.r = []
                continue
            b.r.append(st)
            if len(b.r) > 64:
                best = {}
                for k, v in b.r:
                    if best.get(k, 0) < v:
                        best[k] = v
                b.r = list(best.items())
        for b in writes:
            b.w = st
            b.r = []

    def op(self, e, fn, reads=(), writes=()):
        self._wait(e, self._deps(reads, writes))
        inst = fn(self.eng[e])
        self.cnt[e] += 1
        inst.then_inc(self.sems[e], 1)
        self.ninst += 1
        self._stamp((e, self.cnt[e]), reads, writes)
        return inst

    def group(self, e, fns, reads=(), writes=()):
        self._wait(e, self._deps(reads, writes))
        inst = None
        for fn in fns:
            inst = fn(self.eng[e])
            self.ninst += 1
        self.cnt[e] += 1
        inst.then_inc(self.sems[e], 1)
        self._stamp((e, self.cnt[e]), reads, writes)
        return inst

    def dma(self, q, slot, out, in_, reads=(), writes=()):
        assert (slot[0] == "w") == (q == "pool"), (q, slot)
        self._wait(q, self._deps(reads, writes))
        inst = self.eng[q].dma_start(out=out, in_=in_)
        self.cnt[slot] += 16
        inst.then_inc(self.sems[slot], 16)
        self.ninst += 1
        self._stamp((slot, self.cnt[slot]), reads, writes)
        return inst

    def barrier(self):
        deps = [(k, v) for k, v in self.cnt.items() if v > 0]
        for e in self.eng:
            w = self.waited[e]
            for k, v in deps:
                if k != e and w.get(k, 0) < v:
                    self.eng[e].wait_ge(self.sems[k], v)
                    w[k] = v
        for k in self.live_slots:
            (self.free_sw if k[0] == "w" else self.free_slots).append(k)
        self.live_slots = []


class Ctx:
    pass


def _sb(K, ph, name, shape, dt):
    K.uid += 1
    return ph.enter_context(K.nc.sbuf_tensor("%s_%d" % (name, K.uid), shape, dt))


def _ps(K, ph, name, shape, dt):
    K.uid += 1
    return ph.enter_context(K.nc.psum_tensor("%s_%d" % (name, K.uid), shape, dt))


def phase_mod(K):
    nc, S = K.nc, K.S
    with contextlib.ExitStack() as ph:
        cT = _sb(K, ph, "cT", [128, KC], F32)
        cond = _sb(K, ph, "cond", [128, KC], BF16)
        mixg = _sb(K, ph, "mixg", [128, DEPTH, KC], F32)
        ffng = _sb(K, ph, "ffng", [128, DEPTH, KC], F32)
        modb = _sb(K, ph, "modb", [128, DEPTH, 48], F32)
        modv = _sb(K, ph, "modv", [128, 48], F32)
        wsl = [_sb(K, ph, "mwsl%d" % i, [128, KC, 512], BF16) for i in range(2)]
        mps = _ps(K, ph, "modps", [128, 512], F32)
        b_c, b_cond, b_mixg, b_ffng, b_modb, b_modv = [Buf() for _ in range(6)]
        b_mps = PB()
        b_w = [Buf(), Buf()]
        sl = [S.new_slot(sw=True), S.new_slot(sw=True)]
        s0 = S.new_slot()
        S.dma("sp", s0, cT[:], K.inp["cT"], writes=[b_c])
        S.dma("sp", S.new_slot(), mixg[:], K.inp["mixg"], writes=[b_mixg])
        S.dma("sp", S.new_slot(), ffng[:], K.inp["ffng"], writes=[b_ffng])
        S.dma("sp", S.new_slot(), modb[:], K.inp["modb"], writes=[b_modb])
        S.op("act", lambda e: e.activation(out=cond[:], in_=cT[:], func=AF.Silu), reads=[b_c], writes=[b_cond])
        for l in range(DEPTH):
            wv = K.inp["mod_w"][l].rearrange("(kc p) n -> p kc n", p=128)
            for s in range(12):
                w = wsl[s % 2]
                bw = b_w[s % 2]
                S.dma("pool", sl[s % 2], w[:], wv[:, :, s * 512:(s + 1) * 512], writes=[bw])
                for mm in range(4):
                    m = s * 4 + mm
                    S.group("pe", [lambda e, kc=kc, mm=mm, m=m, w=w: e.matmul(
                        mps[:, m:m + 1], lhsT=w[:, kc, mm * 128:(mm + 1) * 128], rhs=cond[:, kc:kc + 1],
                        start=(kc == 0), stop=(kc == KC - 1)) for kc in range(KC)], reads=[bw, b_cond], writes=[b_mps])
            S.op("dve", lambda e, l=l: e.tensor_tensor(out=modv[:], in0=mps[:, 0:48], in1=modb[:, l, :], op=ALU.add),
                 reads=[b_mps, b_modb], writes=[b_modv])
            V = K.vec
            bv = K.b_vec
            S.op("dve", lambda e, l=l: e.scalar_tensor_tensor(out=V["gs1"][:, l, :], in0=modv[:, 8:16], scalar=1.0,
                                                             in1=mixg[:, l, :], op0=ALU.add, op1=ALU.mult),
                 reads=[b_modv, b_mixg], writes=[bv])
            S.op("dve", lambda e, l=l: e.tensor_scalar(out=V["gs1"][:, l, :], in0=V["gs1"][:, l, :], scalar1=32.0,
                                                      scalar2=None, op0=ALU.mult), reads=[bv], writes=[bv])
            S.op("dve", lambda e, l=l: e.scalar_tensor_tensor(out=V["gs2"][:, l, :], in0=modv[:, 32:40], scalar=1.0,
                                                             in1=ffng[:, l, :], op0=ALU.add, op1=ALU.mult),
                 reads=[b_modv, b_ffng], writes=[bv])
            S.op("dve", lambda e, l=l: e.tensor_scalar(out=V["gs2"][:, l, :], in0=V["gs2"][:, l, :], scalar1=32.0,
                                                      scalar2=None, op0=ALU.mult), reads=[bv], writes=[bv])
            for nm, off in (("sh1", 0), ("g1", 16), ("sh2", 24), ("g2", 40)):
                S.op("dve", lambda e, l=l, nm=nm, off=off: e.tensor_copy(out=V[nm][:, l, :], in_=modv[:, off:off + 8]),
                     reads=[b_modv], writes=[bv])
        if K.dbg is not None and "dbg_vec" in K.dbg:
            for i, nm in enumerate(("gs1", "sh1", "g1", "gs2", "sh2", "g2")):
                S.dma("sp", S.new_slot(), K.out["dbg_vec"][:, i, :, :], K.vec[nm][:], reads=[K.b_vec], writes=[Buf()])
        S.barrier()


def emit_norm(K, N, x, b_x, h, b_h, gs, sh, W):
    S = K.S
    S.op("act", lambda e: e.activation(out=N["sq"][:, :, 0:W], in_=x[:, :, 0:W], func=AF.Square),
         reads=[b_x], writes=[N["b_sq"]])
    S.group("pe", [lambda e, kc=kc: e.matmul(N["ss"][:, 0:W], lhsT=K.ones_bf, rhs=N["sq"][:, kc, 0:W],
                                             start=(kc == 0), stop=(kc == KC - 1)) for kc in range(KC)],
            reads=[N["b_sq"], K.b_const], writes=[N["b_ss"]])
    S.op("act", lambda e: e.activation(out=N["rstd"][:, 0:W], in_=N["ss"][:, 0:W], func=AF.Ln,
                                       bias=K.eps1024[:, 0:1], scale=1.0),
         reads=[N["b_ss"], K.b_const], writes=[N["b_rstd"]])
    S.op("act", lambda e: e.activation(out=N["rstd"][:, 0:W], in_=N["rstd"][:, 0:W], func=AF.Exp, scale=-0.5),
         reads=[N["b_rstd"]], writes=[N["b_rstd"]])
    for kc in range(KC):
        S.op("dve", lambda e, kc=kc: e.tensor_tensor(out=N["tmp"][:, kc, 0:W], in0=x[:, kc, 0:W],
                                                    in1=N["rstd"][:, 0:W], op=ALU.mult),
             reads=[b_x, N["b_rstd"]], writes=[N["b_tmp"]])
        S.op("act", lambda e, kc=kc: e.activation(out=h[:, kc, 0:W], in_=N["tmp"][:, kc, 0:W], func=AF.Identity,
                                                 bias=sh[:, kc:kc + 1], scale=gs[:, kc:kc + 1]),
             reads=[N["b_tmp"], K.b_vec], writes=[b_h])


def norm_gen(K, N, x, b_x, h, b_h, gs, sh, W):
    S = K.S
    S.op("act", lambda e: e.activation(out=N["sq"][:, :, 0:W], in_=x[:, :, 0:W], func=AF.Square),
         reads=[b_x], writes=[N["b_sq"]])
    yield
    S.group("pe", [lambda e, kc=kc: e.matmul(N["ss"][:, 0:W], lhsT=K.ones_bf, rhs=N["sq"][:, kc, 0:W],
                                             start=(kc == 0), stop=(kc == KC - 1)) for kc in range(KC)],
            reads=[N["b_sq"], K.b_const], writes=[N["b_ss"]])
    yield
    S.op("act", lambda e: e.activation(out=N["rstd"][:, 0:W], in_=N["ss"][:, 0:W], func=AF.Ln,
                                       bias=K.eps1024[:, 0:1], scale=1.0),
         reads=[N["b_ss"], K.b_const], writes=[N["b_rstd"]])
    yield
    S.op("act", lambda e: e.activation(out=N["rstd"][:, 0:W], in_=N["rstd"][:, 0:W], func=AF.Exp, scale=-0.5),
         reads=[N["b_rstd"]], writes=[N["b_rstd"]])
    yield
    for kc in range(KC):
        S.op("dve", lambda e, kc=kc: e.tensor_tensor(out=N["tmp"][:, kc, 0:W], in0=x[:, kc, 0:W],
                                                    in1=N["rstd"][:, 0:W], op=ALU.mult),
             reads=[b_x, N["b_rstd"]], writes=[N["b_tmpk"][kc]])
        yield
        S.op("act", lambda e, kc=kc: e.activation(out=h[:, kc, 0:W], in_=N["tmp"][:, kc, 0:W], func=AF.Identity,
                                                 bias=sh[:, kc:kc + 1], scale=gs[:, kc:kc + 1]),
             reads=[N["b_tmpk"][kc], K.b_vec], writes=[b_h])
        yield


def alloc_norm(K, ph, W):
    N = {}
    N["sq"] = _sb(K, ph, "nsq", [128, KC, W], BF16)
    N["tmp"] = _sb(K, ph, "ntmp", [128, KC, W], F32)
    N["rstd"] = _sb(K, ph, "nrstd", [128, W], F32)
    N["ss"] = _ps(K, ph, "nss", [128, 512], F32)
    for k in ("b_sq", "b_tmp", "b_rstd"):
        N[k] = Buf()
    N["b_ss"] = PB()
    N["b_tmpk"] = [Buf() for _ in range(KC)]
    return N


def phase_x0(K):
    nc, S = K.nc, K.S
    with contextlib.ExitStack() as ph:
        xin = [_sb(K, ph, "xin%d" % i, [128, D], F32) for i in range(2)]
        b_xin = [Buf(), Buf()]
        xt = [_sb(K, ph, "xt%d" % i, [128, KC, TW], F32) for i in range(2)]
        b_xt = [Buf(), Buf()]
        ht = [_sb(K, ph, "ht%d" % i, [128, KC, TW], BF16) for i in range(2)]
        b_ht = [Buf(), Buf()]
        pt = [_ps(K, ph, "x0pt%d" % i, [128, 4, 128], F32) for i in range(2)]
        b_pt = [PB(), PB()]
        N = alloc_norm(K, ph, TW)
        sl = [S.new_slot(), S.new_slot()]
        so = [S.new_slot(), S.new_slot()]
        so2 = [S.new_slot(), S.new_slot()]
        for blk in range(T // 128):
            xi = xin[blk % 2]
            bxi = b_xin[blk % 2]
            S.dma("sp", sl[blk % 2], xi[:], K.inp["x"][blk * 128:(blk + 1) * 128, :], writes=[bxi])
            ti = (blk // 2) % 2
            half = blk % 2
            for hf in range(2):
                S.group("pe", [lambda e, q=q, hf=hf, xi=xi: e.matmul(
                    pt[hf][:, q, :], lhsT=xi[:, (hf * 4 + q) * 128:(hf * 4 + q + 1) * 128], rhs=K.ident_f, start=True, stop=True)
                    for q in range(4)], reads=[bxi, K.b_const], writes=[b_pt[hf]])
                S.op("act" if hf == 0 else "dve",
                     (lambda e, hf=hf, ti=ti, half=half: e.copy(out=xt[ti][:, hf * 4:(hf + 1) * 4, half * 128:(half + 1) * 128], in_=pt[hf][:]))
                     if hf == 0 else
                     (lambda e, hf=hf, ti=ti, half=half: e.tensor_copy(out=xt[ti][:, hf * 4:(hf + 1) * 4, half * 128:(half + 1) * 128], in_=pt[hf][:])),
                     reads=[b_pt[hf]], writes=[b_xt[ti]])
            if half == 1:
                t = blk // 2
                emit_norm(K, N, xt[ti], b_xt[ti], ht[ti], b_ht[ti], K.vec["gs1"][:, 0, :], K.vec["sh1"][:, 0, :], TW)
                S.dma("sp", so[ti], K.xT_d[:, :, t * TW:(t + 1) * TW], xt[ti][:], reads=[b_xt[ti]], writes=[K.b_xT[t]])
                S.dma("sp", so2[ti], K.hT_d[:, :, t * TW:(t + 1) * TW], ht[ti][:], reads=[b_ht[ti]], writes=[K.b_hT[t]])
        S.barrier()


def phase_t1(K, l, w_out_ap):
    nc, S = K.nc, K.S
    NLT = 3
    with contextlib.ExitStack() as ph:
        wo = _sb(K, ph, "wo", [128, KC, D], BF16)
        b_wo = Buf()
        S.dma("pool", S.new_slot(sw=True), wo[:], w_out_ap.rearrange("(kc p) n -> p kc n", p=128), writes=[b_wo])
        V = K.vec
        LT = []
        for i in range(NLT):
            d = {"xt": _sb(K, ph, "t1x%d" % i, [128, KC, TW], F32), "b_xt": Buf(), "s_x": S.new_slot(),
                 "ot": _sb(K, ph, "t1o%d" % i, [128, KC, TW], BF16), "b_ot": Buf(), "s_o": S.new_slot(),
                 "ht": _sb(K, ph, "t1h%d" % i, [128, KC, TW], BF16), "b_ht": Buf(),
                 "acc": _ps(K, ph, "t1acc%d" % i, [128, 2, TW], F32), "b_acc": PB(),
                 "N": alloc_norm(K, ph, TW), "s_so": S.new_slot(), "s_so2": S.new_slot()}
            LT.append(d)

        def job(li, t):
            d = LT[li]
            xt, ot, ht, acc = d["xt"], d["ot"], d["ht"], d["acc"]
            S.dma("sp", d["s_x"], xt[:], K.xT_d[:, :, t * TW:(t + 1) * TW], reads=[K.b_xT[t]], writes=[d["b_xt"]])
            S.dma("sp", d["s_o"], ot[:], K.oT_d[:, :, t * TW:(t + 1) * TW], reads=[K.b_oT[t]], writes=[d["b_ot"]])
            yield
            for m in range(KC):
                S.group("pe", [lambda e, kc=kc, m=m: e.matmul(
                    acc[:, m % 2, :], lhsT=wo[:, kc, m * 128:(m + 1) * 128], rhs=ot[:, kc, :],
                    start=(kc == 0), stop=(kc == KC - 1)) for kc in range(KC)], reads=[b_wo, d["b_ot"]], writes=[d["b_acc"]])
                yield
                S.op("dve", lambda e, m=m: e.scalar_tensor_tensor(
                    out=xt[:, m, :], in0=acc[:, m % 2, :], scalar=V["g1"][:, l, m:m + 1], in1=xt[:, m, :],
                    op0=ALU.mult, op1=ALU.add), reads=[d["b_acc"], d["b_xt"], K.b_vec], writes=[d["b_xt"]])
                yield
            yield from norm_gen(K, d["N"], xt, d["b_xt"], ht, d["b_ht"], V["gs2"][:, l, :], V["sh2"][:, l, :], TW)
            S.dma("sp", d["s_so"], K.xT_d[:, :, t * TW:(t + 1) * TW], xt[:], reads=[d["b_xt"]], writes=[K.b_xT[t]])
            S.dma("sp", d["s_so2"], K.h2T_d[:, :, t * TW:(t + 1) * TW], ht[:], reads=[d["b_ht"]], writes=[K.b_h2T[t]])
            yield

        run_rolling([(lambda li, t=t: job(li, t)) for t in range(NT)], NLT, stagger=6)
        S.barrier()


def phase_t2(K, l, w_gu_ap, w_dn_ap, last):
    nc, S = K.nc, K.S
    with contextlib.ExitStack() as ph:
        wgu = _sb(K, ph, "wgu", [128, KC, 2 * FH], BF16)
        wdn = _sb(K, ph, "wdn", [128, NJ, D], BF16)
        b_wgu = [Buf() for _ in range(KC)]
        b_wdn = Buf()
        xt = [_sb(K, ph, "t2x%d" % i, [128, KC, TW], F32) for i in range(2)]
        b_xt = [Buf(), Buf()]
        h2 = [_sb(K, ph, "t2h%d" % i, [128, KC, TW], BF16) for i in range(2)]
        b_h2 = [Buf(), Buf()]
        sg = [_sb(K, ph, "t2sg%d" % i, [128, TW], F32) for i in range(2)]
        b_sg = [Buf(), Buf()]
        act = [_sb(K, ph, "t2act%d" % i, [128, TW], BF16) for i in range(3)]
        b_act = [Buf() for _ in range(3)]
        pg = [_ps(K, ph, "t2pg%d" % i, [128, 2, TW], F32) for i in range(2)]
        b_pg = [PB(), PB()]
        pacc = [_ps(K, ph, "t2pa%d" % i, [128, 2, TW], F32) for i in range(4)]
        b_pacc = [PB() for _ in range(4)]
        sw = S.new_slot(sw=True)
        slx = [S.new_slot(), S.new_slot()]
        slh = [S.new_slot(), S.new_slot()]
        so = [S.new_slot(), S.new_slot()]
        so2 = [S.new_slot(), S.new_slot()]
        gv = w_gu_ap.rearrange("(kc p) n -> p kc n", p=128)
        for kc in range(KC):
            S.dma("pool", S.new_slot(sw=True), wgu[:, kc, :], gv[:, kc, :], writes=[b_wgu[kc]])
        S.dma("pool", sw, wdn[:], w_dn_ap.rearrange("(j p) n -> p j n", p=128), writes=[b_wdn])
        V = K.vec
        if last:
            osb = [_sb(K, ph, "t2os%d" % i, [128, D], F32) for i in range(2)]
            b_osb = [Buf(), Buf()]
        else:
            N = alloc_norm(K, ph, TW)
            hn = [_sb(K, ph, "t2hn%d" % i, [128, KC, TW], BF16) for i in range(2)]
            b_hn = [Buf(), Buf()]

        def down(t, i, j, a):
            S.group("pe", [lambda e, m=m, j=j, a=a: e.matmul(
                pacc[m // 2][:, m % 2, :], lhsT=wdn[:, j, m * 128:(m + 1) * 128], rhs=a[:],
                start=(j == 0 and m % 2 == 0), stop=(j == NJ - 1), skip_group_check=True)
                for m in range(KC)], reads=[b_wdn, b_act[j % 3]], writes=b_pacc)

        for t in range(NT):
            i = t % 2
            S.dma("sp", slx[i], xt[i][:], K.xT_d[:, :, t * TW:(t + 1) * TW], reads=[K.b_xT[t]], writes=[b_xt[i]])
            S.dma("sp", slh[i], h2[i][:], K.h2T_d[:, :, t * TW:(t + 1) * TW], reads=[K.b_h2T[t]], writes=[b_h2[i]])
            for j in range(NJ):
                p = pg[j % 2]
                bp = b_pg[j % 2]
                S.group("pe", [lambda e, kc=kc, half=half, col0=col0, p=p, i=i: e.matmul(
                    p[:, half, :], lhsT=wgu[:, kc, col0:col0 + 128], rhs=h2[i][:, kc, :],
                    start=(kc == 0), stop=(kc == KC - 1))
                    for half, col0 in ((0, j * 128), (1, FH + j * 128)) for kc in range(KC)],
                    reads=b_wgu + [b_h2[i]], writes=[bp])
                if j > 0:
                    down(t, i, j - 1, act[(j - 1) % 3])
                s_ = sg[j % 2]
                S.op("act", lambda e, p=p, s_=s_: e.activation(out=s_[:], in_=p[:, 0, :], func=AF.Silu),
                     reads=[bp], writes=[b_sg[j % 2]])
                a = act[j % 3]
                S.op("dve", lambda e, p=p, s_=s_, a=a: e.tensor_tensor(out=a[:], in0=p[:, 1, :], in1=s_[:], op=ALU.mult),
                     reads=[bp, b_sg[j % 2]], writes=[b_act[j % 3]])
            down(t, i, NJ - 1, act[(NJ - 1) % 3])
            for m in range(KC):
                S.op("dve", lambda e, m=m, i=i: e.scalar_tensor_tensor(
                    out=xt[i][:, m, :], in0=pacc[m // 2][:, m % 2, :], scalar=V["g2"][:, l, m:m + 1], in1=xt[i][:, m, :],
                    op0=ALU.mult, op1=ALU.add), reads=[b_pacc[m // 2], b_xt[i], K.b_vec], writes=[b_xt[i]])
            if last:
                for sub in range(TW // 128):
                    o_ = osb[sub % 2]
                    bo = b_osb[sub % 2]
                    for hf in range(2):
                        p = pg[hf]
                        pv = p[:].rearrange("p a (b c) -> p (a b) c", c=128)
                        S.group("pe", [lambda e, hf=hf, q=q, pv=pv, sub=sub, i=i: e.matmul(
                            pv[:, q, :], lhsT=xt[i][:, hf * 4 + q, sub * 128:(sub + 1) * 128], rhs=K.ident_f,
                            start=True, stop=True) for q in range(4)], reads=[b_xt[i], K.b_const], writes=[b_pg[hf]])
                        if hf == 0:
                            S.op("act", lambda e, pv=pv, o_=o_: e.copy(out=o_[:, 0:512].rearrange("p (a c) -> p a c", c=128), in_=pv),
                                 reads=[b_pg[hf]], writes=[bo])
                        else:
                            S.op("dve", lambda e, pv=pv, o_=o_: e.tensor_copy(out=o_[:, 512:1024].rearrange("p (a c) -> p a c", c=128), in_=pv),
                                 reads=[b_pg[hf]], writes=[bo])
                    r0 = t * TW + sub * 128
                    S.dma("sp", so[sub % 2], K.out["y"][r0:r0 + 128, :], o_[:], reads=[bo], writes=[K.b_ys[sub % 2]])
            else:
                emit_norm(K, N, xt[i], b_xt[i], hn[i], b_hn[i], V["gs1"][:, l + 1, :], V["sh1"][:, l + 1, :], TW)
                S.dma("sp", so[i], K.xT_d[:, :, t * TW:(t + 1) * TW], xt[i][:], reads=[b_xt[i]], writes=[K.b_xT[t]])
                S.dma("sp", so2[i], K.hT_d[:, :, t * TW:(t + 1) * TW], hn[i][:], reads=[b_hn[i]], writes=[K.b_hT[t]])
        S.barrier()


MIXERS = {}

def phase_gdn(K, l, j):
    nc, S = K.nc, K.S
    NB = T // 128
    wv = K.inp["gdn_w_in"][j].rearrange("(kc p) n -> p kc n", p=128)
    with contextlib.ExitStack() as gph:
        gsb = lambda n, shp, dt: _sb(K, gph, n, shp, dt)
        beta = gsb("g_beta", [128, NB, 8], F32)
        gc = gsb("g_gc", [128, NB, 8], F32)
        egc = gsb("g_egc", [128, NB, 8], F32)
        etail = gsb("g_etail", [128, NB, 8], F32)
        glast = gsb("g_glast", [128, NB, 8], F32)
        bgt = gsb("g_bg", [128, NB, 8], F32)
        cw = gsb("g_cw", [128, 2, 24, 4], F32)
        hc = gsb("g_hc", [128, 2, 2, 8], F32)
        ng = gsb("g_ng", [128, 2], F32)
        b_g = Buf()
        b_cst = Buf()
        scst = S.new_slot()
        S.dma("sp", scst, cw[:], K.inp["gdn_conv"], writes=[b_cst])
        S.dma("sp", scst, hc[:], K.inp["gdn_hc"], writes=[b_cst])
        S.dma("sp", scst, ng[:], K.inp["gdn_ng"], writes=[b_cst])
        with contextlib.ExitStack() as ph:
            sb = lambda n, shp, dt: _sb(K, ph, n, shp, dt)
            hT = sb("g_hT", [128, KC, T], BF16)
            b_hT = [Buf() for _ in range(KC)]
            for kc in range(KC):
                S.dma("sp", S.new_slot(), hT[:, kc, :], K.hT_d[:, kc, :], reads=K.b_hT, writes=[b_hT[kc]])
            pf = [_ps(K, ph, "g_pf%d" % i, [128, 512], F32) for i in range(4)]
            b_pf = [PB() for _ in range(4)]
            wba = sb("g_wba", [128, KC, 16], BF16)
            b_wba = Buf()
            S.dma("pool", S.new_slot(sw=True), wba[:], wv[:, :, 4096:4112], writes=[b_wba])
            ba = sb("g_ba", [128, NB, 16], F32)
            gg = sb("g_g", [128, NB, 8], F32)
            negA = sb("g_negA", [128, 8], F32)
            for blk in range(NB):
                S.group("pe", [lambda e, kc=kc, blk=blk: e.matmul(
                    pf[0][:, blk * 16:(blk + 1) * 16], lhsT=hT[:, kc, blk * 128:(blk + 1) * 128], rhs=wba[:, kc, :],
                    start=(kc == 0), stop=(kc == KC - 1)) for kc in range(KC)], reads=b_hT + [b_wba], writes=[b_pf[0]])
            S.op("act", lambda e: e.copy(out=ba[:].rearrange("p a b -> p (a b)"), in_=pf[0][:]), reads=[b_pf[0]], writes=[b_g])
            S.op("act", lambda e: e.activation(out=beta[:], in_=ba[:, :, 0:8], func=AF.Sigmoid), reads=[b_g], writes=[b_g])
            S.op("act", lambda e: e.activation(out=negA[:], in_=hc[:, j, 0, :], func=AF.Exp), reads=[b_cst], writes=[b_g])
            S.op("dve", lambda e: e.tensor_scalar(out=negA[:], in0=negA[:], scalar1=-1.0, scalar2=None, op0=ALU.mult),
                 reads=[b_g], writes=[b_g])
            S.op("dve", lambda e: e.tensor_tensor(out=gg[:], in0=ba[:, :, 8:16],
                                                 in1=hc[:, j, 1, :].unsqueeze(1).to_broadcast([128, NB, 8]), op=ALU.add),
                 reads=[b_g, b_cst], writes=[b_g])
            S.op("act", lambda e: e.activation(out=gg[:], in_=gg[:], func=AF.Exp), reads=[b_g], writes=[b_g])
            S.op("act", lambda e: e.activation(out=gg[:], in_=gg[:], func=AF.Ln, bias=K.eps1024[:, 2:3], scale=1.0),
                 reads=[b_g, K.b_const], writes=[b_g])
            S.op("dve", lambda e: e.tensor_tensor(out=gg[:], in0=gg[:], in1=negA[:].unsqueeze(1).to_broadcast([128, NB, 8]),
                                                 op=ALU.mult), reads=[b_g], writes=[b_g])
            ggf = gg[:].rearrange("p a b -> p (a b)")
            S.op("pe", lambda e: e.matmul(pf[1][:, 0:256], lhsT=K.triU_f, rhs=ggf, start=True, stop=True),
                 reads=[b_g, K.b_const], writes=[b_pf[1]])
            S.op("pe", lambda e: e.matmul(pf[2][:, 0:256], lhsT=K.ones_f, rhs=ggf, start=True, stop=True),
                 reads=[b_g, K.b_const], writes=[b_pf[2]])
            fl = lambda t: t[:].rearrange("p a b -> p (a b)")
            S.op("act", lambda e: e.copy(out=fl(gc), in_=pf[1][:, 0:256]), reads=[b_pf[1]], writes=[b_g])
            S.op("act", lambda e: e.activation(out=fl(egc), in_=pf[1][:, 0:256], func=AF.Exp), reads=[b_pf[1]], writes=[b_g])
            S.op("act", lambda e: e.activation(out=fl(glast), in_=pf[2][:, 0:256], func=AF.Exp), reads=[b_pf[2]], writes=[b_g])
            S.op("dve", lambda e: e.tensor_tensor(out=fl(etail), in0=pf[2][:, 0:256], in1=fl(gc), op=ALU.subtract),
                 reads=[b_pf[2], b_g], writes=[b_g])
            S.op("act", lambda e: e.activation(out=fl(etail), in_=fl(etail), func=AF.Exp), reads=[b_g], writes=[b_g])
            S.op("dve", lambda e: e.tensor_tensor(out=fl(bgt), in0=fl(beta), in1=fl(egc), op=ALU.mult), reads=[b_g], writes=[b_g])
            NW = 4
            w1 = [sb("g_w1%d" % i, [128, KC, 128], BF16) for i in range(NW)]
            b_w1 = [Buf() for _ in range(NW)]
            s_w1 = [S.new_slot(sw=True) for _ in range(NW)]
            stg = [sb("g_stg%d" % i, [128, 3 + T], F32) for i in range(2)]
            b_stg = [[Buf() for _ in range(9)] for _ in range(2)]
            for i in range(2):
                S.op("pool", lambda e, i=i: e.memset(stg[i][:, 0:3], 0.0), writes=[b_stg[i][0]])

            def load_w1(ci):
                S.dma("pool", s_w1[ci % NW], w1[ci % NW][:], wv[:, :, ci * 128:(ci + 1) * 128], writes=[b_w1[ci % NW]])

            for ci in range(NW - 1):
                load_w1(ci)
            NLA = 4
            LAa = [{"cacc": sb("g_caccL%d" % i, [128, 512], F32), "b_cacc": Buf(),
                    "fo": sb("g_foL%d" % i, [128, 512], BF16), "b_fo": Buf(), "s_fo": S.new_slot()} for i in range(NLA)]

            def ajob(li, ci, tt):
                d = LAa[li]
                if tt == 0 and ci + NW - 1 < 32:
                    load_w1(ci + NW - 1)
                W = w1[ci % NW]
                bW = b_w1[ci % NW]
                sg_ = stg[ci % 2]
                bsg = b_stg[ci % 2]
                pa, bpa = pf[li], b_pf[li]
                S.group("pe", [lambda e, kc=kc: e.matmul(
                    pa[:], lhsT=W[:, kc, :], rhs=hT[:, kc, tt * 512:(tt + 1) * 512],
                    start=(kc == 0), stop=(kc == KC - 1)) for kc in range(KC)], reads=b_hT + [bW], writes=[bpa])
                yield
                o = tt * 512
                f_, bf_ = d["fo"], d["b_fo"]
                if ci >= 24:
                    S.op("act", lambda e: e.activation(out=f_[:], in_=pa[:], func=AF.Silu), reads=[bpa], writes=[bf_])
                    yield
                else:
                    S.op("act", lambda e: e.copy(out=sg_[:, 3 + o:3 + o + 512], in_=pa[:]), reads=[bpa], writes=[bsg[1 + tt]])
                    yield
                    ca, bca = d["cacc"], d["b_cacc"]
                    rd = [bsg[tt], bsg[1 + tt], b_cst]
                    S.op("dve", lambda e: e.tensor_scalar(
                        out=ca[:], in0=sg_[:, 3 + o:3 + o + 512], scalar1=cw[:, j, ci, 3:4], scalar2=None, op0=ALU.mult),
                        reads=rd, writes=[bca])
                    yield
                    for k in (2, 1, 0):
                        S.op("dve", lambda e, k=k: e.scalar_tensor_tensor(
                            out=ca[:], in0=sg_[:, k + o:k + o + 512], scalar=cw[:, j, ci, k:k + 1], in1=ca[:],
                            op0=ALU.mult, op1=ALU.add), reads=rd + [bca], writes=[bca])
                        yield
                    S.op("act", lambda e: e.activation(out=f_[:], in_=ca[:], func=AF.Silu), reads=[bca], writes=[bf_])
                    yield
                S.dma("sp", d["s_fo"], K.F_d[ci, :, o:o + 512], f_[:], reads=[bf_], writes=[])
                yield

            run_rolling([(lambda li, ci=ci, tt=tt: ajob(li, ci, tt)) for ci in range(32) for tt in range(8)], NLA, stagger=2)
            S.barrier()
        with contextlib.ExitStack() as ph:
            sb = lambda n, shp, dt: _sb(K, ph, n, shp, dt)
            NL = 8
            junk_sh = sb("g_junk_sh", [128, 128], F32)
            pl = [_ps(K, ph, "g_pl%d" % i, [128, 512], F32) for i in range(NL)]
            L = []
            for h in range(NL):
                d = {}
                d["pb"] = PB()
                d["ps"] = pl[h]
                d["F"] = [[sb("g_F%d_%d_%d" % (h, c, i), [128, 512], BF16) for i in range(2)] for c in range(4)]
                d["bF"] = [[Buf() for i in range(2)] for c in range(4)]
                d["sF"] = [[S.new_slot() for i in range(2)] for c in range(4)]
                d["oT"] = [sb("g_oT%d_%d" % (h, i), [128, 512], BF16) for i in range(1)] * 2
                d["boT"] = [Buf()] * 2
                d["soT"] = [S.new_slot()] * 2
                d["junk"] = junk_sh
                d["b_junk"] = None
                for nm, shp, dt in (("tm", [128, 3, 128], BF16), ("sc", [128, 16], F32),
                                    ("var", [128, 7, 128], BF16), ("fmT", [128, 4, 128], BF16), ("diagG", [128, 128], F32),
                                    ("dTs", [128, 128], F32), ("dTf", [128, 128], F32),
                                    ("attnT", [128, 128], BF16), ("N0", [128, 2, 128], BF16), ("N1", [128, 2, 128], BF16),
                                    ("N2", [128, 2, 128], BF16), ("X0", [128, 2, 128], BF16), ("X1", [128, 2, 128], BF16),
                                    ("MA", [128, 5, 2, 128], BF16), ("PP", [128, 2, 128], BF16), ("u", [128, 128], F32),
                                    ("wT", [128, 128], BF16), ("vnew", [128, 128], BF16), ("S", [128, 128], F32),
                                    ("Sbf", [128, 128], BF16), ("os", [128, 128], BF16)):
                    d[nm] = sb("g_%s%d" % (nm, h), shp, dt)
                    d["b_" + nm] = Buf()
                L.append(d)

            def load_F(h, tt):
                d = L[h]
                i = tt % 2
                for c in range(4):
                    S.dma("sp", d["sF"][c][i], d["F"][c][i][:], K.F_d[c * 8 + h, :, tt * 512:(tt + 1) * 512],
                          reads=[K.b_F], writes=[d["bF"][c][i]])

            pv2 = lambda t: t[:].rearrange("p a b -> p (a b)")

            def chunk(h, n):
                d = L[h]
                ps, pb = d["ps"], d["pb"]
                tt = n // 4
                fi = tt % 2
                o4 = (n % 4) * 128
                Fq, Fk, Fv, Fz = [d["F"][c][fi] for c in range(4)]
                bFq, bFk, bFv, bFz = [d["bF"][c][fi] for c in range(4)]
                gcol = lambda t: t[:, n, h:h + 1]
                tm, sc, var, fmT, junk = d["tm"], d["sc"], d["var"], d["fmT"], d["junk"]
                b_tm, b_sc, b_var, b_fmT, b_junk = d["b_tm"], d["b_sc"], d["b_var"], d["b_fmT"], d["b_junk"]
                S.group("pe", [lambda e, c=c, Fc=Fc: e.matmul(ps[:, c * 128:(c + 1) * 128], lhsT=Fc[:, o4:o4 + 128], rhs=K.ident_bf,
                                                            start=True, stop=True) for c, Fc in enumerate((Fq, Fk, Fv))],
                        reads=[bFq, bFk, bFv, K.b_const], writes=[pb])
                yield
                S.op("act", lambda e: e.copy(out=tm[:].rearrange("p a b -> p (a b)"), in_=ps[:, 0:384]), reads=[pb], writes=[b_tm])
                yield
                S.op("act", lambda e: e.activation(out=junk[:], in_=tm[:, 0, :], func=AF.Square, scale=128.0 ** 0.5, accum_out=sc[:, 0:1]),
                     reads=[b_tm], writes=[b_sc])
                S.op("act", lambda e: e.activation(out=junk[:], in_=tm[:, 1, :], func=AF.Square, accum_out=sc[:, 1:2]),
                     reads=[b_tm], writes=[b_sc])
                yield
                S.op("act", lambda e: e.activation(out=sc[:, 2:4], in_=sc[:, 0:2], func=AF.Ln, bias=K.eps1024[:, 1:2], scale=1.0),
                     reads=[b_sc, K.b_const], writes=[b_sc])
                S.op("act", lambda e: e.activation(out=sc[:, 2:4], in_=sc[:, 2:4], func=AF.Exp, scale=-0.5),
                     reads=[b_sc], writes=[b_sc])
                yield
                rq, rk = sc[:, 2:3], sc[:, 3:4]
                zb = K.eps1024[:, 3:4]
                S.op("act", lambda e: e.activation(out=var[:, 0, :], in_=tm[:, 0, :], func=AF.Identity, bias=zb, scale=rq),
                     reads=[b_tm, b_sc, K.b_const], writes=[b_var])
                S.op("dve", lambda e: e.tensor_scalar(out=var[:, 1, :], in0=tm[:, 0, :], scalar1=rq, scalar2=gcol(egc), op0=ALU.mult, op1=ALU.mult),
                     reads=[b_tm, b_sc, b_g], writes=[b_var])
                yield
                S.op("act", lambda e: e.activation(out=var[:, 2, :], in_=tm[:, 1, :], func=AF.Identity, bias=zb, scale=rk),
                     reads=[b_tm, b_sc, K.b_const], writes=[b_var])
                S.op("dve", lambda e: e.tensor_scalar(out=var[:, 3, :], in0=tm[:, 1, :], scalar1=rk, scalar2=gcol(beta), op0=ALU.mult, op1=ALU.mult),
                     reads=[b_tm, b_sc, b_g], writes=[b_var])
                yield
                S.op("act", lambda e: e.activation(out=var[:, 6, :], in_=tm[:, 2, :], func=AF.Identity, bias=zb, scale=gcol(beta)),
                     reads=[b_tm, b_g, K.b_const], writes=[b_var])
                S.op("dve", lambda e: e.tensor_scalar(out=var[:, 4, :], in0=tm[:, 1, :], scalar1=rk, scalar2=gcol(bgt), op0=ALU.mult, op1=ALU.mult),
                     reads=[b_tm, b_sc, b_g], writes=[b_var])
                yield
                S.op("dve", lambda e: e.tensor_scalar(out=var[:, 5, :], in0=tm[:, 1, :], scalar1=rk, scalar2=gcol(etail), op0=ALU.mult, op1=ALU.mult),
                     reads=[b_tm, b_sc, b_g], writes=[b_var])
                yield
                S.group("pe", [lambda e, vi=vi: e.matmul(ps[:, vi * 128:(vi + 1) * 128], lhsT=var[:, vi, :], rhs=K.ident_bf,
                                                        start=True, stop=True) for vi in range(4)],
                        reads=[b_var, K.b_const], writes=[pb])
                yield
                S.op("dve", lambda e: e.tensor_copy(out=fmT[:].rearrange("p a b -> p (a b)"), in_=ps[:, 0:512]),
                     reads=[pb], writes=[b_fmT])
                qhT, qdT, khT, kbT = fmT[:, 0, :], fmT[:, 1, :], fmT[:, 2, :], fmT[:, 3, :]
                S.op("act", lambda e: e.activation(out=d["diagG"][:], in_=K.ident_f, func=AF.Identity, bias=K.eps1024[:, 3:4], scale=gcol(gc)),
                     reads=[K.b_const, b_g], writes=[d["b_diagG"]])
                yield
                S.op("pe", lambda e: e.matmul(ps[:, 0:128], lhsT=K.ones_f, rhs=d["diagG"][:], start=True, stop=True),
                     reads=[d["b_diagG"], K.b_const], writes=[pb])
                yield
                S.op("dve", lambda e: e.scalar_tensor_tensor(out=d["dTs"][:], in0=ps[:, 0:128], scalar=gcol(gc), in1=K.cst_f[:, 5, :],
                                                            op0=ALU.subtract, op1=ALU.min), reads=[pb, b_g, K.b_const], writes=[d["b_dTs"]])
                yield
                S.op("act", lambda e: e.activation(out=d["dTs"][:], in_=d["dTs"][:], func=AF.Exp), reads=[d["b_dTs"]], writes=[d["b_dTs"]])
                S.group("pe", [lambda e: e.matmul(ps[:, 128:256], lhsT=khT, rhs=kbT, start=True, stop=True),
                               lambda e: e.matmul(ps[:, 256:384], lhsT=khT, rhs=qhT, start=True, stop=True)],
                        reads=[b_fmT], writes=[pb])
                yield
                S.op("pool", lambda e: e.tensor_tensor(out=d["dTf"][:], in0=d["dTs"][:], in1=K.ident_f, op=ALU.add),
                     reads=[d["b_dTs"], K.b_const], writes=[d["b_dTf"]])
                NP = d["N0"]
                S.op("dve", lambda e: e.tensor_tensor(out=NP[:, 1, :], in0=ps[:, 128:256], in1=d["dTs"][:], op=ALU.mult),
                     reads=[pb, d["b_dTs"]], writes=[d["b_N0"]])
                yield
                S.op("dve", lambda e: e.tensor_tensor(out=d["attnT"][:], in0=ps[:, 256:384], in1=d["dTf"][:], op=ALU.mult),
                     reads=[pb, d["b_dTf"]], writes=[d["b_attnT"]])
                yield
                S.op("pe", lambda e: e.matmul(ps[:, 384:512], lhsT=NP[:, 1, :], rhs=K.ident_bf, start=True, stop=True),
                     reads=[d["b_N0"], K.b_const], writes=[pb])
                yield
                S.op("act", lambda e: e.copy(out=NP[:, 0, :], in_=ps[:, 384:512]), reads=[pb], writes=[d["b_N0"]])
                yield
                Nn = [d["N0"], d["N1"], d["N2"]]
                b_N = [d["b_N0"], d["b_N1"], d["b_N2"]]
                XX = [d["X0"], d["X1"]]
                b_XX = [d["b_X0"], d["b_X1"]]
                PP = d["PP"]
                MA = d["MA"]
                S.op("pool", lambda e: e.tensor_tensor(out=MA[:], in0=NP[:].unsqueeze(1).to_broadcast([128, 5, 2, 128]),
                                                      in1=K.cst2[:, 0:5, :, :], op=ALU.mult),
                     reads=[b_N[0], K.b_const], writes=[d["b_MA"]])
                Nn[1] = MA[:, 0, :, :]
                b_N[1] = d["b_MA"]
                yield
                S.op("pool", lambda e: e.tensor_tensor(out=XX[0][:], in0=K.cst2[:, 5, :, :], in1=MA[:, 0, :, :], op=ALU.subtract),
                     reads=[K.b_const, b_N[1]], writes=[b_XX[0]])
                yield
                cx = 0
                for lv, (a, ba_, nx, bnx) in enumerate(((MA[:, 0, :, :], d["b_MA"], d["N2"], d["b_N2"]),
                                                        (d["N2"], d["b_N2"], d["N1"], d["b_N1"]))):
                    S.group("pe", [lambda e, a=a: e.matmul(ps[:, 0:128], lhsT=a[:, 1, :], rhs=a[:, 0, :], start=True, stop=True),
                                   lambda e, a=a: e.matmul(ps[:, 128:256], lhsT=a[:, 0, :], rhs=a[:, 1, :], start=True, stop=True)],
                            reads=[ba_], writes=[pb])
                    yield
                    S.op("act", lambda e, nx=nx: e.copy(out=pv2(nx), in_=ps[:, 0:256]), reads=[pb], writes=[bnx])
                    yield
                    xs, xd = XX[cx], XX[1 - cx]
                    S.group("pe", [lambda e, nx=nx, xs=xs: e.matmul(ps[:, 256:384], lhsT=nx[:, 1, :], rhs=xs[:, 0, :], start=True, stop=True),
                                   lambda e, nx=nx, xs=xs: e.matmul(ps[:, 384:512], lhsT=nx[:, 0, :], rhs=xs[:, 1, :], start=True, stop=True)],
                            reads=[bnx, b_XX[cx]], writes=[pb])
                    yield
                    S.op("dve", lambda e, xs=xs, xd=xd: e.tensor_tensor(out=pv2(xd), in0=ps[:, 256:512], in1=pv2(xs), op=ALU.add),
                         reads=[pb, b_XX[cx]], writes=[b_XX[1 - cx]])
                    yield
                    cx = 1 - cx
                for li in range(4):
                    xs, xd = XX[cx], XX[1 - cx]
                    S.group("pe", [lambda e, xs=xs, li=li: e.matmul(ps[:, 0:128], lhsT=MA[:, 1 + li, 0, :], rhs=xs[:, 1, :], start=True, stop=True),
                                   lambda e, xs=xs, li=li: e.matmul(ps[:, 128:256], lhsT=MA[:, 1 + li, 1, :], rhs=xs[:, 0, :], start=True, stop=True)],
                            reads=[d["b_MA"], b_XX[cx]], writes=[pb])
                    yield
                    S.op("act", lambda e: e.copy(out=pv2(PP), in_=ps[:, 0:256]), reads=[pb], writes=[d["b_PP"]])
                    yield
                    S.group("pe", [lambda e, xs=xs: e.matmul(ps[:, 256:384], lhsT=xs[:, 1, :], rhs=PP[:, 1, :], start=True, stop=True),
                                   lambda e, xs=xs: e.matmul(ps[:, 384:512], lhsT=xs[:, 0, :], rhs=PP[:, 0, :], start=True, stop=True)],
                            reads=[d["b_PP"], b_XX[cx]], writes=[pb])
                    yield
                    S.op("dve", lambda e, xs=xs, xd=xd: e.tensor_tensor(out=pv2(xd), in0=pv2(xs), in1=ps[:, 256:512], op=ALU.subtract),
                         reads=[pb, b_XX[cx]], writes=[b_XX[1 - cx]])
                    yield
                    cx = 1 - cx
                XTf = XX[cx][:, 1, :]
                bXTf = b_XX[cx]
                S.group("pe", [lambda e: e.matmul(ps[:, 0:128], lhsT=XTf, rhs=var[:, 6, :], start=True, stop=True),
                               lambda e: e.matmul(ps[:, 128:256], lhsT=var[:, 4, :], rhs=XTf, start=True, stop=True)],
                        reads=[bXTf, b_var], writes=[pb])
                yield
                wT, vnew, Sst, Sbf, u_sb = d["wT"], d["vnew"], d["S"], d["Sbf"], d["u"]
                S.op("act", lambda e: e.copy(out=wT[:], in_=ps[:, 128:256]), reads=[pb], writes=[d["b_wT"]])
                if n == 0:
                    S.op("dve", lambda e: e.tensor_copy(out=vnew[:], in_=ps[:, 0:128]), reads=[pb], writes=[d["b_vnew"]])
                    yield
                    S.group("pe", [lambda e: e.matmul(ps[:, 128:256], lhsT=d["attnT"][:], rhs=vnew[:], start=True, stop=True),
                                   lambda e: e.matmul(ps[:, 256:384], lhsT=var[:, 5, :], rhs=vnew[:], start=True, stop=True)],
                            reads=[d["b_attnT"], d["b_vnew"], b_var], writes=[pb])
                    yield
                    S.op("dve", lambda e: e.tensor_copy(out=Sst[:], in_=ps[:, 256:384]), reads=[pb], writes=[d["b_S"]])
                else:
                    S.op("dve", lambda e: e.tensor_copy(out=u_sb[:], in_=ps[:, 0:128]), reads=[pb], writes=[d["b_u"]])
                    yield
                    S.op("pe", lambda e: e.matmul(ps[:, 0:128], lhsT=wT[:], rhs=Sbf[:], start=True, stop=True),
                         reads=[d["b_wT"], d["b_S"]], writes=[pb])
                    yield
                    S.op("dve", lambda e: e.tensor_tensor(out=vnew[:], in0=u_sb[:], in1=ps[:, 0:128], op=ALU.subtract),
                         reads=[d["b_u"], pb], writes=[d["b_vnew"]])
                    yield
                    S.group("pe", [lambda e: e.matmul(ps[:, 128:256], lhsT=qdT, rhs=Sbf[:], start=True, stop=False),
                                   lambda e: e.matmul(ps[:, 128:256], lhsT=d["attnT"][:], rhs=vnew[:], start=False, stop=True),
                                   lambda e: e.matmul(ps[:, 256:384], lhsT=var[:, 5, :], rhs=vnew[:], start=True, stop=True)],
                            reads=[b_fmT, d["b_S"], d["b_attnT"], d["b_vnew"], b_var], writes=[pb])
                    yield
                    S.op("dve", lambda e: e.scalar_tensor_tensor(out=Sst[:], in0=Sst[:], scalar=gcol(glast), in1=ps[:, 256:384],
                                                                op0=ALU.mult, op1=ALU.add), reads=[d["b_S"], b_g, pb], writes=[d["b_S"]])
                yield
                if n < NB - 1:
                    S.op("pool", lambda e: e.tensor_copy(out=Sbf[:], in_=Sst[:]), reads=[d["b_S"]], writes=[d["b_S"]])
                S.op("act", lambda e: e.activation(out=junk[:], in_=ps[:, 128:256], func=AF.Square, accum_out=sc[:, 9:10]),
                     reads=[pb], writes=[b_sc])
                yield
                S.op("act", lambda e: e.activation(out=sc[:, 10:11], in_=sc[:, 9:10], func=AF.Ln, bias=K.eps1024[:, 1:2], scale=1.0 / 128.0),
                     reads=[b_sc, K.b_const], writes=[b_sc])
                S.op("act", lambda e: e.activation(out=sc[:, 10:11], in_=sc[:, 10:11], func=AF.Exp, scale=-0.5),
                     reads=[b_sc], writes=[b_sc])
                yield
                S.op("dve", lambda e: e.tensor_scalar(out=d["os"][:], in0=ps[:, 128:256], scalar1=sc[:, 10:11], scalar2=None, op0=ALU.mult),
                     reads=[pb, b_sc], writes=[d["b_os"]])
                yield
                S.op("pe", lambda e: e.matmul(ps[:, 384:512], lhsT=d["os"][:], rhs=K.ident_bf, start=True, stop=True),
                     reads=[d["b_os"], K.b_const], writes=[pb])
                yield
                oi = tt % 2
                S.op("dve", lambda e: e.scalar_tensor_tensor(out=d["oT"][oi][:, o4:o4 + 128], in0=ps[:, 384:512], scalar=ng[:, j:j + 1],
                                                            in1=Fz[:, o4:o4 + 128], op0=ALU.mult, op1=ALU.mult),
                     reads=[pb, b_cst, bFz], writes=[d["boT"][oi]])
                if n % 4 == 3:
                    S.dma("sp", d["soT"][oi], K.oT_d[:, h, tt * 512:(tt + 1) * 512], d["oT"][oi][:], reads=[d["boT"][oi]],
                          writes=[K.b_oT[2 * tt], K.b_oT[2 * tt + 1]])
                yield

            def head_chain(h):
                load_F(h, 0)
                for n in range(NB):
                    if n % 4 == 0 and n // 4 + 1 < 8:
                        load_F(h, n // 4 + 1)
                    yield from chunk(h, n)

            run_rolling([(lambda li: head_chain(li)) for _ in range(NL)], NL, stagger=5)
            S.barrier()


MIXERS["gdn"] = phase_gdn

def run_lanes(gens):
    live = list(gens)
    while live:
        nxt = []
        for g_ in live:
            try:
                next(g_)
                nxt.append(g_)
            except StopIteration:
                pass
        live = nxt


def run_rolling(jobs, nl, stagger=1):
    jobs = list(jobs)
    active = {}
    nxt = 0
    step = 0
    while nxt < len(jobs) or active:
        for li in range(nl):
            if li not in active and nxt < len(jobs) and step >= li * stagger:
                active[li] = jobs[nxt](li)
                nxt += 1
        for li in list(active.keys()):
            try:
                next(active[li])
            except StopIteration:
                del active[li]
        step += 1


def phase_dsw(K, l, j):
    nc, S = K.nc, K.S
    NB = 32
    SC = 128.0 ** -0.5
    NL = 7
    with contextlib.ExitStack() as ph:
        sb = lambda n, shp, dt: _sb(K, ph, n, shp, dt)
        hT = sb("d_hT", [128, KC, T], BF16)
        b_hT = [Buf() for _ in range(KC)]
        for kc in range(KC):
            S.dma("sp", S.new_slot(), hT[:, kc, :], K.hT_d[:, kc, :], reads=K.b_hT, writes=[b_hT[kc]])
        gain = sb("d_gain", [128, 6, 128], F32)
        rope = sb("d_rope", [128, 2, 32, 16], F32)
        b_rope = Buf()
        s_rope = S.new_slot()
        mask2 = sb("d_mask2", [128, 2, 256], BF16)
        b_cst = Buf()
        scst = S.new_slot()
        S.dma("sp", scst, gain[:], K.inp["dsw_gain"][:, j, :, :], writes=[b_cst])
        for hd in range(2):
            S.op("pool", lambda e, hd=hd: e.tensor_copy(out=mask2[:, hd, 0:128], in_=K.cst_f[:, 2, :]), reads=[K.b_const], writes=[b_cst])
            S.op("pool", lambda e, hd=hd: e.tensor_copy(out=mask2[:, hd, 128:256], in_=K.cst_f[:, 4, :]), reads=[K.b_const], writes=[b_cst])
        pA = [_ps(K, ph, "d_pA%d" % i, [128, 512], F32) for i in range(NL)]
        b_pA = [PB() for _ in range(NL)]
        pB = [pA[i][:].rearrange("p (a c) -> p a c", c=256) for i in range(NL)]
        b_pB = b_pA
        wsl = [sb("d_wsl%d" % i, [128, KC, 512], BF16) for i in range(2)]
        b_wsl = [Buf(), Buf()]
        s_wsl = [S.new_slot(sw=True), S.new_slot(sw=True)]
        qT = sb("d_qT", [128, 4, NB, 128], BF16)
        kT = sb("d_kT", [128, 4, NB, 128], BF16)
        b_qT = [Buf() for _ in range(NB)]
        b_kT = [Buf() for _ in range(NB)]
        Va = sb("d_Va", [128, NB, 4, 130], BF16)
        b_Va = [Buf() for _ in range(NB)]
        S.op("pool", lambda e: e.memset(Va[:].rearrange("p a b c -> p (a b c)"), 1.0), writes=b_Va)
        junk = sb("d_junk", [128, 128], F32)
        LA = []
        for i in range(NL):
            ypt = sb("d_ypt%d" % i, [128, 512], BF16)
            rst = sb("d_rst%d" % i, [128, 264], F32)
            b_ypt, b_rst = Buf(), Buf()
            d = {"ss": sb("d_ss%d" % i, [128, 8], F32), "b_ss": Buf(),
                 "y": ypt[:].rearrange("p (a c) -> p a c", c=128), "b_y": b_ypt,
                 "rt": rst[:, 0:256].rearrange("p (t a c) -> p t a c", t=2, a=4), "b_rt": b_rst,
                 "pT": ypt[:].rearrange("p (a c) -> p a c", c=256), "b_pT": b_ypt,
                 "st": rst[:].rearrange("p (a c) -> p a c", c=132), "b_st": b_rst, "s_st": S.new_slot()}
            LA.append(d)
        wv = K.inp["dsw_w_in"][j].rearrange("(kc p) n -> p kc n", p=128)
        slabs = [(g, hh, t3) for g in range(3) for hh in range(2) for t3 in range(3)]

        def load_slab(si):
            g, hh, t3 = slabs[si]
            col0 = ((g * 3 + t3) * 8 + hh * 4) * 128
            S.dma("pool", s_wsl[si % 2], wsl[si % 2][:], wv[:, :, col0:col0 + 512], writes=[b_wsl[si % 2]])

        def proj_blk(li, si, b):
            g, hh, t3 = slabs[si]
            dl = LA[li]
            dd = DSW_DIL[g]
            nsb = NB // dd
            r, sbk = b // nsb, b % nsb
            start = r + dd * sbk * 128
            W = wsl[si % 2]
            pa, bpa = pA[li], b_pA[li]
            S.group("pe", [lambda e, kc=kc: e.matmul(pa[:], lhsT=hT[:, kc, start:start + 127 * dd + 1:dd], rhs=W[:, kc, :],
                                                    start=(kc == 0), stop=(kc == KC - 1)) for kc in range(KC)],
                    reads=b_hT + [b_wsl[si % 2]], writes=[bpa])
            yield
            if t3 == 2:
                S.op("act", lambda e: e.copy(out=Va[:, b, :, 0:128], in_=pa[:].rearrange("p (a c) -> p a c", c=128)),
                     reads=[bpa], writes=[b_Va[b]])
                return
            ss, y, rt = dl["ss"], dl["y"], dl["rt"]
            for hd in range(4):
                S.op("act", lambda e, hd=hd: e.activation(out=junk[:], in_=pa[:, hd * 128:(hd + 1) * 128], func=AF.Square,
                                                         accum_out=ss[:, hd:hd + 1]), reads=[bpa], writes=[dl["b_ss"]])
                if hd == 1:
                    yield
            yield
            S.op("act", lambda e: e.activation(out=ss[:, 4:8], in_=ss[:, 0:4], func=AF.Ln, bias=K.eps1024[:, 1:2], scale=1.0 / 128.0),
                 reads=[dl["b_ss"], K.b_const], writes=[dl["b_ss"]])
            S.op("act", lambda e: e.activation(out=ss[:, 4:8], in_=ss[:, 4:8], func=AF.Exp, scale=-0.5), reads=[dl["b_ss"]], writes=[dl["b_ss"]])
            yield
            for hd in range(4):
                S.op("dve", lambda e, hd=hd: e.scalar_tensor_tensor(
                    out=y[:, hd, :], in0=pa[:, hd * 128:(hd + 1) * 128], scalar=ss[:, 4 + hd:5 + hd], in1=gain[:, g * 2 + t3, :],
                    op0=ALU.mult, op1=ALU.mult), reads=[bpa, dl["b_ss"], b_cst], writes=[dl["b_y"]])
                if hd == 1:
                    yield
            yield
            cs2 = rope[:, 0, b, :].unsqueeze(1).unsqueeze(1).to_broadcast([128, 4, 2, 16])
            sn2 = rope[:, 1, b, :].unsqueeze(1).unsqueeze(1).to_broadcast([128, 4, 2, 16])
            y32 = y[:, :, 0:32].rearrange("p a (t c) -> p a t c", t=2)
            S.op("dve", lambda e: e.tensor_tensor(out=rt[:, 0, :, :].rearrange("p a (t c) -> p a t c", t=2), in0=y32, in1=cs2, op=ALU.mult),
                 reads=[dl["b_y"], b_rope], writes=[dl["b_rt"]])
            S.op("dve", lambda e: e.tensor_tensor(out=rt[:, 1, :, :].rearrange("p a (t c) -> p a t c", t=2), in0=y32, in1=sn2, op=ALU.mult),
                 reads=[dl["b_y"], b_rope], writes=[dl["b_rt"]])
            yield
            S.op("dve", lambda e: e.tensor_tensor(out=y[:, :, 0:16], in0=rt[:, 0, :, 0:16], in1=rt[:, 1, :, 16:32], op=ALU.subtract),
                 reads=[dl["b_rt"]], writes=[dl["b_y"]])
            S.op("dve", lambda e: e.tensor_tensor(out=y[:, :, 16:32], in0=rt[:, 0, :, 16:32], in1=rt[:, 1, :, 0:16], op=ALU.add),
                 reads=[dl["b_rt"]], writes=[dl["b_y"]])
            yield
            S.group("pe", [lambda e, hd=hd: e.matmul(pa[:, hd * 128:(hd + 1) * 128], lhsT=y[:, hd, :], rhs=K.ident_bf, start=True, stop=True)
                           for hd in range(4)], reads=[dl["b_y"], K.b_const], writes=[bpa])
            yield
            dst = qT if t3 == 0 else kT
            bd = b_qT[b] if t3 == 0 else b_kT[b]
            S.op("act", lambda e: e.copy(out=dst[:, :, b, :], in_=pa[:].rearrange("p (a c) -> p a c", c=128)), reads=[bpa], writes=[bd])
            yield

        def att_unit(li, g, hh, b, pr):
            dl = LA[li]
            dd = DSW_DIL[g]
            nsb = NB // dd
            r, sbk = b // nsb, b % nsb
            has_prev = sbk > 0
            ps_, bps = pB[li], b_pB[li]
            fns = []
            for h2 in range(2):
                hd = pr * 2 + h2
                fns.append(lambda e, hd=hd, h2=h2: e.matmul(ps_[:, h2, 0:128], lhsT=kT[:, hd, b, :], rhs=qT[:, hd, b, :], start=True, stop=True))
                if has_prev:
                    fns.append(lambda e, hd=hd, h2=h2: e.matmul(ps_[:, h2, 128:256], lhsT=kT[:, hd, b - 1, :], rhs=qT[:, hd, b, :],
                                                               start=True, stop=True))
            S.group("pe", fns, reads=[b_qT[b], b_kT[b]] + ([b_kT[b - 1]] if has_prev else []), writes=[bps])
            yield
            p_, bp = dl["pT"], dl["b_pT"]
            wd = 256 if has_prev else 128
            S.op("act", lambda e: e.activation(out=p_[:, :, 0:wd], in_=ps_[:, :, 0:wd], func=AF.Exp, scale=SC), reads=[bps], writes=[bp])
            yield
            S.op("dve", lambda e: e.tensor_tensor(out=p_[:, :, 0:wd], in0=p_[:, :, 0:wd], in1=mask2[:, :, 0:wd], op=ALU.mult),
                 reads=[bp, b_cst], writes=[bp])
            yield
            fns = []
            for h2 in range(2):
                hd = pr * 2 + h2
                fns.append(lambda e, hd=hd, h2=h2: e.matmul(ps_[:, h2, 0:129], lhsT=p_[:, h2, 0:128], rhs=Va[:, b, hd, 0:129],
                                                           start=True, stop=not has_prev, skip_group_check=True))
                if has_prev:
                    fns.append(lambda e, hd=hd, h2=h2: e.matmul(ps_[:, h2, 0:129], lhsT=p_[:, h2, 128:256], rhs=Va[:, b - 1, hd, 0:129],
                                                               start=False, stop=True, skip_group_check=True))
            S.group("pe", fns, reads=[bp, b_Va[b]] + ([b_Va[b - 1]] if has_prev else []), writes=[bps])
            yield
            st_, bst = dl["st"], dl["b_st"]
            S.op("dve", lambda e: e.tensor_copy(out=st_[:, :, 0:129], in_=ps_[:, :, 0:129]), reads=[bps], writes=[bst])
            tok0 = r + dd * sbk * 128
            h0 = hh * 4 + pr * 2
            S.dma("sp", dl["s_st"], K.num_d[g, tok0:tok0 + 127 * dd + 1:dd, h0:h0 + 2, :], st_, reads=[bst], writes=[])
            yield

        load_slab(0)
        for si in range(len(slabs)):
            g, hh, t3 = slabs[si]
            if si + 1 < len(slabs):
                load_slab(si + 1)
            if hh == 0 and t3 == 0:
                S.dma("sp", s_rope, rope[:], K.inp["rope"][:, g, :, :, :], writes=[b_rope])
            run_rolling([(lambda li, b=b, si=si: proj_blk(li, si, b)) for b in range(NB)], NL, stagger=3)
            if t3 == 2:
                units = [(b, pr) for b in range(NB) for pr in range(2)]
                run_rolling([(lambda li, b=b, pr=pr, g=g, hh=hh: att_unit(li, g, hh, b, pr)) for (b, pr) in units], NL, stagger=1)
        S.barrier()
    with contextlib.ExitStack() as ph:
        sb = lambda n, shp, dt: _sb(K, ph, n, shp, dt)
        NC_ = 4
        LB = []
        for i in range(NC_):
            d = {"nin": [sb("d_nin%d_%d" % (i, g), [128, 8, 132], F32) for g in range(3)], "b_nin": [Buf() for g in range(3)],
                 "s_nin": [S.new_slot() for g in range(3)], "rden": sb("d_rden%d" % i, [128, 8], F32), "b_rden": Buf(),
                 "otm": sb("d_otm%d" % i, [128, 8, 128], BF16), "b_otm": Buf(),
                 "ps": _ps(K, ph, "d_pc%d_a" % i, [128, 512], F32), "ps2": _ps(K, ph, "d_pc%d_b" % i, [128, 512], F32),
                 "b_ps": PB(), "b_ps2": PB(),
                 "ot": sb("d_ot%d" % i, [128, KC, 128], BF16), "b_ot": Buf(), "s_o": S.new_slot()}
            LB.append(d)

        def comb(li, blk):
            d = LB[li]
            nin, bn = d["nin"], d["b_nin"]
            for g in range(3):
                S.dma("sp", d["s_nin"][g], nin[g][:], K.num_d[g, blk * 128:(blk + 1) * 128, :, :], reads=[K.b_num], writes=[bn[g]])
            yield
            S.op("dve", lambda e: e.tensor_tensor(out=nin[0][:], in0=nin[0][:], in1=nin[1][:], op=ALU.add), reads=[bn[0], bn[1]], writes=[bn[0]])
            yield
            S.op("dve", lambda e: e.tensor_tensor(out=nin[0][:], in0=nin[0][:], in1=nin[2][:], op=ALU.add), reads=[bn[0], bn[2]], writes=[bn[0]])
            yield
            S.op("dve", lambda e: e.reciprocal(out=d["rden"][:].unsqueeze(2), in_=nin[0][:, :, 128:129]), reads=[bn[0]], writes=[d["b_rden"]])
            yield
            S.op("dve", lambda e: e.tensor_tensor(out=d["otm"][:], in0=nin[0][:, :, 0:128],
                                                 in1=d["rden"][:].unsqueeze(2).to_broadcast([128, 8, 128]), op=ALU.mult),
                 reads=[bn[0], d["b_rden"]], writes=[d["b_otm"]])
            yield
            S.group("pe", [lambda e, hd=hd: e.matmul(d["ps"][:, hd * 128:(hd + 1) * 128], lhsT=d["otm"][:, hd, :], rhs=K.ident_bf,
                                                    start=True, stop=True) for hd in range(4)], reads=[d["b_otm"], K.b_const], writes=[d["b_ps"]])
            S.group("pe", [lambda e, hd=hd: e.matmul(d["ps2"][:, hd * 128:(hd + 1) * 128], lhsT=d["otm"][:, 4 + hd, :], rhs=K.ident_bf,
                                                    start=True, stop=True) for hd in range(4)], reads=[d["b_otm"], K.b_const], writes=[d["b_ps2"]])
            yield
            S.op("act", lambda e: e.copy(out=d["ot"][:, 0:4, :], in_=d["ps"][:].rearrange("p (a c) -> p a c", c=128)), reads=[d["b_ps"]], writes=[d["b_ot"]])
            yield
            S.op("act", lambda e: e.copy(out=d["ot"][:, 4:8, :], in_=d["ps2"][:].rearrange("p (a c) -> p a c", c=128)), reads=[d["b_ps2"]], writes=[d["b_ot"]])
            S.dma("sp", d["s_o"], K.oT_d[:, :, blk * 128:(blk + 1) * 128], d["ot"][:], reads=[d["b_ot"]], writes=[K.b_oT[blk // 2]])
            yield

        run_rolling([(lambda li, blk=blk: comb(li, blk)) for blk in range(T // 128)], NC_, stagger=2)
        S.barrier()


MIXERS["dsw"] = phase_dsw


def build(n_layers=DEPTH, mixers=True, dbg=None, mixer_seq=None):
    nc = bass.Bass("TRN2", target_bir_lowering=False)
    K = Ctx()
    K.nc = nc
    K.uid = 0
    inp = {}

    def di(name, shape, dt=F32):
        inp[name] = nc.dram_tensor(name, shape, dt, kind="ExternalInput").ap()

    di("x", [T, D])
    di("cT", [128, KC])
    di("mod_w", [DEPTH, D, 6 * D])
    di("modb", [128, DEPTH, 48])
    di("mixg", [128, DEPTH, KC])
    di("ffng", [128, DEPTH, KC])
    di("gdn_w_in", [2, D, GDN_IN])
    di("gdn_conv", [128, 2, 24, 4])
    di("gdn_hc", [128, 2, 2, 8])
    di("gdn_ng", [128, 2])
    di("gdn_w_out", [2, D, D])
    di("dsw_w_in", [2, D, DSW_IN])
    di("dsw_gain", [128, 2, 6, 128])
    di("dsw_w_out", [2, D, D])
    di("ffn_w_gate_up", [DEPTH, D, 2 * FH])
    di("ffn_w_down", [DEPTH, FH, D])
    di("cst", [128, 6, 128])
    di("rope", [128, 3, 2, 32, 16])
    di("cst2", [128, 6, 2, 128])
    K.inp = inp
    K.out = {"y": nc.dram_tensor("y", [T, D], F32, kind="ExternalOutput").ap()}
    sk = "ExternalOutput" if dbg is not None else "Internal"
    K.xT_d = nc.dram_tensor("xT_d", [128, KC, T], F32, kind=sk).ap()
    K.hT_d = nc.dram_tensor("hT_d", [128, KC, T], BF16, kind=sk).ap()
    K.oT_d = nc.dram_tensor("oT_d", [128, KC, T], BF16, kind=sk).ap()
    K.h2T_d = nc.dram_tensor("h2T_d", [128, KC, T], BF16, kind=sk).ap()
    K.num_d = nc.dram_tensor("num_d", [3, T, 8, 132], F32, kind="Internal").ap()
    K.F_d = nc.dram_tensor("F_d", [32, 128, T], BF16, kind="Internal").ap()
    K.b_F = Buf()
    K.b_xT = [Buf() for _ in range(NT)]
    K.b_hT = [Buf() for _ in range(NT)]
    K.b_oT = [Buf() for _ in range(NT)]
    K.b_h2T = [Buf() for _ in range(NT)]
    K.b_num = Buf()
    K.b_y = Buf()
    K.b_ys = [Buf(), Buf()]
    K.dbg = dbg
    if dbg:
        for nm, (shape, dt) in dbg.items():
            K.out[nm] = nc.dram_tensor(nm, shape, dt, kind="ExternalOutput").ap()

    with contextlib.ExitStack() as st:
        S = Sched(nc, st)
        K.S = S
        K.st = st
        K.cst_f = st.enter_context(nc.sbuf_tensor("cst_f", [128, 6, 128], F32))
        K.cst_b = st.enter_context(nc.sbuf_tensor("cst_b", [128, 6, 128], BF16))
        K.eps1024 = st.enter_context(nc.sbuf_tensor("eps1024", [128, 4], F32))
        K.b_const = Buf()
        K.cst2 = st.enter_context(nc.sbuf_tensor("cst2_sb", [128, 6, 2, 128], BF16))
        s0 = S.new_slot()
        s0w = S.new_slot(sw=True)
        K.b_const2 = Buf()
        S.dma("pool", s0w, K.cst2[:], inp["cst2"], writes=[K.b_const2])
        S.dma("pool", s0w, K.cst_b[:], inp["cst"], writes=[K.b_const2])
        S.dma("sp", s0, K.cst_f[:], inp["cst"], writes=[K.b_const])
        S.op("dve", lambda e: e.memset(K.eps1024[:, 0:1], 1024.0 * EPS), writes=[K.b_const])
        S.op("dve", lambda e: e.memset(K.eps1024[:, 1:2], EPS), writes=[K.b_const])
        S.op("dve", lambda e: e.memset(K.eps1024[:, 2:3], 1.0), writes=[K.b_const])
        S.op("dve", lambda e: e.memset(K.eps1024[:, 3:4], 0.0), writes=[K.b_const])
        S.barrier()
        K.ident_f = K.cst_f[:, 0, :]
        K.ones_f = K.cst_f[:, 1, :]
        K.triU_f = K.cst_f[:, 2, :]
        K.triUs_f = K.cst_f[:, 3, :]
        K.ident_bf = K.cst_b[:, 0, :]
        K.ones_bf = K.cst_b[:, 1, :]
        K.vec = {nm: st.enter_context(nc.sbuf_tensor("v_" + nm, [128, DEPTH, KC], F32))
                 for nm in ("gs1", "sh1", "g1", "gs2", "sh2", "g2")}
        K.b_vec = Buf()

        phase_mod(K)
        phase_x0(K)
        for l in range(n_layers):
            j = l // 2
            mix = mixer_seq[l] if mixer_seq else ("gdn" if l % 2 == 0 else "dsw")
            if mixers:
                MIXERS[mix](K, l, j)
            else:
                stub_mixer(K)
            phase_t1(K, l, inp["gdn_w_out"][j] if mix == "gdn" else inp["dsw_w_out"][j])
            phase_t2(K, l, inp["ffn_w_gate_up"][l], inp["ffn_w_down"][l], last=(l == n_layers - 1))
        deps = [K.b_ys[0].w, K.b_ys[1].w]
        S._wait("sp", deps)
        K.ninst = S.ninst
    return nc, K


def stub_mixer(K):
    S = K.S
    with contextlib.ExitStack() as ph:
        tb = [_sb(K, ph, "stub%d" % i, [128, KC, TW], BF16) for i in range(2)]
        bt = [Buf(), Buf()]
        sl = [S.new_slot(), S.new_slot()]
        so2_ = [S.new_slot(), S.new_slot()]
        for t in range(NT):
            i = t % 2
            S.dma("sp", sl[i], tb[i][:], K.hT_d[:, :, t * TW:(t + 1) * TW], reads=[K.b_hT[t]], writes=[bt[i]])
            S.dma("sp", so2_[i], K.oT_d[:, :, t * TW:(t + 1) * TW], tb[i][:], reads=[bt[i]], writes=[K.b_oT[t]])
        S.barrier()


def _fm(v):
    v = np.asarray(v, np.float32)
    lead = v.shape[:-1]
    r = v.reshape(lead + (KC, 128))
    r = np.moveaxis(r, -1, 0)
    return np.ascontiguousarray(r)


def host_consts():
    cst = np.zeros((128, 6, 128), np.float32)
    p = np.arange(128)[:, None]
    f = np.arange(128)[None, :]
    cst[:, 0] = (p == f)
    cst[:, 1] = 1.0
    cst[:, 2] = (f >= p)
    cst[:, 3] = (f > p)
    cst[:, 4] = (f <= p)
    cst[:, 5] = np.where(f > p, 0.0, -30000.0)
    half = 8 * 2
    inv = np.exp(-math.log(ROPE_THETA) * (2.0 * np.arange(16, dtype=np.float32) / 32.0)).astype(np.float32)
    rope = np.zeros((128, 3, 2, 32, 16), np.float32)
    for g, d in enumerate(DSW_DIL):
        nsb = (T // d) // 128
        for b in range(32):
            r = b // nsb
            sb = b % nsb
            pos = (r + d * (sb * 128 + np.arange(128))).astype(np.float32)
            ang = (pos[:, None] * inv[None, :]).astype(np.float32)
            rope[:, g, 0, b, :] = np.cos(ang)
            rope[:, g, 1, b, :] = np.sin(ang)
    cst2 = np.zeros((128, 6, 2, 128), np.float32)
    bd = (p // 8 == f // 8).astype(np.float32)
    cst2[:, 0, 0] = bd
    cst2[:, 0, 1] = bd
    for li, sz in enumerate((16, 32, 64, 128)):
        em = ((p // sz == f // sz) & (p % sz >= sz // 2) & (f % sz < sz // 2)).astype(np.float32)
        cst2[:, 1 + li, 0] = em
        cst2[:, 1 + li, 1] = em.T
    cst2[:, 5, 0] = (p == f)
    cst2[:, 5, 1] = (p == f)
    return cst, rope, cst2


def make_in_maps(inputs, n_cores=8):
    f = lambda a: np.ascontiguousarray(np.asarray(a, np.float32))
    cst, rope, cst2 = host_consts()
    mod_b = f(inputs["mod_b"])
    modb = np.ascontiguousarray(np.moveaxis(mod_b.reshape(DEPTH, 48, 128), -1, 0))
    mixg = np.ascontiguousarray(np.moveaxis(f(inputs["mix_norm_g"]).reshape(DEPTH, KC, 128), -1, 0))
    ffng = np.ascontiguousarray(np.moveaxis(f(inputs["ffn_norm_g"]).reshape(DEPTH, KC, 128), -1, 0))
    conv = f(inputs["gdn_conv_w"])
    gdn_conv = np.ascontiguousarray(np.transpose(conv.reshape(2, 4, 24, 128), (3, 0, 2, 1)))
    hc = np.stack([f(inputs["gdn_A_log"]), f(inputs["gdn_dt_bias"])], axis=1)
    gdn_hc = np.ascontiguousarray(np.broadcast_to(hc[None], (128, 2, 2, 8)))
    gdn_ng = np.ascontiguousarray(f(inputs["gdn_norm_g"]).T)
    qg = f(inputs["dsw_q_norm_g"])
    kg = f(inputs["dsw_k_norm_g"])
    gain = np.stack([qg, kg], axis=2).reshape(2, 6, 128)
    dsw_gain = np.ascontiguousarray(np.broadcast_to(gain[None], (128, 2, 6, 128)))
    shared = {
        "mod_w": f(inputs["mod_w"]), "modb": modb, "mixg": mixg, "ffng": ffng,
        "gdn_w_in": f(inputs["gdn_w_in"]), "gdn_conv": gdn_conv, "gdn_hc": gdn_hc, "gdn_ng": gdn_ng,
        "gdn_w_out": f(inputs["gdn_w_out"]), "dsw_w_in": f(inputs["dsw_w_in"]), "dsw_gain": dsw_gain,
        "dsw_w_out": f(inputs["dsw_w_out"]), "ffn_w_gate_up": f(inputs["ffn_w_gate_up"]),
        "ffn_w_down": f(inputs["ffn_w_down"]), "cst": cst, "rope": rope, "cst2": cst2,
    }
    x = f(inputs["x"])
    c = f(inputs["c"])
    maps = []
    for i in range(n_cores):
        b = i % 4
        m = dict(shared)
        m["x"] = np.ascontiguousarray(x[b])
        m["cT"] = np.ascontiguousarray(c[b].reshape(KC, 128).T)
        maps.append(m)
    return maps


def kernel(**inputs):
    nc, K = build()
    maps = make_in_maps(inputs, 8)
    res = run_bass_kernel_spmd(nc, maps, core_ids=list(range(8)))
    out = np.stack([np.asarray(res.results[b]["y"], np.float32) for b in range(4)], axis=0)
    return out
```

```python
import contextlib
import math
import numpy as np
import concourse.bass as bass
import concourse.mybir as mybir
from concourse.alu_op_type import AluOpType as ALU
from concourse.bass_utils import run_bass_kernel_spmd

AF = mybir.ActivationFunctionType
F32 = mybir.dt.float32
BF16 = mybir.dt.bfloat16

D = 1024
T = 4096
DEPTH = 4
KC = 8
FH = 2816
NJ = FH // 128
EPS = 1e-6
TW = 256
NT = T // TW
GDN_IN = 4112
DSW_IN = 9216
DSW_DIL = (1, 4, 16)
ROPE_THETA = 500000.0


class Buf:
    __slots__ = ("w", "r", "name", "excl")

    def __init__(self, name="", excl=False):
        self.w = None
        self.r = []
        self.name = name
        self.excl = excl


def PB():
    return Buf(excl=True)


class Sched:
    def __init__(self, nc, stack):
        self.nc = nc
        self.eng = {"pe": nc.tensor, "dve": nc.vector, "act": nc.scalar,
                    "pool": nc.gpsimd, "sp": nc.sync}
        self.sems = {}
        self.cnt = {}
        self.stack = stack
        for e in self.eng:
            self.sems[e] = stack.enter_context(nc.semaphore("s_" + e))
            self.cnt[e] = 0
        self.waited = {e: {} for e in self.eng}
        self.nslot = 0
        self.ninst = 0
        self.free_slots = []
        self.free_sw = []
        self.live_slots = []

    def new_slot(self, sw=False):
        fl = self.free_sw if sw else self.free_slots
        if fl:
            k = fl.pop()
        else:
            k = ("w%d" if sw else "d%d") % self.nslot
            self.nslot += 1
            self.sems[k] = self.stack.enter_context(self.nc.semaphore("s_" + k))
            self.cnt[k] = 0
        self.live_slots.append(k)
        return k

    def _wait(self, e, deps):
        best = {}
        for d in deps:
            if d is None:
                continue
            k, v = d
            if best.get(k, 0) < v:
                best[k] = v
        w = self.waited[e]
        for k, v in best.items():
            if k == e and e == "pe":
                continue
            if w.get(k, 0) < v:
                self.eng[e].wait_ge(self.sems[k], v)
                w[k] = v
                self.ninst += 1

    @staticmethod
    def _deps(reads, writes):
        deps = []
        for b in reads:
            deps.append(b.w)
            if b.excl:
                deps.extend(b.r)
        for b in writes:
            deps.append(b.w)
            deps.extend(b.r)
        return deps

    @staticmethod
    def _stamp(st, reads, writes):
        for b in reads:
            if b.excl:
                b.w = st
                b.r = []
                continue
            b.r.append(st)
            if len(b.r) > 64:
                best = {}
                for k, v in b.r:
                    if best.get(k, 0) < v:
                        best[k] = v
                b.r = list(best.items())
        for b in writes:
            b.w = st
            b.r = []

    def op(self, e, fn, reads=(), writes=()):
        self._wait(e, self._deps(reads, writes))
        inst = fn(self.eng[e])
        self.cnt[e] += 1
        inst.then_inc(self.sems[e], 1)
        self.ninst += 1
        self._stamp((e, self.cnt[e]), reads, writes)
        return inst

    def group(self, e, fns, reads=(), writes=()):
        self._wait(e, self._deps(reads, writes))
        inst = None
        for fn in fns:
            inst = fn(self.eng[e])
            self.ninst += 1
        self.cnt[e] += 1
        inst.then_inc(self.sems[e], 1)
        self._stamp((e, self.cnt[e]), reads, writes)
        return inst

    def dma(self, q, slot, out, in_, reads=(), writes=()):
        assert (slot[0] == "w") == (q == "pool"), (q, slot)
        self._wait(q, self._deps(reads, writes))
        inst = self.eng[q].dma_start(out=out, in_=in_)
        self.cnt[slot] += 16
        inst.then_inc(self.sems[slot], 16)
        self.ninst += 1
        self._stamp((slot, self.cnt[slot]), reads, writes)
        return inst

    def barrier(self):
        deps = [(k, v) for k, v in self.cnt.items() if v > 0]
        for e in self.eng:
            w = self.waited[e]
            for k, v in deps:
                if k != e and w.get(k, 0) < v:
                    self.eng[e].wait_ge(self.sems[k], v)
                    w[k] = v
        for k in self.live_slots:
            (self.free_sw if k[0] == "w" else self.free_slots).append(k)
        self.live_slots = []


class Ctx:
    pass


def _sb(K, ph, name, shape, dt):
    K.uid += 1
    return ph.enter_context(K.nc.sbuf_tensor("%s_%d" % (name, K.uid), shape, dt))


def _ps(K, ph, name, shape, dt):
    K.uid += 1
    return ph.enter_context(K.nc.psum_tensor("%s_%d" % (name, K.uid), shape, dt))


def phase_mod(K):
    nc, S = K.nc, K.S
    with contextlib.ExitStack() as ph:
        cT = _sb(K, ph, "cT", [128, KC], F32)
        cond = _sb(K, ph, "cond", [128, KC], BF16)
        mixg = _sb(K, ph, "mixg", [128, DEPTH, KC], F32)
        ffng = _sb(K, ph, "ffng", [128, DEPTH, KC], F32)
        modb = _sb(K, ph, "modb", [128, DEPTH, 48], F32)
        modv = _sb(K, ph, "modv", [128, 48], F32)
        wsl = [_sb(K, ph, "mwsl%d" % i, [128, KC, 512], BF16) for i in range(2)]
        mps = _ps(K, ph, "modps", [128, 512], F32)
        b_c, b_cond, b_mixg, b_ffng, b_modb, b_modv = [Buf() for _ in range(6)]
        b_mps = PB()
        b_w = [Buf(), Buf()]
        sl = [S.new_slot(sw=True), S.new_slot(sw=True)]
        s0 = S.new_slot()
        S.dma("sp", s0, cT[:], K.inp["cT"], writes=[b_c])
        S.dma("sp", S.new_slot(), mixg[:], K.inp["mixg"], writes=[b_mixg])
        S.dma("sp", S.new_slot(), ffng[:], K.inp["ffng"], writes=[b_ffng])
        S.dma("sp", S.new_slot(), modb[:], K.inp["modb"], writes=[b_modb])
        S.op("act", lambda e: e.activation(out=cond[:], in_=cT[:], func=AF.Silu), reads=[b_c], writes=[b_cond])
        for l in range(DEPTH):
            wv = K.inp["mod_w"][l].rearrange("(kc p) n -> p kc n", p=128)
            for s in range(12):
                w = wsl[s % 2]
                bw = b_w[s % 2]
                S.dma("pool", sl[s % 2], w[:], wv[:, :, s * 512:(s + 1) * 512], writes=[bw])
                for mm in range(4):
                    m = s * 4 + mm
                    S.group("pe", [lambda e, kc=kc, mm=mm, m=m, w=w: e.matmul(
                        mps[:, m:m + 1], lhsT=w[:, kc, mm * 128:(mm + 1) * 128], rhs=cond[:, kc:kc + 1],
                        start=(kc == 0), stop=(kc == KC - 1)) for kc in range(KC)], reads=[bw, b_cond], writes=[b_mps])
            S.op("dve", lambda e, l=l: e.tensor_tensor(out=modv[:], in0=mps[:, 0:48], in1=modb[:, l, :], op=ALU.add),
                 reads=[b_mps, b_modb], writes=[b_modv])
            V = K.vec
            bv = K.b_vec
            S.op("dve", lambda e, l=l: e.scalar_tensor_tensor(out=V["gs1"][:, l, :], in0=modv[:, 8:16], scalar=1.0,
                                                             in1=mixg[:, l, :], op0=ALU.add, op1=ALU.mult),
                 reads=[b_modv, b_mixg], writes=[bv])
            S.op("dve", lambda e, l=l: e.tensor_scalar(out=V["gs1"][:, l, :], in0=V["gs1"][:, l, :], scalar1=32.0,
                                                      scalar2=None, op0=ALU.mult), reads=[bv], writes=[bv])
            S.op("dve", lambda e, l=l: e.scalar_tensor_tensor(out=V["gs2"][:, l, :], in0=modv[:, 32:40], scalar=1.0,
                                                             in1=ffng[:, l, :], op0=ALU.add, op1=ALU.mult),
                 reads=[b_modv, b_ffng], writes=[bv])
            S.op("dve", lambda e, l=l: e.tensor_scalar(out=V["gs2"][:, l, :], in0=V["gs2"][:, l, :], scalar1=32.0,
                                                      scalar2=None, op0=ALU.mult), reads=[bv], writes=[bv])
            for nm, off in (("sh1", 0), ("g1", 16), ("sh2", 24), ("g2", 40)):
                S.op("dve", lambda e, l=l, nm=nm, off=off: e.tensor_copy(out=V[nm][:, l, :], in_=modv[:, off:off + 8]),
                     reads=[b_modv], writes=[bv])
        if K.dbg is not None and "dbg_vec" in K.dbg:
            for i, nm in enumerate(("gs1", "sh1", "g1", "gs2", "sh2", "g2")):
                S.dma("sp", S.new_slot(), K.out["dbg_vec"][:, i, :, :], K.vec[nm][:], reads=[K.b_vec], writes=[Buf()])
        S.barrier()


def emit_norm(K, N, x, b_x, h, b_h, gs, sh, W):
    S = K.S
    S.op("act", lambda e: e.activation(out=N["sq"][:, :, 0:W], in_=x[:, :, 0:W], func=AF.Square),
         reads=[b_x], writes=[N["b_sq"]])
    S.group("pe", [lambda e, kc=kc: e.matmul(N["ss"][:, 0:W], lhsT=K.ones_bf, rhs=N["sq"][:, kc, 0:W],
                                             start=(kc == 0), stop=(kc == KC - 1)) for kc in range(KC)],
            reads=[N["b_sq"], K.b_const], writes=[N["b_ss"]])
    S.op("act", lambda e: e.activation(out=N["rstd"][:, 0:W], in_=N["ss"][:, 0:W], func=AF.Ln,
                                       bias=K.eps1024[:, 0:1], scale=1.0),
         reads=[N["b_ss"], K.b_const], writes=[N["b_rstd"]])
    S.op("act", lambda e: e.activation(out=N["rstd"][:, 0:W], in_=N["rstd"][:, 0:W], func=AF.Exp, scale=-0.5),
         reads=[N["b_rstd"]], writes=[N["b_rstd"]])
    for kc in range(KC):
        S.op("dve", lambda e, kc=kc: e.tensor_tensor(out=N["tmp"][:, kc, 0:W], in0=x[:, kc, 0:W],
                                                    in1=N["rstd"][:, 0:W], op=ALU.mult),
             reads=[b_x, N["b_rstd"]], writes=[N["b_tmp"]])
        S.op("act", lambda e, kc=kc: e.activation(out=h[:, kc, 0:W], in_=N["tmp"][:, kc, 0:W], func=AF.Identity,
                                                 bias=sh[:, kc:kc + 1], scale=gs[:, kc:kc + 1]),
             reads=[N["b_tmp"], K.b_vec], writes=[b_h])


def norm_gen(K, N, x, b_x, h, b_h, gs, sh, W):
    S = K.S
    S.op("act", lambda e: e.activation(out=N["sq"][:, :, 0:W], in_=x[:, :, 0:W], func=AF.Square),
         reads=[b_x], writes=[N["b_sq"]])
    yield
    S.group("pe", [lambda e, kc=kc: e.matmul(N["ss"][:, 0:W], lhsT=K.ones_bf, rhs=N["sq"][:, kc, 0:W],
                                             start=(kc == 0), stop=(kc == KC - 1)) for kc in range(KC)],
            reads=[N["b_sq"], K.b_const], writes=[N["b_ss"]])
    yield
    S.op("act", lambda e: e.activation(out=N["rstd"][:, 0:W], in_=N["ss"][:, 0:W], func=AF.Ln,
                                       bias=K.eps1024[:, 0:1], scale=1.0),
         reads=[N["b_ss"], K.b_const], writes=[N["b_rstd"]])
    yield
    S.op("act", lambda e: e.activation(out=N["rstd"][:, 0:W], in_=N["rstd"][:, 0:W], func=AF.Exp, scale=-0.5),
         reads=[N["b_rstd"]], writes=[N["b_rstd"]])
    yield
    for kc in range(KC):
        S.op("dve", lambda e, kc=kc: e.tensor_tensor(out=N["tmp"][:, kc, 0:W], in0=x[:, kc, 0:W],
                                                    in1=N["rstd"][:, 0:W], op=ALU.mult),
             reads=[b_x, N["b_rstd"]], writes=[N["b_tmpk"][kc]])
        yield
        S.op("act", lambda e, kc=kc: e.activation(out=h[:, kc, 0:W], in_=N["tmp"][:, kc, 0:W], func=AF.Identity,
                                                 bias=sh[:, kc:kc + 1], scale=gs[:, kc:kc + 1]),
             reads=[N["b_tmpk"][kc], K.b_vec], writes=[b_h])
        yield


def alloc_norm(K, ph, W):
    N = {}
    N["sq"] = _sb(K, ph, "nsq", [128, KC, W], BF16)
    N["tmp"] = _sb(K, ph, "ntmp", [128, KC, W], F32)
    N["rstd"] = _sb(K, ph, "nrstd", [128, W], F32)
    N["ss"] = _ps(K, ph, "nss", [128, 512], F32)
    for k in ("b_sq", "b_tmp", "b_rstd"):
        N[k] = Buf()
    N["b_ss"] = PB()
    N["b_tmpk"] = [Buf() for _ in range(KC)]
    return N


def phase_x0(K):
    nc, S = K.nc, K.S
    with contextlib.ExitStack() as ph:
        xin = [_sb(K, ph, "xin%d" % i, [128, D], F32) for i in range(2)]
        b_xin = [Buf(), Buf()]
        xt = [_sb(K, ph, "xt%d" % i, [128, KC, TW], F32) for i in range(2)]
        b_xt = [Buf(), Buf()]
        ht = [_sb(K, ph, "ht%d" % i, [128, KC, TW], BF16) for i in range(2)]
        b_ht = [Buf(), Buf()]
        pt = [_ps(K, ph, "x0pt%d" % i, [128, 4, 128], F32) for i in range(2)]
        b_pt = [PB(), PB()]
        N = alloc_norm(K, ph, TW)
        sl = [S.new_slot(), S.new_slot()]
        so = [S.new_slot(), S.new_slot()]
        so2 = [S.new_slot(), S.new_slot()]
        for blk in range(T // 128):
            xi = xin[blk % 2]
            bxi = b_xin[blk % 2]
            S.dma("sp", sl[blk % 2], xi[:], K.inp["x"][blk * 128:(blk + 1) * 128, :], writes=[bxi])
            ti = (blk // 2) % 2
            half = blk % 2
            for hf in range(2):
                S.group("pe", [lambda e, q=q, hf=hf, xi=xi: e.matmul(
                    pt[hf][:, q, :], lhsT=xi[:, (hf * 4 + q) * 128:(hf * 4 + q + 1) * 128], rhs=K.ident_f, start=True, stop=True)
                    for q in range(4)], reads=[bxi, K.b_const], writes=[b_pt[hf]])
                S.op("act" if hf == 0 else "dve",
                     (lambda e, hf=hf, ti=ti, half=half: e.copy(out=xt[ti][:, hf * 4:(hf + 1) * 4, half * 128:(half + 1) * 128], in_=pt[hf][:]))
                     if hf == 0 else
                     (lambda e, hf=hf, ti=ti, half=half: e.tensor_copy(out=xt[ti][:, hf * 4:(hf + 1) * 4, half * 128:(half + 1) * 128], in_=pt[hf][:])),
                     reads=[b_pt[hf]], writes=[b_xt[ti]])
            if half == 1:
                t = blk // 2
                emit_norm(K, N, xt[ti], b_xt[ti], ht[ti], b_ht[ti], K.vec["gs1"][:, 0, :], K.vec["sh1"][:, 0, :], TW)
                S.dma("sp", so[ti], K.xT_d[:, :, t * TW:(t + 1) * TW], xt[ti][:], reads=[b_xt[ti]], writes=[K.b_xT[t]])
                S.dma("sp", so2[ti], K.hT_d[:, :, t * TW:(t + 1) * TW], ht[ti][:], reads=[b_ht[ti]], writes=[K.b_hT[t]])
        S.barrier()


def phase_t1(K, l, w_out_ap):
    nc, S = K.nc, K.S
    NLT = 3
    with contextlib.ExitStack() as ph:
        wo = _sb(K, ph, "wo", [128, KC, D], BF16)
        b_wo = Buf()
        S.dma("pool", S.new_slot(sw=True), wo[:], w_out_ap.rearrange("(kc p) n -> p kc n", p=128), writes=[b_wo])
        V = K.vec
        LT = []
        for i in range(NLT):
            d = {"xt": _sb(K, ph, "t1x%d" % i, [128, KC, TW], F32), "b_xt": Buf(), "s_x": S.new_slot(),
                 "ot": _sb(K, ph, "t1o%d" % i, [128, KC, TW], BF16), "b_ot": Buf(), "s_o": S.new_slot(),
                 "ht": _sb(K, ph, "t1h%d" % i, [128, KC, TW], BF16), "b_ht": Buf(),
                 "acc": _ps(K, ph, "t1acc%d" % i, [128, 2, TW], F32), "b_acc": PB(),
                 "N": alloc_norm(K, ph, TW), "s_so": S.new_slot(), "s_so2": S.new_slot()}
            LT.append(d)

        def job(li, t):
            d = LT[li]
            xt, ot, ht, acc = d["xt"], d["ot"], d["ht"], d["acc"]
            S.dma("sp", d["s_x"], xt[:], K.xT_d[:, :, t * TW:(t + 1) * TW], reads=[K.b_xT[t]], writes=[d["b_xt"]])
            S.dma("sp", d["s_o"], ot[:], K.oT_d[:, :, t * TW:(t + 1) * TW], reads=[K.b_oT[t]], writes=[d["b_ot"]])
            yield
            for m in range(KC):
                S.group("pe", [lambda e, kc=kc, m=m: e.matmul(
                    acc[:, m % 2, :], lhsT=wo[:, kc, m * 128:(m + 1) * 128], rhs=ot[:, kc, :],
                    start=(kc == 0), stop=(kc == KC - 1)) for kc in range(KC)], reads=[b_wo, d["b_ot"]], writes=[d["b_acc"]])
                yield
                S.op("dve", lambda e, m=m: e.scalar_tensor_tensor(
                    out=xt[:, m, :], in0=acc[:, m % 2, :], scalar=V["g1"][:, l, m:m + 1], in1=xt[:, m, :],
                    op0=ALU.mult, op1=ALU.add), reads=[d["b_acc"], d["b_xt"], K.b_vec], writes=[d["b_xt"]])
                yield
            yield from norm_gen(K, d["N"], xt, d["b_xt"], ht, d["b_ht"], V["gs2"][:, l, :], V["sh2"][:, l, :], TW)
            S.dma("sp", d["s_so"], K.xT_d[:, :, t * TW:(t + 1) * TW], xt[:], reads=[d["b_xt"]], writes=[K.b_xT[t]])
            S.dma("sp", d["s_so2"], K.h2T_d[:, :, t * TW:(t + 1) * TW], ht[:], reads=[d["b_ht"]], writes=[K.b_h2T[t]])
            yield

        run_rolling([(lambda li, t=t: job(li, t)) for t in range(NT)], NLT, stagger=6)
        S.barrier()


def phase_t2(K, l, w_gu_ap, w_dn_ap, last):
    nc, S = K.nc, K.S
    TW2 = 512
    NT2 = T // TW2
    with contextlib.ExitStack() as ph:
        wgu = _sb(K, ph, "wgu", [128, KC, 2 * FH], BF16)
        wdn = _sb(K, ph, "wdn", [128, NJ, D], BF16)
        b_wgu = [Buf() for _ in range(KC)]
        b_wdn = Buf()
        xt = _sb(K, ph, "t2x", [128, KC, TW2], F32)
        b_xt = Buf()
        h2 = [_sb(K, ph, "t2h%d" % i, [128, KC, TW2], BF16) for i in range(2)]
        b_h2 = [Buf(), Buf()]
        act = _sb(K, ph, "t2act", [128, NJ, TW2], BF16)
        b_act = [Buf() for _ in range(NJ)]
        sg = [_sb(K, ph, "t2sg", [128, TW2], F32)] * 2
        b_sg = [Buf()] * 2
        pg = [_ps(K, ph, "t2pg%d" % i, [128, TW2], F32) for i in range(4)]
        b_pg = [PB() for _ in range(4)]
        pacc = [_ps(K, ph, "t2pa%d" % i, [128, TW2], F32) for i in range(4)]
        b_pacc = [PB() for _ in range(4)]
        slx = S.new_slot()
        slh = [S.new_slot(), S.new_slot()]
        so = S.new_slot()
        so2 = S.new_slot()
        gv = w_gu_ap.rearrange("(kc p) n -> p kc n", p=128)
        for kc in range(KC):
            S.dma("pool", S.new_slot(sw=True), wgu[:, kc, :], gv[:, kc, :], writes=[b_wgu[kc]])
        S.dma("pool", S.new_slot(sw=True), wdn[:], w_dn_ap.rearrange("(j p) n -> p j n", p=128), writes=[b_wdn])
        V = K.vec
        if last:
            osb = [_sb(K, ph, "t2os%d" % i, [128, D], F32) for i in range(2)]
            b_osb = [Buf(), Buf()]
            s_os = [S.new_slot(), S.new_slot()]
        else:
            N = {"sq": _sb(K, ph, "t2nsq", [128, 2, TW2], BF16), "tmp": _sb(K, ph, "t2ntmp", [128, 2, TW2], F32),
                 "rstd": _sb(K, ph, "t2nrstd", [128, TW2], F32), "ss": pacc[0], "b_ss": b_pacc[0],
                 "b_sqk": [Buf(), Buf()], "b_rstd": Buf(), "b_tmpk": [Buf(), Buf()]}
            hn = _sb(K, ph, "t2hn", [128, 4, TW2], BF16)
            b_hn = Buf()

        def load_h2(t):
            S.dma("sp", slh[t % 2], h2[t % 2][:], K.h2T_d[:, :, t * TW2:(t + 1) * TW2],
                  reads=[K.b_h2T[2 * t], K.b_h2T[2 * t + 1]], writes=[b_h2[t % 2]])

        def epilogue(t):
            tb = [K.b_xT[2 * t], K.b_xT[2 * t + 1]]
            if last:
                for sub in range(TW2 // 128):
                    o_ = osb[sub % 2]
                    bo = b_osb[sub % 2]
                    for hf in range(2):
                        p, bp = pacc[hf], b_pacc[hf]
                        S.group("pe", [lambda e, hf=hf, q=q, p=p, sub=sub: e.matmul(
                            p[:, q * 128:(q + 1) * 128], lhsT=xt[:, hf * 4 + q, sub * 128:(sub + 1) * 128], rhs=K.ident_f,
                            start=True, stop=True) for q in range(4)], reads=[b_xt, K.b_const], writes=[bp])
                        yield
                        if hf == 0:
                            S.op("act", lambda e, p=p, o_=o_: e.copy(out=o_[:, 0:512], in_=p[:]), reads=[bp], writes=[bo])
                        else:
                            S.op("dve", lambda e, p=p, o_=o_: e.tensor_copy(out=o_[:, 512:1024], in_=p[:]), reads=[bp], writes=[bo])
                        yield
                    r0 = t * TW2 + sub * 128
                    S.dma("sp", s_os[sub % 2], K.out["y"][r0:r0 + 128, :], o_[:], reads=[bo], writes=[K.b_ys[sub % 2]])
                    yield
                S.dma("sp", so, K.xT_d[:, :, t * TW2:(t + 1) * TW2], xt[:], reads=[b_xt], writes=tb)
                yield
            else:
                for kc in range(KC):
                    S.op("act", lambda e, kc=kc: e.activation(out=N["sq"][:, kc % 2, :], in_=xt[:, kc, :], func=AF.Square),
                         reads=[b_xt], writes=[N["b_sqk"][kc % 2]])
                    yield
                    S.op("pe", lambda e, kc=kc: e.matmul(N["ss"][:], lhsT=K.ones_bf, rhs=N["sq"][:, kc % 2, :],
                                                        start=(kc == 0), stop=(kc == KC - 1)),
                         reads=[N["b_sqk"][kc % 2], K.b_const], writes=[N["b_ss"]])
                    yield
                S.op("act", lambda e: e.activation(out=N["rstd"][:], in_=N["ss"][:], func=AF.Ln, bias=K.eps1024[:, 0:1], scale=1.0),
                     reads=[N["b_ss"], K.b_const], writes=[N["b_rstd"]])
                yield
                S.op("act", lambda e: e.activation(out=N["rstd"][:], in_=N["rstd"][:], func=AF.Exp, scale=-0.5),
                     reads=[N["b_rstd"]], writes=[N["b_rstd"]])
                yield
                gs, sh = V["gs1"][:, l + 1, :], V["sh1"][:, l + 1, :]
                for kc in range(KC):
                    S.op("dve", lambda e, kc=kc: e.tensor_tensor(out=N["tmp"][:, kc % 2, :], in0=xt[:, kc, :], in1=N["rstd"][:], op=ALU.mult),
                         reads=[b_xt, N["b_rstd"]], writes=[N["b_tmpk"][kc % 2]])
                    yield
                    S.op("act", lambda e, kc=kc: e.activation(out=hn[:, kc % 4, :], in_=N["tmp"][:, kc % 2, :], func=AF.Identity,
                                                             bias=sh[:, kc:kc + 1], scale=gs[:, kc:kc + 1]),
                         reads=[N["b_tmpk"][kc % 2], K.b_vec], writes=[b_hn])
                    yield
                    if kc % 4 == 3:
                        k0 = kc - 3
                        S.dma("sp", so2, K.hT_d[:, k0:k0 + 4, t * TW2:(t + 1) * TW2], hn[:], reads=[b_hn],
                              writes=[K.b_hT[2 * t], K.b_hT[2 * t + 1]])
                S.dma("sp", so, K.xT_d[:, :, t * TW2:(t + 1) * TW2], xt[:], reads=[b_xt], writes=tb)
                yield

        load_h2(0)
        pend = None
        for t in range(NT2):
            i = t % 2
            if t + 1 < NT2:
                load_h2(t + 1)
            for j in range(NJ):
                g_, u_ = pg[2 * (j % 2)], pg[2 * (j % 2) + 1]
                bg_, bu_ = b_pg[2 * (j % 2)], b_pg[2 * (j % 2) + 1]
                S.group("pe", [lambda e, kc=kc, j=j, g_=g_, i=i: e.matmul(
                    g_[:], lhsT=wgu[:, kc, j * 128:(j + 1) * 128], rhs=h2[i][:, kc, :],
                    start=(kc == 0), stop=(kc == KC - 1)) for kc in range(KC)], reads=b_wgu + [b_h2[i]], writes=[bg_])
                S.group("pe", [lambda e, kc=kc, j=j, u_=u_, i=i: e.matmul(
                    u_[:], lhsT=wgu[:, kc, FH + j * 128:FH + (j + 1) * 128], rhs=h2[i][:, kc, :],
                    start=(kc == 0), stop=(kc == KC - 1)) for kc in range(KC)], reads=b_wgu + [b_h2[i]], writes=[bu_])
                s_ = sg[j % 2]
                S.op("act", lambda e, g_=g_, s_=s_: e.activation(out=s_[:], in_=g_[:], func=AF.Silu), reads=[bg_], writes=[b_sg[j % 2]])
                S.op("dve", lambda e, u_=u_, s_=s_, j=j: e.tensor_tensor(out=act[:, j, :], in0=u_[:], in1=s_[:], op=ALU.mult),
                     reads=[bu_, b_sg[j % 2]], writes=[b_act[j]])
                if pend is not None:
                    for _ in range(2):
                        try:
                            next(pend)
                        except StopIteration:
                            pend = None
                            break
            while pend is not None:
                try:
                    next(pend)
                except StopIteration:
                    pend = None
            S.dma("sp", slx, xt[:], K.xT_d[:, :, t * TW2:(t + 1) * TW2], reads=[K.b_xT[2 * t], K.b_xT[2 * t + 1]], writes=[b_xt])
            for half in range(2):
                banks = pacc if half == 0 else pg
                bbanks = b_pacc if half == 0 else b_pg
                for mm in range(4):
                    m = half * 4 + mm
                    S.group("pe", [lambda e, j=j, m=m, mm=mm, banks=banks: e.matmul(
                        banks[mm][:], lhsT=wdn[:, j, m * 128:(m + 1) * 128], rhs=act[:, j, :],
                        start=(j == 0), stop=(j == NJ - 1)) for j in range(NJ)], reads=[b_wdn] + b_act, writes=[bbanks[mm]])
                    S.op("dve", lambda e, m=m, mm=mm, banks=banks: e.scalar_tensor_tensor(
                        out=xt[:, m, :], in0=banks[mm][:], scalar=V["g2"][:, l, m:m + 1], in1=xt[:, m, :],
                        op0=ALU.mult, op1=ALU.add), reads=[bbanks[mm], b_xt, K.b_vec], writes=[b_xt])
            pend = epilogue(t)
        while pend is not None:
            try:
                next(pend)
            except StopIteration:
                pend = None
        S.barrier()


MIXERS = {}

def phase_gdn(K, l, j):
    nc, S = K.nc, K.S
    NB = T // 128
    wv = K.inp["gdn_w_in"][j].rearrange("(kc p) n -> p kc n", p=128)
    with contextlib.ExitStack() as gph:
        gsb = lambda n, shp, dt: _sb(K, gph, n, shp, dt)
        K.cst2 = gsb("cst2_sb", [128, 6, 2, 128], BF16)
        S.dma("pool", S.new_slot(sw=True), K.cst2[:], K.inp["cst2"], writes=[K.b_const])
        beta = gsb("g_beta", [128, NB, 8], F32)
        gc = gsb("g_gc", [128, NB, 8], F32)
        egc = gsb("g_egc", [128, NB, 8], F32)
        etail = gsb("g_etail", [128, NB, 8], F32)
        glast = gsb("g_glast", [128, NB, 8], F32)
        bgt = gsb("g_bg", [128, NB, 8], F32)
        cw = gsb("g_cw", [128, 2, 24, 4], F32)
        hc = gsb("g_hc", [128, 2, 2, 8], F32)
        ng = gsb("g_ng", [128, 2], F32)
        b_g = Buf()
        b_cst = Buf()
        scst = S.new_slot()
        S.dma("sp", scst, cw[:], K.inp["gdn_conv"], writes=[b_cst])
        S.dma("sp", scst, hc[:], K.inp["gdn_hc"], writes=[b_cst])
        S.dma("sp", scst, ng[:], K.inp["gdn_ng"], writes=[b_cst])
        with contextlib.ExitStack() as ph:
            sb = lambda n, shp, dt: _sb(K, ph, n, shp, dt)
            hT = sb("g_hT", [128, KC, T], BF16)
            b_hT = [Buf() for _ in range(KC)]
            for kc in range(KC):
                S.dma("sp", S.new_slot(), hT[:, kc, :], K.hT_d[:, kc, :], reads=K.b_hT, writes=[b_hT[kc]])
            pf = [_ps(K, ph, "g_pf%d" % i, [128, 512], F32) for i in range(4)]
            b_pf = [PB() for _ in range(4)]
            wba = sb("g_wba", [128, KC, 16], BF16)
            b_wba = Buf()
            S.dma("pool", S.new_slot(sw=True), wba[:], wv[:, :, 4096:4112], writes=[b_wba])
            ba = sb("g_ba", [128, NB, 16], F32)
            gg = sb("g_g", [128, NB, 8], F32)
            negA = sb("g_negA", [128, 8], F32)
            for blk in range(NB):
                S.group("pe", [lambda e, kc=kc, blk=blk: e.matmul(
                    pf[0][:, blk * 16:(blk + 1) * 16], lhsT=hT[:, kc, blk * 128:(blk + 1) * 128], rhs=wba[:, kc, :],
                    start=(kc == 0), stop=(kc == KC - 1)) for kc in range(KC)], reads=b_hT + [b_wba], writes=[b_pf[0]])
            S.op("act", lambda e: e.copy(out=ba[:].rearrange("p a b -> p (a b)"), in_=pf[0][:]), reads=[b_pf[0]], writes=[b_g])
            S.op("act", lambda e: e.activation(out=beta[:], in_=ba[:, :, 0:8], func=AF.Sigmoid), reads=[b_g], writes=[b_g])
            S.op("act", lambda e: e.activation(out=negA[:], in_=hc[:, j, 0, :], func=AF.Exp), reads=[b_cst], writes=[b_g])
            S.op("dve", lambda e: e.tensor_scalar(out=negA[:], in0=negA[:], scalar1=-1.0, scalar2=None, op0=ALU.mult),
                 reads=[b_g], writes=[b_g])
            S.op("dve", lambda e: e.tensor_tensor(out=gg[:], in0=ba[:, :, 8:16],
                                                 in1=hc[:, j, 1, :].unsqueeze(1).to_broadcast([128, NB, 8]), op=ALU.add),
                 reads=[b_g, b_cst], writes=[b_g])
            S.op("act", lambda e: e.activation(out=gg[:], in_=gg[:], func=AF.Exp), reads=[b_g], writes=[b_g])
            S.op("act", lambda e: e.activation(out=gg[:], in_=gg[:], func=AF.Ln, bias=K.eps1024[:, 2:3], scale=1.0),
                 reads=[b_g, K.b_const], writes=[b_g])
            S.op("dve", lambda e: e.tensor_tensor(out=gg[:], in0=gg[:], in1=negA[:].unsqueeze(1).to_broadcast([128, NB, 8]),
                                                 op=ALU.mult), reads=[b_g], writes=[b_g])
            ggf = gg[:].rearrange("p a b -> p (a b)")
            S.op("pe", lambda e: e.matmul(pf[1][:, 0:256], lhsT=K.triU_f, rhs=ggf, start=True, stop=True),
                 reads=[b_g, K.b_const], writes=[b_pf[1]])
            S.op("pe", lambda e: e.matmul(pf[2][:, 0:256], lhsT=K.ones_f, rhs=ggf, start=True, stop=True),
                 reads=[b_g, K.b_const], writes=[b_pf[2]])
            fl = lambda t: t[:].rearrange("p a b -> p (a b)")
            S.op("act", lambda e: e.copy(out=fl(gc), in_=pf[1][:, 0:256]), reads=[b_pf[1]], writes=[b_g])
            S.op("act", lambda e: e.activation(out=fl(egc), in_=pf[1][:, 0:256], func=AF.Exp), reads=[b_pf[1]], writes=[b_g])
            S.op("act", lambda e: e.activation(out=fl(glast), in_=pf[2][:, 0:256], func=AF.Exp), reads=[b_pf[2]], writes=[b_g])
            S.op("dve", lambda e: e.tensor_tensor(out=fl(etail), in0=pf[2][:, 0:256], in1=fl(gc), op=ALU.subtract),
                 reads=[b_pf[2], b_g], writes=[b_g])
            S.op("act", lambda e: e.activation(out=fl(etail), in_=fl(etail), func=AF.Exp), reads=[b_g], writes=[b_g])
            S.op("dve", lambda e: e.tensor_tensor(out=fl(bgt), in0=fl(beta), in1=fl(egc), op=ALU.mult), reads=[b_g], writes=[b_g])
            NW = 4
            w1 = [sb("g_w1%d" % i, [128, KC, 128], BF16) for i in range(NW)]
            b_w1 = [Buf() for _ in range(NW)]
            s_w1 = [S.new_slot(sw=True) for _ in range(NW)]
            stg = [sb("g_stg%d" % i, [128, 3 + T], F32) for i in range(2)]
            b_stg = [[Buf() for _ in range(9)] for _ in range(2)]
            for i in range(2):
                S.op("pool", lambda e, i=i: e.memset(stg[i][:, 0:3], 0.0), writes=[b_stg[i][0]])

            def load_w1(ci):
                S.dma("pool", s_w1[ci % NW], w1[ci % NW][:], wv[:, :, ci * 128:(ci + 1) * 128], writes=[b_w1[ci % NW]])

            for ci in range(NW - 1):
                load_w1(ci)
            NLA = 4
            LAa = [{"cacc": sb("g_caccL%d" % i, [128, 512], F32), "b_cacc": Buf(),
                    "fo": sb("g_foL%d" % i, [128, 512], BF16), "b_fo": Buf(), "s_fo": S.new_slot()} for i in range(NLA)]

            def ajob(li, ci, tt):
                d = LAa[li]
                if tt == 0 and ci + NW - 1 < 32:
                    load_w1(ci + NW - 1)
                W = w1[ci % NW]
                bW = b_w1[ci % NW]
                sg_ = stg[ci % 2]
                bsg = b_stg[ci % 2]
                pa, bpa = pf[li], b_pf[li]
                S.group("pe", [lambda e, kc=kc: e.matmul(
                    pa[:], lhsT=W[:, kc, :], rhs=hT[:, kc, tt * 512:(tt + 1) * 512],
                    start=(kc == 0), stop=(kc == KC - 1)) for kc in range(KC)], reads=b_hT + [bW], writes=[bpa])
                yield
                o = tt * 512
                f_, bf_ = d["fo"], d["b_fo"]
                if ci >= 24:
                    S.op("act", lambda e: e.activation(out=f_[:], in_=pa[:], func=AF.Silu), reads=[bpa], writes=[bf_])
                    yield
                else:
                    S.op("act", lambda e: e.copy(out=sg_[:, 3 + o:3 + o + 512], in_=pa[:]), reads=[bpa], writes=[bsg[1 + tt]])
                    yield
                    ca, bca = d["cacc"], d["b_cacc"]
                    rd = [bsg[tt], bsg[1 + tt], b_cst]
                    S.op("dve", lambda e: e.tensor_scalar(
                        out=ca[:], in0=sg_[:, 3 + o:3 + o + 512], scalar1=cw[:, j, ci, 3:4], scalar2=None, op0=ALU.mult),
                        reads=rd, writes=[bca])
                    yield
                    for k in (2, 1, 0):
                        S.op("dve", lambda e, k=k: e.scalar_tensor_tensor(
                            out=ca[:], in0=sg_[:, k + o:k + o + 512], scalar=cw[:, j, ci, k:k + 1], in1=ca[:],
                            op0=ALU.mult, op1=ALU.add), reads=rd + [bca], writes=[bca])
                        yield
                    S.op("act", lambda e: e.activation(out=f_[:], in_=ca[:], func=AF.Silu), reads=[bca], writes=[bf_])
                    yield
                S.dma("sp", d["s_fo"], K.F_d[ci, :, o:o + 512], f_[:], reads=[bf_], writes=[])
                yield

            run_rolling([(lambda li, ci=ci, tt=tt: ajob(li, ci, tt)) for ci in range(32) for tt in range(8)], NLA, stagger=2)
            S.barrier()
        with contextlib.ExitStack() as ph:
            sb = lambda n, shp, dt: _sb(K, ph, n, shp, dt)
            NL = 8
            junk_sh = sb("g_junk_sh", [128, 128], F32)
            pl = [_ps(K, ph, "g_pl%d" % i, [128, 512], F32) for i in range(NL)]
            L = []
            for h in range(NL):
                d = {}
                d["pb"] = PB()
                d["ps"] = pl[h]
                d["F"] = [[sb("g_F%d_%d_%d" % (h, c, i), [128, 512], BF16) for i in range(2)] for c in range(4)]
                d["bF"] = [[Buf() for i in range(2)] for c in range(4)]
                d["sF"] = [[S.new_slot() for i in range(2)] for c in range(4)]
                d["oT"] = [sb("g_oT%d_%d" % (h, i), [128, 512], BF16) for i in range(1)] * 2
                d["boT"] = [Buf()] * 2
                d["soT"] = [S.new_slot()] * 2
                d["junk"] = junk_sh
                d["b_junk"] = None
                for nm, shp, dt in (("tm", [128, 3, 128], BF16), ("sc", [128, 16], F32),
                                    ("var", [128, 7, 128], BF16), ("fmT", [128, 4, 128], BF16), ("diagG", [128, 128], F32),
                                    ("dTs", [128, 128], F32), ("dTf", [128, 128], F32),
                                    ("attnT", [128, 128], BF16), ("N0", [128, 2, 128], BF16), ("N1", [128, 2, 128], BF16),
                                    ("N2", [128, 2, 128], BF16), ("X0", [128, 2, 128], BF16), ("X1", [128, 2, 128], BF16),
                                    ("MA", [128, 5, 2, 128], BF16), ("PP", [128, 2, 128], BF16), ("u", [128, 128], F32),
                                    ("wT", [128, 128], BF16), ("vnew", [128, 128], BF16), ("S", [128, 128], F32),
                                    ("Sbf", [128, 128], BF16), ("os", [128, 128], BF16)):
                    d[nm] = sb("g_%s%d" % (nm, h), shp, dt)
                    d["b_" + nm] = Buf()
                L.append(d)

            def load_F(h, tt):
                d = L[h]
                i = tt % 2
                for c in range(4):
                    S.dma("sp", d["sF"][c][i], d["F"][c][i][:], K.F_d[c * 8 + h, :, tt * 512:(tt + 1) * 512],
                          reads=[K.b_F], writes=[d["bF"][c][i]])

            pv2 = lambda t: t[:].rearrange("p a b -> p (a b)")

            def chunk(h, n):
                d = L[h]
                ps, pb = d["ps"], d["pb"]
                tt = n // 4
                fi = tt % 2
                o4 = (n % 4) * 128
                Fq, Fk, Fv, Fz = [d["F"][c][fi] for c in range(4)]
                bFq, bFk, bFv, bFz = [d["bF"][c][fi] for c in range(4)]
                gcol = lambda t: t[:, n, h:h + 1]
                tm, sc, var, fmT, junk = d["tm"], d["sc"], d["var"], d["fmT"], d["junk"]
                b_tm, b_sc, b_var, b_fmT, b_junk = d["b_tm"], d["b_sc"], d["b_var"], d["b_fmT"], d["b_junk"]
                S.group("pe", [lambda e, c=c, Fc=Fc: e.matmul(ps[:, c * 128:(c + 1) * 128], lhsT=Fc[:, o4:o4 + 128], rhs=K.ident_bf,
                                                            start=True, stop=True) for c, Fc in enumerate((Fq, Fk, Fv))],
                        reads=[bFq, bFk, bFv, K.b_const], writes=[pb])
                yield
                S.op("act", lambda e: e.copy(out=tm[:].rearrange("p a b -> p (a b)"), in_=ps[:, 0:384]), reads=[pb], writes=[b_tm])
                yield
                S.op("act", lambda e: e.activation(out=junk[:], in_=tm[:, 0, :], func=AF.Square, scale=128.0 ** 0.5, accum_out=sc[:, 0:1]),
                     reads=[b_tm], writes=[b_sc])
                S.op("act", lambda e: e.activation(out=junk[:], in_=tm[:, 1, :], func=AF.Square, accum_out=sc[:, 1:2]),
                     reads=[b_tm], writes=[b_sc])
                yield
                S.op("act", lambda e: e.activation(out=sc[:, 2:4], in_=sc[:, 0:2], func=AF.Ln, bias=K.eps1024[:, 1:2], scale=1.0),
                     reads=[b_sc, K.b_const], writes=[b_sc])
                S.op("act", lambda e: e.activation(out=sc[:, 2:4], in_=sc[:, 2:4], func=AF.Exp, scale=-0.5),
                     reads=[b_sc], writes=[b_sc])
                yield
                rq, rk = sc[:, 2:3], sc[:, 3:4]
                zb = K.eps1024[:, 3:4]
                S.op("act", lambda e: e.activation(out=var[:, 0, :], in_=tm[:, 0, :], func=AF.Identity, bias=zb, scale=rq),
                     reads=[b_tm, b_sc, K.b_const], writes=[b_var])
                S.op("dve", lambda e: e.tensor_scalar(out=var[:, 1, :], in0=tm[:, 0, :], scalar1=rq, scalar2=gcol(egc), op0=ALU.mult, op1=ALU.mult),
                     reads=[b_tm, b_sc, b_g], writes=[b_var])
                yield
                S.op("act", lambda e: e.activation(out=var[:, 2, :], in_=tm[:, 1, :], func=AF.Identity, bias=zb, scale=rk),
                     reads=[b_tm, b_sc, K.b_const], writes=[b_var])
                S.op("dve", lambda e: e.tensor_scalar(out=var[:, 3, :], in0=tm[:, 1, :], scalar1=rk, scalar2=gcol(beta), op0=ALU.mult, op1=ALU.mult),
                     reads=[b_tm, b_sc, b_g], writes=[b_var])
                yield
                S.op("act", lambda e: e.activation(out=var[:, 6, :], in_=tm[:, 2, :], func=AF.Identity, bias=zb, scale=gcol(beta)),
                     reads=[b_tm, b_g, K.b_const], writes=[b_var])
                S.op("dve", lambda e: e.tensor_scalar(out=var[:, 4, :], in0=tm[:, 1, :], scalar1=rk, scalar2=gcol(bgt), op0=ALU.mult, op1=ALU.mult),
                     reads=[b_tm, b_sc, b_g], writes=[b_var])
                yield
                S.op("dve", lambda e: e.tensor_scalar(out=var[:, 5, :], in0=tm[:, 1, :], scalar1=rk, scalar2=gcol(etail), op0=ALU.mult, op1=ALU.mult),
                     reads=[b_tm, b_sc, b_g], writes=[b_var])
                yield
                S.group("pe", [lambda e, vi=vi: e.matmul(ps[:, vi * 128:(vi + 1) * 128], lhsT=var[:, vi, :], rhs=K.ident_bf,
                                                        start=True, stop=True) for vi in range(4)],
                        reads=[b_var, K.b_const], writes=[pb])
                yield
                S.op("dve", lambda e: e.tensor_copy(out=fmT[:].rearrange("p a b -> p (a b)"), in_=ps[:, 0:512]),
                     reads=[pb], writes=[b_fmT])
                qhT, qdT, khT, kbT = fmT[:, 0, :], fmT[:, 1, :], fmT[:, 2, :], fmT[:, 3, :]
                S.op("act", lambda e: e.activation(out=d["diagG"][:], in_=K.ident_f, func=AF.Identity, bias=K.eps1024[:, 3:4], scale=gcol(gc)),
                     reads=[K.b_const, b_g], writes=[d["b_diagG"]])
                yield
                S.op("pe", lambda e: e.matmul(ps[:, 0:128], lhsT=K.ones_f, rhs=d["diagG"][:], start=True, stop=True),
                     reads=[d["b_diagG"], K.b_const], writes=[pb])
                yield
                S.op("dve", lambda e: e.scalar_tensor_tensor(out=d["dTs"][:], in0=ps[:, 0:128], scalar=gcol(gc), in1=K.cst_f[:, 5, :],
                                                            op0=ALU.subtract, op1=ALU.min), reads=[pb, b_g, K.b_const], writes=[d["b_dTs"]])
                yield
                S.op("act", lambda e: e.activation(out=d["dTs"][:], in_=d["dTs"][:], func=AF.Exp), reads=[d["b_dTs"]], writes=[d["b_dTs"]])
                S.group("pe", [lambda e: e.matmul(ps[:, 128:256], lhsT=khT, rhs=kbT, start=True, stop=True),
                               lambda e: e.matmul(ps[:, 256:384], lhsT=khT, rhs=qhT, start=True, stop=True)],
                        reads=[b_fmT], writes=[pb])
                yield
                S.op("pool", lambda e: e.tensor_tensor(out=d["dTf"][:], in0=d["dTs"][:], in1=K.ident_f, op=ALU.add),
                     reads=[d["b_dTs"], K.b_const], writes=[d["b_dTf"]])
                NP = d["N0"]
                S.op("dve", lambda e: e.tensor_tensor(out=NP[:, 1, :], in0=ps[:, 128:256], in1=d["dTs"][:], op=ALU.mult),
                     reads=[pb, d["b_dTs"]], writes=[d["b_N0"]])
                yield
                S.op("dve", lambda e: e.tensor_tensor(out=d["attnT"][:], in0=ps[:, 256:384], in1=d["dTf"][:], op=ALU.mult),
                     reads=[pb, d["b_dTf"]], writes=[d["b_attnT"]])
                yield
                S.op("pe", lambda e: e.matmul(ps[:, 384:512], lhsT=NP[:, 1, :], rhs=K.ident_bf, start=True, stop=True),
                     reads=[d["b_N0"], K.b_const], writes=[pb])
                yield
                S.op("act", lambda e: e.copy(out=NP[:, 0, :], in_=ps[:, 384:512]), reads=[pb], writes=[d["b_N0"]])
                yield
                Nn = [d["N0"], d["N1"], d["N2"]]
                b_N = [d["b_N0"], d["b_N1"], d["b_N2"]]
                XX = [d["X0"], d["X1"]]
                b_XX = [d["b_X0"], d["b_X1"]]
                PP = d["PP"]
                MA = d["MA"]
                S.op("pool", lambda e: e.tensor_tensor(out=MA[:], in0=NP[:].unsqueeze(1).to_broadcast([128, 5, 2, 128]),
                                                      in1=K.cst2[:, 0:5, :, :], op=ALU.mult),
                     reads=[b_N[0], K.b_const], writes=[d["b_MA"]])
                Nn[1] = MA[:, 0, :, :]
                b_N[1] = d["b_MA"]
                yield
                S.op("pool", lambda e: e.tensor_tensor(out=XX[0][:], in0=K.cst2[:, 5, :, :], in1=MA[:, 0, :, :], op=ALU.subtract),
                     reads=[K.b_const, b_N[1]], writes=[b_XX[0]])
                yield
                cx = 0
                for lv, (a, ba_, nx, bnx) in enumerate(((MA[:, 0, :, :], d["b_MA"], d["N2"], d["b_N2"]),
                                                        (d["N2"], d["b_N2"], d["N1"], d["b_N1"]))):
                    S.group("pe", [lambda e, a=a: e.matmul(ps[:, 0:128], lhsT=a[:, 1, :], rhs=a[:, 0, :], start=True, stop=True),
                                   lambda e, a=a: e.matmul(ps[:, 128:256], lhsT=a[:, 0, :], rhs=a[:, 1, :], start=True, stop=True)],
                            reads=[ba_], writes=[pb])
                    yield
                    S.op("act", lambda e, nx=nx: e.copy(out=pv2(nx), in_=ps[:, 0:256]), reads=[pb], writes=[bnx])
                    yield
                    xs, xd = XX[cx], XX[1 - cx]
                    S.group("pe", [lambda e, nx=nx, xs=xs: e.matmul(ps[:, 256:384], lhsT=nx[:, 1, :], rhs=xs[:, 0, :], start=True, stop=True),
                                   lambda e, nx=nx, xs=xs: e.matmul(ps[:, 384:512], lhsT=nx[:, 0, :], rhs=xs[:, 1, :], start=True, stop=True)],
                            reads=[bnx, b_XX[cx]], writes=[pb])
                    yield
                    S.op("dve", lambda e, xs=xs, xd=xd: e.tensor_tensor(out=pv2(xd), in0=ps[:, 256:512], in1=pv2(xs), op=ALU.add),
                         reads=[pb, b_XX[cx]], writes=[b_XX[1 - cx]])
                    yield
                    cx = 1 - cx
                for li in range(4):
                    xs, xd = XX[cx], XX[1 - cx]
                    S.group("pe", [lambda e, xs=xs, li=li: e.matmul(ps[:, 0:128], lhsT=MA[:, 1 + li, 0, :], rhs=xs[:, 1, :], start=True, stop=True),
                                   lambda e, xs=xs, li=li: e.matmul(ps[:, 128:256], lhsT=MA[:, 1 + li, 1, :], rhs=xs[:, 0, :], start=True, stop=True)],
                            reads=[d["b_MA"], b_XX[cx]], writes=[pb])
                    yield
                    S.op("act", lambda e: e.copy(out=pv2(PP), in_=ps[:, 0:256]), reads=[pb], writes=[d["b_PP"]])
                    yield
                    S.group("pe", [lambda e, xs=xs: e.matmul(ps[:, 256:384], lhsT=xs[:, 1, :], rhs=PP[:, 1, :], start=True, stop=True),
                                   lambda e, xs=xs: e.matmul(ps[:, 384:512], lhsT=xs[:, 0, :], rhs=PP[:, 0, :], start=True, stop=True)],
                            reads=[d["b_PP"], b_XX[cx]], writes=[pb])
                    yield
                    S.op("dve", lambda e, xs=xs, xd=xd: e.tensor_tensor(out=pv2(xd), in0=pv2(xs), in1=ps[:, 256:512], op=ALU.subtract),
                         reads=[pb, b_XX[cx]], writes=[b_XX[1 - cx]])
                    yield
                    cx = 1 - cx
                XTf = XX[cx][:, 1, :]
                bXTf = b_XX[cx]
                S.group("pe", [lambda e: e.matmul(ps[:, 0:128], lhsT=XTf, rhs=var[:, 6, :], start=True, stop=True),
                               lambda e: e.matmul(ps[:, 128:256], lhsT=var[:, 4, :], rhs=XTf, start=True, stop=True)],
                        reads=[bXTf, b_var], writes=[pb])
                yield
                wT, vnew, Sst, Sbf, u_sb = d["wT"], d["vnew"], d["S"], d["Sbf"], d["u"]
                S.op("act", lambda e: e.copy(out=wT[:], in_=ps[:, 128:256]), reads=[pb], writes=[d["b_wT"]])
                if n == 0:
                    S.op("dve", lambda e: e.tensor_copy(out=vnew[:], in_=ps[:, 0:128]), reads=[pb], writes=[d["b_vnew"]])
                    yield
                    S.group("pe", [lambda e: e.matmul(ps[:, 128:256], lhsT=d["attnT"][:], rhs=vnew[:], start=True, stop=True),
                                   lambda e: e.matmul(ps[:, 256:384], lhsT=var[:, 5, :], rhs=vnew[:], start=True, stop=True)],
                            reads=[d["b_attnT"], d["b_vnew"], b_var], writes=[pb])
                    yield
                    S.op("dve", lambda e: e.tensor_copy(out=Sst[:], in_=ps[:, 256:384]), reads=[pb], writes=[d["b_S"]])
                else:
                    S.op("dve", lambda e: e.tensor_copy(out=u_sb[:], in_=ps[:, 0:128]), reads=[pb], writes=[d["b_u"]])
                    yield
                    S.op("pe", lambda e: e.matmul(ps[:, 0:128], lhsT=wT[:], rhs=Sbf[:], start=True, stop=True),
                         reads=[d["b_wT"], d["b_S"]], writes=[pb])
                    yield
                    S.op("dve", lambda e: e.tensor_tensor(out=vnew[:], in0=u_sb[:], in1=ps[:, 0:128], op=ALU.subtract),
                         reads=[d["b_u"], pb], writes=[d["b_vnew"]])
                    yield
                    S.group("pe", [lambda e: e.matmul(ps[:, 128:256], lhsT=qdT, rhs=Sbf[:], start=True, stop=False),
                                   lambda e: e.matmul(ps[:, 128:256], lhsT=d["attnT"][:], rhs=vnew[:], start=False, stop=True),
                                   lambda e: e.matmul(ps[:, 256:384], lhsT=var[:, 5, :], rhs=vnew[:], start=True, stop=True)],
                            reads=[b_fmT, d["b_S"], d["b_attnT"], d["b_vnew"], b_var], writes=[pb])
                    yield
                    S.op("dve", lambda e: e.scalar_tensor_tensor(out=Sst[:], in0=Sst[:], scalar=gcol(glast), in1=ps[:, 256:384],
                                                                op0=ALU.mult, op1=ALU.add), reads=[d["b_S"], b_g, pb], writes=[d["b_S"]])
                yield
                if n < NB - 1:
                    S.op("pool", lambda e: e.tensor_copy(out=Sbf[:], in_=Sst[:]), reads=[d["b_S"]], writes=[d["b_S"]])
                S.op("act", lambda e: e.activation(out=junk[:], in_=ps[:, 128:256], func=AF.Square, accum_out=sc[:, 9:10]),
                     reads=[pb], writes=[b_sc])
                yield
                S.op("act", lambda e: e.activation(out=sc[:, 10:11], in_=sc[:, 9:10], func=AF.Ln, bias=K.eps1024[:, 1:2], scale=1.0 / 128.0),
                     reads=[b_sc, K.b_const], writes=[b_sc])
                S.op("act", lambda e: e.activation(out=sc[:, 10:11], in_=sc[:, 10:11], func=AF.Exp, scale=-0.5),
                     reads=[b_sc], writes=[b_sc])
                yield
                S.op("dve", lambda e: e.tensor_scalar(out=d["os"][:], in0=ps[:, 128:256], scalar1=sc[:, 10:11], scalar2=None, op0=ALU.mult),
                     reads=[pb, b_sc], writes=[d["b_os"]])
                yield
                S.op("pe", lambda e: e.matmul(ps[:, 384:512], lhsT=d["os"][:], rhs=K.ident_bf, start=True, stop=True),
                     reads=[d["b_os"], K.b_const], writes=[pb])
                yield
                oi = tt % 2
                S.op("dve", lambda e: e.scalar_tensor_tensor(out=d["oT"][oi][:, o4:o4 + 128], in0=ps[:, 384:512], scalar=ng[:, j:j + 1],
                                                            in1=Fz[:, o4:o4 + 128], op0=ALU.mult, op1=ALU.mult),
                     reads=[pb, b_cst, bFz], writes=[d["boT"][oi]])
                if n % 4 == 3:
                    S.dma("sp", d["soT"][oi], K.oT_d[:, h, tt * 512:(tt + 1) * 512], d["oT"][oi][:], reads=[d["boT"][oi]],
                          writes=[K.b_oT[2 * tt], K.b_oT[2 * tt + 1]])
                yield

            def head_chain(h):
                load_F(h, 0)
                for n in range(NB):
                    if n % 4 == 0 and n // 4 + 1 < 8:
                        load_F(h, n // 4 + 1)
                    yield from chunk(h, n)

            run_rolling([(lambda li: head_chain(li)) for _ in range(NL)], NL, stagger=5)
            S.barrier()


MIXERS["gdn"] = phase_gdn

def run_lanes(gens):
    live = list(gens)
    while live:
        nxt = []
        for g_ in live:
            try:
                next(g_)
                nxt.append(g_)
            except StopIteration:
                pass
        live = nxt


def run_rolling(jobs, nl, stagger=1):
    jobs = list(jobs)
    active = {}
    nxt = 0
    step = 0
    while nxt < len(jobs) or active:
        for li in range(nl):
            if li not in active and nxt < len(jobs) and step >= li * stagger:
                active[li] = jobs[nxt](li)
                nxt += 1
        for li in list(active.keys()):
            try:
                next(active[li])
            except StopIteration:
                del active[li]
        step += 1


def phase_dsw(K, l, j):
    nc, S = K.nc, K.S
    NB = 32
    SC = 128.0 ** -0.5
    NL = 7
    with contextlib.ExitStack() as ph:
        sb = lambda n, shp, dt: _sb(K, ph, n, shp, dt)
        hT = sb("d_hT", [128, KC, T], BF16)
        b_hT = [Buf() for _ in range(KC)]
        for kc in range(KC):
            S.dma("sp", S.new_slot(), hT[:, kc, :], K.hT_d[:, kc, :], reads=K.b_hT, writes=[b_hT[kc]])
        gain = sb("d_gain", [128, 6, 128], F32)
        rope = sb("d_rope", [128, 2, 32, 16], F32)
        b_rope = Buf()
        s_rope = S.new_slot()
        mask2 = sb("d_mask2", [128, 2, 256], BF16)
        b_cst = Buf()
        scst = S.new_slot()
        S.dma("sp", scst, gain[:], K.inp["dsw_gain"][:, j, :, :], writes=[b_cst])
        for hd in range(2):
            S.op("pool", lambda e, hd=hd: e.tensor_copy(out=mask2[:, hd, 0:128], in_=K.cst_f[:, 2, :]), reads=[K.b_const], writes=[b_cst])
            S.op("pool", lambda e, hd=hd: e.tensor_copy(out=mask2[:, hd, 128:256], in_=K.cst_f[:, 4, :]), reads=[K.b_const], writes=[b_cst])
        pA = [_ps(K, ph, "d_pA%d" % i, [128, 512], F32) for i in range(NL)]
        b_pA = [PB() for _ in range(NL)]
        pB = [pA[i][:].rearrange("p (a c) -> p a c", c=256) for i in range(NL)]
        b_pB = b_pA
        wsl = [sb("d_wsl%d" % i, [128, KC, 512], BF16) for i in range(2)]
        b_wsl = [Buf(), Buf()]
        s_wsl = [S.new_slot(sw=True), S.new_slot(sw=True)]
        qT = sb("d_qT", [128, 4, NB, 128], BF16)
        kT = sb("d_kT", [128, 4, NB, 128], BF16)
        b_qT = [Buf() for _ in range(NB)]
        b_kT = [Buf() for _ in range(NB)]
        Va = sb("d_Va", [128, NB, 4, 130], BF16)
        b_Va = [Buf() for _ in range(NB)]
        S.op("pool", lambda e: e.memset(Va[:].rearrange("p a b c -> p (a b c)"), 1.0), writes=b_Va)
        junk = sb("d_junk", [128, 128], F32)
        LA = []
        for i in range(NL):
            ypt = sb("d_ypt%d" % i, [128, 512], BF16)
            rst = sb("d_rst%d" % i, [128, 264], F32)
            b_ypt, b_rst = Buf(), Buf()
            d = {"ss": sb("d_ss%d" % i, [128, 8], F32), "b_ss": Buf(),
                 "y": ypt[:].rearrange("p (a c) -> p a c", c=128), "b_y": b_ypt,
                 "rt": rst[:, 0:256].rearrange("p (t a c) -> p t a c", t=2, a=4), "b_rt": b_rst,
                 "pT": ypt[:].rearrange("p (a c) -> p a c", c=256), "b_pT": b_ypt,
                 "st": rst[:].rearrange("p (a c) -> p a c", c=132), "b_st": b_rst, "s_st": S.new_slot()}
            LA.append(d)
        wv = K.inp["dsw_w_in"][j].rearrange("(kc p) n -> p kc n", p=128)
        slabs = [(g, hh, t3) for g in range(3) for hh in range(2) for t3 in range(3)]

        def load_slab(si):
            g, hh, t3 = slabs[si]
            col0 = ((g * 3 + t3) * 8 + hh * 4) * 128
            S.dma("pool", s_wsl[si % 2], wsl[si % 2][:], wv[:, :, col0:col0 + 512], writes=[b_wsl[si % 2]])

        def proj_blk(li, si, b):
            g, hh, t3 = slabs[si]
            dl = LA[li]
            dd = DSW_DIL[g]
            nsb = NB // dd
            r, sbk = b // nsb, b % nsb
            start = r + dd * sbk * 128
            W = wsl[si % 2]
            pa, bpa = pA[li], b_pA[li]
            S.group("pe", [lambda e, kc=kc: e.matmul(pa[:], lhsT=hT[:, kc, start:start + 127 * dd + 1:dd], rhs=W[:, kc, :],
                                                    start=(kc == 0), stop=(kc == KC - 1)) for kc in range(KC)],
                    reads=b_hT + [b_wsl[si % 2]], writes=[bpa])
            yield
            if t3 == 2:
                S.op("act", lambda e: e.copy(out=Va[:, b, :, 0:128], in_=pa[:].rearrange("p (a c) -> p a c", c=128)),
                     reads=[bpa], writes=[b_Va[b]])
                return
            ss, y, rt = dl["ss"], dl["y"], dl["rt"]
            for hd in range(4):
                S.op("act", lambda e, hd=hd: e.activation(out=junk[:], in_=pa[:, hd * 128:(hd + 1) * 128], func=AF.Square,
                                                         accum_out=ss[:, hd:hd + 1]), reads=[bpa], writes=[dl["b_ss"]])
                if hd == 1:
                    yield
            yield
            S.op("act", lambda e: e.activation(out=ss[:, 4:8], in_=ss[:, 0:4], func=AF.Ln, bias=K.eps1024[:, 1:2], scale=1.0 / 128.0),
                 reads=[dl["b_ss"], K.b_const], writes=[dl["b_ss"]])
            S.op("act", lambda e: e.activation(out=ss[:, 4:8], in_=ss[:, 4:8], func=AF.Exp, scale=-0.5), reads=[dl["b_ss"]], writes=[dl["b_ss"]])
            yield
            for hd in range(4):
                S.op("dve", lambda e, hd=hd: e.scalar_tensor_tensor(
                    out=y[:, hd, :], in0=pa[:, hd * 128:(hd + 1) * 128], scalar=ss[:, 4 + hd:5 + hd], in1=gain[:, g * 2 + t3, :],
                    op0=ALU.mult, op1=ALU.mult), reads=[bpa, dl["b_ss"], b_cst], writes=[dl["b_y"]])
                if hd == 1:
                    yield
            yield
            cs2 = rope[:, 0, b, :].unsqueeze(1).unsqueeze(1).to_broadcast([128, 4, 2, 16])
            sn2 = rope[:, 1, b, :].unsqueeze(1).unsqueeze(1).to_broadcast([128, 4, 2, 16])
            y32 = y[:, :, 0:32].rearrange("p a (t c) -> p a t c", t=2)
            S.op("dve", lambda e: e.tensor_tensor(out=rt[:, 0, :, :].rearrange("p a (t c) -> p a t c", t=2), in0=y32, in1=cs2, op=ALU.mult),
                 reads=[dl["b_y"], b_rope], writes=[dl["b_rt"]])
            S.op("dve", lambda e: e.tensor_tensor(out=rt[:, 1, :, :].rearrange("p a (t c) -> p a t c", t=2), in0=y32, in1=sn2, op=ALU.mult),
                 reads=[dl["b_y"], b_rope], writes=[dl["b_rt"]])
            yield
            S.op("dve", lambda e: e.tensor_tensor(out=y[:, :, 0:16], in0=rt[:, 0, :, 0:16], in1=rt[:, 1, :, 16:32], op=ALU.subtract),
                 reads=[dl["b_rt"]], writes=[dl["b_y"]])
            S.op("dve", lambda e: e.tensor_tensor(out=y[:, :, 16:32], in0=rt[:, 0, :, 16:32], in1=rt[:, 1, :, 0:16], op=ALU.add),
                 reads=[dl["b_rt"]], writes=[dl["b_y"]])
            yield
            S.group("pe", [lambda e, hd=hd: e.matmul(pa[:, hd * 128:(hd + 1) * 128], lhsT=y[:, hd, :], rhs=K.ident_bf, start=True, stop=True)
                           for hd in range(4)], reads=[dl["b_y"], K.b_const], writes=[bpa])
            yield
            dst = qT if t3 == 0 else kT
            bd = b_qT[b] if t3 == 0 else b_kT[b]
            S.op("act", lambda e: e.copy(out=dst[:, :, b, :], in_=pa[:].rearrange("p (a c) -> p a c", c=128)), reads=[bpa], writes=[bd])
            yield

        def att_unit(li, g, hh, b, pr):
            dl = LA[li]
            dd = DSW_DIL[g]
            nsb = NB // dd
            r, sbk = b // nsb, b % nsb
            has_prev = sbk > 0
            ps_, bps = pB[li], b_pB[li]
            fns = []
            for h2 in range(2):
                hd = pr * 2 + h2
                fns.append(lambda e, hd=hd, h2=h2: e.matmul(ps_[:, h2, 0:128], lhsT=kT[:, hd, b, :], rhs=qT[:, hd, b, :], start=True, stop=True))
                if has_prev:
                    fns.append(lambda e, hd=hd, h2=h2: e.matmul(ps_[:, h2, 128:256], lhsT=kT[:, hd, b - 1, :], rhs=qT[:, hd, b, :],
                                                               start=True, stop=True))
            S.group("pe", fns, reads=[b_qT[b], b_kT[b]] + ([b_kT[b - 1]] if has_prev else []), writes=[bps])
            yield
            p_, bp = dl["pT"], dl["b_pT"]
            wd = 256 if has_prev else 128
            S.op("act", lambda e: e.activation(out=p_[:, :, 0:wd], in_=ps_[:, :, 0:wd], func=AF.Exp, scale=SC), reads=[bps], writes=[bp])
            yield
            S.op("dve", lambda e: e.tensor_tensor(out=p_[:, :, 0:wd], in0=p_[:, :, 0:wd], in1=mask2[:, :, 0:wd], op=ALU.mult),
                 reads=[bp, b_cst], writes=[bp])
            yield
            fns = []
            for h2 in range(2):
                hd = pr * 2 + h2
                fns.append(lambda e, hd=hd, h2=h2: e.matmul(ps_[:, h2, 0:129], lhsT=p_[:, h2, 0:128], rhs=Va[:, b, hd, 0:129],
                                                           start=True, stop=not has_prev, skip_group_check=True))
                if has_prev:
                    fns.append(lambda e, hd=hd, h2=h2: e.matmul(ps_[:, h2, 0:129], lhsT=p_[:, h2, 128:256], rhs=Va[:, b - 1, hd, 0:129],
                                                               start=False, stop=True, skip_group_check=True))
            S.group("pe", fns, reads=[bp, b_Va[b]] + ([b_Va[b - 1]] if has_prev else []), writes=[bps])
            yield
            st_, bst = dl["st"], dl["b_st"]
            S.op("dve", lambda e: e.tensor_copy(out=st_[:, :, 0:129], in_=ps_[:, :, 0:129]), reads=[bps], writes=[bst])
            tok0 = r + dd * sbk * 128
            h0 = hh * 4 + pr * 2
            S.dma("sp", dl["s_st"], K.num_d[g, tok0:tok0 + 127 * dd + 1:dd, h0:h0 + 2, :], st_, reads=[bst], writes=[])
            yield

        load_slab(0)
        for si in range(len(slabs)):
            g, hh, t3 = slabs[si]
            if si + 1 < len(slabs):
                load_slab(si + 1)
            if hh == 0 and t3 == 0:
                S.dma("sp", s_rope, rope[:], K.inp["rope"][:, g, :, :, :], writes=[b_rope])
            run_rolling([(lambda li, b=b, si=si: proj_blk(li, si, b)) for b in range(NB)], NL, stagger=3)
            if t3 == 2:
                units = [(b, pr) for b in range(NB) for pr in range(2)]
                run_rolling([(lambda li, b=b, pr=pr, g=g, hh=hh: att_unit(li, g, hh, b, pr)) for (b, pr) in units], NL, stagger=1)
        S.barrier()
    with contextlib.ExitStack() as ph:
        sb = lambda n, shp, dt: _sb(K, ph, n, shp, dt)
        NC_ = 4
        LB = []
        for i in range(NC_):
            d = {"nin": [sb("d_nin%d_%d" % (i, g), [128, 8, 132], F32) for g in range(3)], "b_nin": [Buf() for g in range(3)],
                 "s_nin": [S.new_slot() for g in range(3)], "rden": sb("d_rden%d" % i, [128, 8], F32), "b_rden": Buf(),
                 "otm": sb("d_otm%d" % i, [128, 8, 128], BF16), "b_otm": Buf(),
                 "ps": _ps(K, ph, "d_pc%d_a" % i, [128, 512], F32), "ps2": _ps(K, ph, "d_pc%d_b" % i, [128, 512], F32),
                 "b_ps": PB(), "b_ps2": PB(),
                 "ot": sb("d_ot%d" % i, [128, KC, 128], BF16), "b_ot": Buf(), "s_o": S.new_slot()}
            LB.append(d)

        def comb(li, blk):
            d = LB[li]
            nin, bn = d["nin"], d["b_nin"]
            for g in range(3):
                S.dma("sp", d["s_nin"][g], nin[g][:], K.num_d[g, blk * 128:(blk + 1) * 128, :, :], reads=[K.b_num], writes=[bn[g]])
            yield
            S.op("dve", lambda e: e.tensor_tensor(out=nin[0][:], in0=nin[0][:], in1=nin[1][:], op=ALU.add), reads=[bn[0], bn[1]], writes=[bn[0]])
            yield
            S.op("dve", lambda e: e.tensor_tensor(out=nin[0][:], in0=nin[0][:], in1=nin[2][:], op=ALU.add), reads=[bn[0], bn[2]], writes=[bn[0]])
            yield
            S.op("dve", lambda e: e.reciprocal(out=d["rden"][:].unsqueeze(2), in_=nin[0][:, :, 128:129]), reads=[bn[0]], writes=[d["b_rden"]])
            yield
            S.op("dve", lambda e: e.tensor_tensor(out=d["otm"][:], in0=nin[0][:, :, 0:128],
                                                 in1=d["rden"][:].unsqueeze(2).to_broadcast([128, 8, 128]), op=ALU.mult),
                 reads=[bn[0], d["b_rden"]], writes=[d["b_otm"]])
            yield
            S.group("pe", [lambda e, hd=hd: e.matmul(d["ps"][:, hd * 128:(hd + 1) * 128], lhsT=d["otm"][:, hd, :], rhs=K.ident_bf,
                                                    start=True, stop=True) for hd in range(4)], reads=[d["b_otm"], K.b_const], writes=[d["b_ps"]])
            S.group("pe", [lambda e, hd=hd: e.matmul(d["ps2"][:, hd * 128:(hd + 1) * 128], lhsT=d["otm"][:, 4 + hd, :], rhs=K.ident_bf,
                                                    start=True, stop=True) for hd in range(4)], reads=[d["b_otm"], K.b_const], writes=[d["b_ps2"]])
            yield
            S.op("act", lambda e: e.copy(out=d["ot"][:, 0:4, :], in_=d["ps"][:].rearrange("p (a c) -> p a c", c=128)), reads=[d["b_ps"]], writes=[d["b_ot"]])
            yield
            S.op("act", lambda e: e.copy(out=d["ot"][:, 4:8, :], in_=d["ps2"][:].rearrange("p (a c) -> p a c", c=128)), reads=[d["b_ps2"]], writes=[d["b_ot"]])
            S.dma("sp", d["s_o"], K.oT_d[:, :, blk * 128:(blk + 1) * 128], d["ot"][:], reads=[d["b_ot"]], writes=[K.b_oT[blk // 2]])
            yield

        run_rolling([(lambda li, blk=blk: comb(li, blk)) for blk in range(T // 128)], NC_, stagger=2)
        S.barrier()


MIXERS["dsw"] = phase_dsw


def build(n_layers=DEPTH, mixers=True, dbg=None, mixer_seq=None):
    nc = bass.Bass("TRN2", target_bir_lowering=False)
    K = Ctx()
    K.nc = nc
    K.uid = 0
    inp = {}

    def di(name, shape, dt=F32):
        inp[name] = nc.dram_tensor(name, shape, dt, kind="ExternalInput").ap()

    di("x", [T, D])
    di("cT", [128, KC])
    di("mod_w", [DEPTH, D, 6 * D])
    di("modb", [128, DEPTH, 48])
    di("mixg", [128, DEPTH, KC])
    di("ffng", [128, DEPTH, KC])
    di("gdn_w_in", [2, D, GDN_IN])
    di("gdn_conv", [128, 2, 24, 4])
    di("gdn_hc", [128, 2, 2, 8])
    di("gdn_ng", [128, 2])
    di("gdn_w_out", [2, D, D])
    di("dsw_w_in", [2, D, DSW_IN])
    di("dsw_gain", [128, 2, 6, 128])
    di("dsw_w_out", [2, D, D])
    di("ffn_w_gate_up", [DEPTH, D, 2 * FH])
    di("ffn_w_down", [DEPTH, FH, D])
    di("cst", [128, 6, 128])
    di("rope", [128, 3, 2, 32, 16])
    di("cst2", [128, 6, 2, 128])
    K.inp = inp
    K.out = {"y": nc.dram_tensor("y", [T, D], F32, kind="ExternalOutput").ap()}
    sk = "ExternalOutput" if dbg is not None else "Internal"
    K.xT_d = nc.dram_tensor("xT_d", [128, KC, T], F32, kind=sk).ap()
    K.hT_d = nc.dram_tensor("hT_d", [128, KC, T], BF16, kind=sk).ap()
    K.oT_d = nc.dram_tensor("oT_d", [128, KC, T], BF16, kind=sk).ap()
    K.h2T_d = nc.dram_tensor("h2T_d", [128, KC, T], BF16, kind=sk).ap()
    K.num_d = nc.dram_tensor("num_d", [3, T, 8, 132], F32, kind="Internal").ap()
    K.F_d = nc.dram_tensor("F_d", [32, 128, T], BF16, kind="Internal").ap()
    K.b_F = Buf()
    K.b_xT = [Buf() for _ in range(NT)]
    K.b_hT = [Buf() for _ in range(NT)]
    K.b_oT = [Buf() for _ in range(NT)]
    K.b_h2T = [Buf() for _ in range(NT)]
    K.b_num = Buf()
    K.b_y = Buf()
    K.b_ys = [Buf(), Buf()]
    K.dbg = dbg
    if dbg:
        for nm, (shape, dt) in dbg.items():
            K.out[nm] = nc.dram_tensor(nm, shape, dt, kind="ExternalOutput").ap()

    with contextlib.ExitStack() as st:
        S = Sched(nc, st)
        K.S = S
        K.st = st
        K.cst_f = st.enter_context(nc.sbuf_tensor("cst_f", [128, 6, 128], F32))
        K.cst_b = st.enter_context(nc.sbuf_tensor("cst_b", [128, 6, 128], BF16))
        K.eps1024 = st.enter_context(nc.sbuf_tensor("eps1024", [128, 4], F32))
        K.b_const = Buf()
        s0 = S.new_slot()
        s0w = S.new_slot(sw=True)
        K.b_const2 = Buf()
        S.dma("pool", s0w, K.cst_b[:], inp["cst"], writes=[K.b_const2])
        S.dma("sp", s0, K.cst_f[:], inp["cst"], writes=[K.b_const])
        S.op("dve", lambda e: e.memset(K.eps1024[:, 0:1], 1024.0 * EPS), writes=[K.b_const])
        S.op("dve", lambda e: e.memset(K.eps1024[:, 1:2], EPS), writes=[K.b_const])
        S.op("dve", lambda e: e.memset(K.eps1024[:, 2:3], 1.0), writes=[K.b_const])
        S.op("dve", lambda e: e.memset(K.eps1024[:, 3:4], 0.0), writes=[K.b_const])
        S.barrier()
        K.ident_f = K.cst_f[:, 0, :]
        K.ones_f = K.cst_f[:, 1, :]
        K.triU_f = K.cst_f[:, 2, :]
        K.triUs_f = K.cst_f[:, 3, :]
        K.ident_bf = K.cst_b[:, 0, :]
        K.ones_bf = K.cst_b[:, 1, :]
        K.vec = {nm: st.enter_context(nc.sbuf_tensor("v_" + nm, [128, DEPTH, KC], F32))
                 for nm in ("gs1", "sh1", "g1", "gs2", "sh2", "g2")}
        K.b_vec = Buf()

        phase_mod(K)
        phase_x0(K)
        for l in range(n_layers):
            j = l // 2
            mix = mixer_seq[l] if mixer_seq else ("gdn" if l % 2 == 0 else "dsw")
            if mixers:
                MIXERS[mix](K, l, j)
            else:
                stub_mixer(K)
            phase_t1(K, l, inp["gdn_w_out"][j] if mix == "gdn" else inp["dsw_w_out"][j])
            phase_t2(K, l, inp["ffn_w_gate_up"][l], inp["ffn_w_down"][l], last=(l == n_layers - 1))
        deps = [K.b_ys[0].w, K.b_ys[1].w]
        S._wait("sp", deps)
        K.ninst = S.ninst
    return nc, K


def stub_mixer(K):
    S = K.S
    with contextlib.ExitStack() as ph:
        tb = [_sb(K, ph, "stub%d" % i, [128, KC, TW], BF16) for i in range(2)]
        bt = [Buf(), Buf()]
        sl = [S.new_slot(), S.new_slot()]
        so2_ = [S.new_slot(), S.new_slot()]
        for t in range(NT):
            i = t % 2
            S.dma("sp", sl[i], tb[i][:], K.hT_d[:, :, t * TW:(t + 1) * TW], reads=[K.b_hT[t]], writes=[bt[i]])
            S.dma("sp", so2_[i], K.oT_d[:, :, t * TW:(t + 1) * TW], tb[i][:], reads=[bt[i]], writes=[K.b_oT[t]])
        S.barrier()


def _fm(v):
    v = np.asarray(v, np.float32)
    lead = v.shape[:-1]
    r = v.reshape(lead + (KC, 128))
    r = np.moveaxis(r, -1, 0)
    return np.ascontiguousarray(r)


def host_consts():
    cst = np.zeros((128, 6, 128), np.float32)
    p = np.arange(128)[:, None]
    f = np.arange(128)[None, :]
    cst[:, 0] = (p == f)
    cst[:, 1] = 1.0
    cst[:, 2] = (f >= p)
    cst[:, 3] = (f > p)
    cst[:, 4] = (f <= p)
    cst[:, 5] = np.where(f > p, 0.0, -30000.0)
    half = 8 * 2
    inv = np.exp(-math.log(ROPE_THETA) * (2.0 * np.arange(16, dtype=np.float32) / 32.0)).astype(np.float32)
    rope = np.zeros((128, 3, 2, 32, 16), np.float32)
    for g, d in enumerate(DSW_DIL):
        nsb = (T // d) // 128
        for b in range(32):
            r = b // nsb
            sb = b % nsb
            pos = (r + d * (sb * 128 + np.arange(128))).astype(np.float32)
            ang = (pos[:, None] * inv[None, :]).astype(np.float32)
            rope[:, g, 0, b, :] = np.cos(ang)
            rope[:, g, 1, b, :] = np.sin(ang)
    cst2 = np.zeros((128, 6, 2, 128), np.float32)
    bd = (p // 8 == f // 8).astype(np.float32)
    cst2[:, 0, 0] = bd
    cst2[:, 0, 1] = bd
    for li, sz in enumerate((16, 32, 64, 128)):
        em = ((p // sz == f // sz) & (p % sz >= sz // 2) & (f % sz < sz // 2)).astype(np.float32)
        cst2[:, 1 + li, 0] = em
        cst2[:, 1 + li, 1] = em.T
    cst2[:, 5, 0] = (p == f)
    cst2[:, 5, 1] = (p == f)
    return cst, rope, cst2


def make_in_maps(inputs, n_cores=8):
    f = lambda a: np.ascontiguousarray(np.asarray(a, np.float32))
    cst, rope, cst2 = host_consts()
    mod_b = f(inputs["mod_b"])
    modb = np.ascontiguousarray(np.moveaxis(mod_b.reshape(DEPTH, 48, 128), -1, 0))
    mixg = np.ascontiguousarray(np.moveaxis(f(inputs["mix_norm_g"]).reshape(DEPTH, KC, 128), -1, 0))
    ffng = np.ascontiguousarray(np.moveaxis(f(inputs["ffn_norm_g"]).reshape(DEPTH, KC, 128), -1, 0))
    conv = f(inputs["gdn_conv_w"])
    gdn_conv = np.ascontiguousarray(np.transpose(conv.reshape(2, 4, 24, 128), (3, 0, 2, 1)))
    hc = np.stack([f(inputs["gdn_A_log"]), f(inputs["gdn_dt_bias"])], axis=1)
    gdn_hc = np.ascontiguousarray(np.broadcast_to(hc[None], (128, 2, 2, 8)))
    gdn_ng = np.ascontiguousarray(f(inputs["gdn_norm_g"]).T)
    qg = f(inputs["dsw_q_norm_g"])
    kg = f(inputs["dsw_k_norm_g"])
    gain = np.stack([qg, kg], axis=2).reshape(2, 6, 128)
    dsw_gain = np.ascontiguousarray(np.broadcast_to(gain[None], (128, 2, 6, 128)))
    shared = {
        "mod_w": f(inputs["mod_w"]), "modb": modb, "mixg": mixg, "ffng": ffng,
        "gdn_w_in": f(inputs["gdn_w_in"]), "gdn_conv": gdn_conv, "gdn_hc": gdn_hc, "gdn_ng": gdn_ng,
        "gdn_w_out": f(inputs["gdn_w_out"]), "dsw_w_in": f(inputs["dsw_w_in"]), "dsw_gain": dsw_gain,
        "dsw_w_out": f(inputs["dsw_w_out"]), "ffn_w_gate_up": f(inputs["ffn_w_gate_up"]),
        "ffn_w_down": f(inputs["ffn_w_down"]), "cst": cst, "rope": rope, "cst2": cst2,
    }
    x = f(inputs["x"])
    c = f(inputs["c"])
    maps = []
    for i in range(n_cores):
        b = i % 4
        m = dict(shared)
        m["x"] = np.ascontiguousarray(x[b])
        m["cT"] = np.ascontiguousarray(c[b].reshape(KC, 128).T)
        maps.append(m)
    return maps


def kernel(**inputs):
    nc, K = build()
    maps = make_in_maps(inputs, 8)
    res = run_bass_kernel_spmd(nc, maps, core_ids=list(range(8)))
    out = np.stack([np.asarray(res.results[b]["y"], np.float32) for b in range(4)], axis=0)
    return out
```

```python
import contextlib
import math
import numpy as np
import concourse.bass as bass
import concourse.mybir as mybir
from concourse.alu_op_type import AluOpType as ALU
from concourse.bass_utils import run_bass_kernel_spmd

AF = mybir.ActivationFunctionType
F32 = mybir.dt.float32
BF16 = mybir.dt.bfloat16

D = 1024
T = 4096
DEPTH = 4
KC = 8
FH = 2816
NJ = FH // 128
EPS = 1e-6
TW = 256
NT = T // TW
GDN_IN = 4112
DSW_IN = 9216
DSW_DIL = (1, 4, 16)
ROPE_THETA = 500000.0


class Buf:
    __slots__ = ("w", "r", "name", "excl")

    def __init__(self, name="", excl=False):
        self.w = None
        self.r = []
        self.name = name
        self.excl = excl


def PB():
    return Buf(excl=True)


class Sched:
    def __init__(self, nc, stack):
        self.nc = nc
        self.eng = {"pe": nc.tensor, "dve": nc.vector, "act": nc.scalar,
                    "pool": nc.gpsimd, "sp": nc.sync}
        self.sems = {}
        self.cnt = {}
        self.stack = stack
        for e in self.eng:
            self.sems[e] = stack.enter_context(nc.semaphore("s_" + e))
            self.cnt[e] = 0
        self.waited = {e: {} for e in self.eng}
        self.nslot = 0
        self.ninst = 0
        self.free_slots = []
        self.free_sw = []
        self.live_slots = []

    def new_slot(self, sw=False):
        fl = self.free_sw if sw else self.free_slots
        if fl:
            k = fl.pop()
        else:
            k = ("w%d" if sw else "d%d") % self.nslot
            self.nslot += 1
            self.sems[k] = self.stack.enter_context(self.nc.semaphore("s_" + k))
            self.cnt[k] = 0
        self.live_slots.append(k)
        return k

    def _wait(self, e, deps):
        best = {}
        for d in deps:
            if d is None:
                continue
            k, v = d
            if best.get(k, 0) < v:
                best[k] = v
        w = self.waited[e]
        for k, v in best.items():
            if k == e and e == "pe":
                continue
            if w.get(k, 0) < v:
                self.eng[e].wait_ge(self.sems[k], v)
                w[k] = v
                self.ninst += 1

    @staticmethod
    def _deps(reads, writes):
        deps = []
        for b in reads:
            deps.append(b.w)
            if b.excl:
                deps.extend(b.r)
        for b in writes:
            deps.append(b.w)
            deps.extend(b.r)
        return deps

    @staticmethod
    def _stamp(st, reads, writes):
        for b in reads:
            if b.excl:
                b.w = st
                b.r = []
                continue
            b.r.append(st)
            if len(b.r) > 64:
                best = {}
                for k, v in b.r:
                    if best.get(k, 0) < v:
                        best[k] = v
                b.r = list(best.items())
        for b in writes:
            b.w = st
            b.r = []

    def op(self, e, fn, reads=(), writes=()):
        self._wait(e, self._deps(reads, writes))
        inst = fn(self.eng[e])
        self.cnt[e] += 1
        inst.then_inc(self.sems[e], 1)
        self.ninst += 1
        self._stamp((e, self.cnt[e]), reads, writes)
        return inst

    def group(self, e, fns, reads=(), writes=()):
        self._wait(e, self._deps(reads, writes))
        inst = None
        for fn in fns:
            inst = fn(self.eng[e])
            self.ninst += 1
        self.cnt[e] += 1
        inst.then_inc(self.sems[e], 1)
        self._stamp((e, self.cnt[e]), reads, writes)
        return inst

    def dma(self, q, slot, out, in_, reads=(), writes=()):
        assert (slot[0] == "w") == (q == "pool"), (q, slot)
        self._wait(q, self._deps(reads, writes))
        inst = self.eng[q].dma_start(out=out, in_=in_)
        self.cnt[slot] += 16
        inst.then_inc(self.sems[slot], 16)
        self.ninst += 1
        self._stamp((slot, self.cnt[slot]), reads, writes)
        return inst

    def barrier(self):
        deps = [(k, v) for k, v in self.cnt.items() if v > 0]
        for e in self.eng:
            w = self.waited[e]
            for k, v in deps:
                if k != e and w.get(k, 0) < v:
                    self.eng[e].wait_ge(self.sems[k], v)
                    w[k] = v
        for k in self.live_slots:
            (self.free_sw if k[0] == "w" else self.free_slots).append(k)
        self.live_slots = []


class Ctx:
    pass


def _sb(K, ph, name, shape, dt):
    K.uid += 1
    return ph.enter_context(K.nc.sbuf_tensor("%s_%d" % (name, K.uid), shape, dt))


def _ps(K, ph, name, shape, dt):
    K.uid += 1
    return ph.enter_context(K.nc.psum_tensor("%s_%d" % (name, K.uid), shape, dt))


def mod_setup(K, ph):
    nc, S = K.nc, K.S
    M = {}
    M["cT"] = _sb(K, ph, "cT", [128, KC], F32)
    M["cond"] = _sb(K, ph, "cond", [128, KC], BF16)
    M["mixg"] = _sb(K, ph, "mixg", [128, DEPTH, KC], F32)
    M["ffng"] = _sb(K, ph, "ffng", [128, DEPTH, KC], F32)
    M["modb"] = _sb(K, ph, "modb", [128, DEPTH, 48], F32)
    M["modv"] = _sb(K, ph, "modv", [128, 48], F32)
    M["wsl"] = [_sb(K, ph, "mwsl%d" % i, [128, KC, 512], BF16) for i in range(2)]
    M["mps"] = _ps(K, ph, "modps", [128, 512], F32)
    for k in ("b_c", "b_cond", "b_mixg", "b_ffng", "b_modb", "b_modv"):
        M[k] = Buf()
    M["b_mps"] = PB()
    M["b_w"] = [Buf(), Buf()]
    M["sl"] = [S.new_slot(sw=True), S.new_slot(sw=True)]
    S.dma("sp", S.new_slot(), M["cT"][:], K.inp["cT"], writes=[M["b_c"]])
    S.dma("sp", S.new_slot(), M["mixg"][:], K.inp["mixg"], writes=[M["b_mixg"]])
    S.dma("sp", S.new_slot(), M["ffng"][:], K.inp["ffng"], writes=[M["b_ffng"]])
    S.dma("sp", S.new_slot(), M["modb"][:], K.inp["modb"], writes=[M["b_modb"]])
    S.op("act", lambda e: e.activation(out=M["cond"][:], in_=M["cT"][:], func=AF.Silu), reads=[M["b_c"]], writes=[M["b_cond"]])
    return M


def mod_layer_gen(K, M, l):
    nc, S = K.nc, K.S
    wsl, mps, cond, modv, modb, mixg, ffng = M["wsl"], M["mps"], M["cond"], M["modv"], M["modb"], M["mixg"], M["ffng"]
    wv = K.inp["mod_w"][l].rearrange("(kc p) n -> p kc n", p=128)
    for s_ in range(12):
        w = wsl[s_ % 2]
        bw = M["b_w"][s_ % 2]
        S.dma("pool", M["sl"][s_ % 2], w[:], wv[:, :, s_ * 512:(s_ + 1) * 512], writes=[bw])
        yield
        for mm in range(4):
            m = s_ * 4 + mm
            S.group("pe", [lambda e, kc=kc, mm=mm, m=m, w=w: e.matmul(
                mps[:, m:m + 1], lhsT=w[:, kc, mm * 128:(mm + 1) * 128], rhs=cond[:, kc:kc + 1],
                start=(kc == 0), stop=(kc == KC - 1)) for kc in range(KC)], reads=[bw, M["b_cond"]], writes=[M["b_mps"]])
            yield
    S.op("dve", lambda e: e.tensor_tensor(out=modv[:], in0=mps[:, 0:48], in1=modb[:, l, :], op=ALU.add),
         reads=[M["b_mps"], M["b_modb"]], writes=[M["b_modv"]])
    V = K.vec
    bv = K.b_vecl[l]
    b_modv = M["b_modv"]
    S.op("dve", lambda e: e.scalar_tensor_tensor(out=V["gs1"][:, l, :], in0=modv[:, 8:16], scalar=1.0,
                                                in1=mixg[:, l, :], op0=ALU.add, op1=ALU.mult),
         reads=[b_modv, M["b_mixg"]], writes=[bv])
    S.op("dve", lambda e: e.tensor_scalar(out=V["gs1"][:, l, :], in0=V["gs1"][:, l, :], scalar1=32.0,
                                         scalar2=None, op0=ALU.mult), reads=[bv], writes=[bv])
    S.op("dve", lambda e: e.scalar_tensor_tensor(out=V["gs2"][:, l, :], in0=modv[:, 32:40], scalar=1.0,
                                                in1=ffng[:, l, :], op0=ALU.add, op1=ALU.mult),
         reads=[b_modv, M["b_ffng"]], writes=[bv])
    S.op("dve", lambda e: e.tensor_scalar(out=V["gs2"][:, l, :], in0=V["gs2"][:, l, :], scalar1=32.0,
                                         scalar2=None, op0=ALU.mult), reads=[bv], writes=[bv])
    for nm, off in (("sh1", 0), ("g1", 16), ("sh2", 24), ("g2", 40)):
        S.op("dve", lambda e, nm=nm, off=off: e.tensor_copy(out=V[nm][:, l, :], in_=modv[:, off:off + 8]),
             reads=[b_modv], writes=[bv])
    yield


def emit_norm(K, N, x, b_x, h, b_h, gs, sh, W):
    S = K.S
    S.op("act", lambda e: e.activation(out=N["sq"][:, :, 0:W], in_=x[:, :, 0:W], func=AF.Square),
         reads=[b_x], writes=[N["b_sq"]])
    S.group("pe", [lambda e, kc=kc: e.matmul(N["ss"][:, 0:W], lhsT=K.ones_bf, rhs=N["sq"][:, kc, 0:W],
                                             start=(kc == 0), stop=(kc == KC - 1)) for kc in range(KC)],
            reads=[N["b_sq"], K.b_const], writes=[N["b_ss"]])
    S.op("act", lambda e: e.activation(out=N["rstd"][:, 0:W], in_=N["ss"][:, 0:W], func=AF.Ln,
                                       bias=K.eps1024[:, 0:1], scale=1.0),
         reads=[N["b_ss"], K.b_const], writes=[N["b_rstd"]])
    S.op("act", lambda e: e.activation(out=N["rstd"][:, 0:W], in_=N["rstd"][:, 0:W], func=AF.Exp, scale=-0.5),
         reads=[N["b_rstd"]], writes=[N["b_rstd"]])
    for kc in range(KC):
        S.op("dve", lambda e, kc=kc: e.tensor_tensor(out=N["tmp"][:, kc, 0:W], in0=x[:, kc, 0:W],
                                                    in1=N["rstd"][:, 0:W], op=ALU.mult),
             reads=[b_x, N["b_rstd"]], writes=[N["b_tmp"]])
        S.op("act", lambda e, kc=kc: e.activation(out=h[:, kc, 0:W], in_=N["tmp"][:, kc, 0:W], func=AF.Identity,
                                                 bias=sh[:, kc:kc + 1], scale=gs[:, kc:kc + 1]),
             reads=[N["b_tmp"], K.b_vec], writes=[b_h])


def norm_gen(K, N, x, b_x, h, b_h, gs, sh, W, bvec=None):
    S = K.S
    bvec = K.b_vec if bvec is None else bvec
    S.op("act", lambda e: e.activation(out=N["sq"][:, :, 0:W], in_=x[:, :, 0:W], func=AF.Square),
         reads=[b_x], writes=[N["b_sq"]])
    yield
    S.group("pe", [lambda e, kc=kc: e.matmul(N["ss"][:, 0:W], lhsT=K.ones_bf, rhs=N["sq"][:, kc, 0:W],
                                             start=(kc == 0), stop=(kc == KC - 1)) for kc in range(KC)],
            reads=[N["b_sq"], K.b_const], writes=[N["b_ss"]])
    yield
    S.op("act", lambda e: e.activation(out=N["rstd"][:, 0:W], in_=N["ss"][:, 0:W], func=AF.Ln,
                                       bias=K.eps1024[:, 0:1], scale=1.0),
         reads=[N["b_ss"], K.b_const], writes=[N["b_rstd"]])
    yield
    S.op("act", lambda e: e.activation(out=N["rstd"][:, 0:W], in_=N["rstd"][:, 0:W], func=AF.Exp, scale=-0.5),
         reads=[N["b_rstd"]], writes=[N["b_rstd"]])
    yield
    for kc in range(KC):
        S.op("dve", lambda e, kc=kc: e.tensor_tensor(out=N["tmp"][:, kc, 0:W], in0=x[:, kc, 0:W],
                                                    in1=N["rstd"][:, 0:W], op=ALU.mult),
             reads=[b_x, N["b_rstd"]], writes=[N["b_tmpk"][kc]])
        yield
        S.op("act", lambda e, kc=kc: e.activation(out=h[:, kc, 0:W], in_=N["tmp"][:, kc, 0:W], func=AF.Identity,
                                                 bias=sh[:, kc:kc + 1], scale=gs[:, kc:kc + 1]),
             reads=[N["b_tmpk"][kc], bvec], writes=[b_h])
        yield


def alloc_norm(K, ph, W):
    N = {}
    N["sq"] = _sb(K, ph, "nsq", [128, KC, W], BF16)
    N["tmp"] = _sb(K, ph, "ntmp", [128, KC, W], F32)
    N["rstd"] = _sb(K, ph, "nrstd", [128, W], F32)
    N["ss"] = _ps(K, ph, "nss", [128, 512], F32)
    for k in ("b_sq", "b_tmp", "b_rstd"):
        N[k] = Buf()
    N["b_ss"] = PB()
    N["b_tmpk"] = [Buf() for _ in range(KC)]
    return N


def phase_x0(K):
    nc, S = K.nc, K.S
    NLX = 3
    with contextlib.ExitStack() as ph:
        M = mod_setup(K, ph)
        for _ in mod_layer_gen(K, M, 0):
            pass
        LX = []
        for i in range(NLX):
            d = {"xin": _sb(K, ph, "xin%d" % i, [128, 2, D], F32), "b_xin": Buf(), "s_in": S.new_slot(),
                 "xt": _sb(K, ph, "xt%d" % i, [128, KC, TW], F32), "b_xt": Buf(),
                 "ht": _sb(K, ph, "ht%d" % i, [128, KC, TW], BF16), "b_ht": Buf(),
                 "pt": _ps(K, ph, "x0pt%d" % i, [128, 4, 128], F32), "b_pt": PB(),
                 "N": alloc_norm(K, ph, TW), "s_o": S.new_slot(), "s_o2": S.new_slot()}
            LX.append(d)

        def job(li, t):
            d = LX[li]
            xin, xt, ht, pt = d["xin"], d["xt"], d["ht"], d["pt"]
            S.dma("sp", d["s_in"], xin[:], K.inp["x"][t * TW:(t + 1) * TW, :].rearrange("(a p) n -> p a n", p=128), writes=[d["b_xin"]])
            yield
            for half in range(2):
                for hf in range(2):
                    S.group("pe", [lambda e, q=q, hf=hf, half=half: e.matmul(
                        pt[:, q, :], lhsT=xin[:, half, (hf * 4 + q) * 128:(hf * 4 + q + 1) * 128], rhs=K.ident_f, start=True, stop=True)
                        for q in range(4)], reads=[d["b_xin"], K.b_const], writes=[d["b_pt"]])
                    yield
                    if hf == 0:
                        S.op("act", lambda e, hf=hf, half=half: e.copy(out=xt[:, hf * 4:(hf + 1) * 4, half * 128:(half + 1) * 128], in_=pt[:]),
                             reads=[d["b_pt"]], writes=[d["b_xt"]])
                    else:
                        S.op("dve", lambda e, hf=hf, half=half: e.tensor_copy(out=xt[:, hf * 4:(hf + 1) * 4, half * 128:(half + 1) * 128], in_=pt[:]),
                             reads=[d["b_pt"]], writes=[d["b_xt"]])
                    yield
            yield from norm_gen(K, d["N"], xt, d["b_xt"], ht, d["b_ht"], K.vec["gs1"][:, 0, :], K.vec["sh1"][:, 0, :], TW, bvec=K.b_vecl[0])
            S.dma("sp", d["s_o"], K.xT_d[:, :, t * TW:(t + 1) * TW], xt[:], reads=[d["b_xt"]], writes=[K.b_xT[t]])
            S.dma("sp", d["s_o2"], K.hT_d[:, :, t * TW:(t + 1) * TW], ht[:], reads=[d["b_ht"]], writes=[K.b_hT[t]])
            yield

        def mod_rest(li):
            for l in range(1, DEPTH):
                yield from mod_layer_gen(K, M, l)

        jobs = [(lambda li, t=t: job(li, t)) for t in range(NT)]
        active = {NLX: mod_rest(NLX)}
        nxt = 0
        step = 0
        while nxt < len(jobs) or active:
            for li in range(NLX):
                if li not in active and nxt < len(jobs) and step >= li * 5:
                    active[li] = jobs[nxt](li)
                    nxt += 1
            for li in list(active.keys()):
                try:
                    next(active[li])
                except StopIteration:
                    del active[li]
            step += 1
        S.barrier()


def phase_t1(K, l, w_out_ap):
    nc, S = K.nc, K.S
    NLT = 3
    with contextlib.ExitStack() as ph:
        wo = _sb(K, ph, "wo", [128, KC, D], BF16)
        b_wo = Buf()
        S.dma("pool", S.new_slot(sw=True), wo[:], w_out_ap.rearrange("(kc p) n -> p kc n", p=128), writes=[b_wo])
        V = K.vec
        LT = []
        for i in range(NLT):
            d = {"xt": _sb(K, ph, "t1x%d" % i, [128, KC, TW], F32), "b_xt": Buf(), "s_x": S.new_slot(),
                 "ot": _sb(K, ph, "t1o%d" % i, [128, KC, TW], BF16), "b_ot": Buf(), "s_o": S.new_slot(),
                 "ht": _sb(K, ph, "t1h%d" % i, [128, KC, TW], BF16), "b_ht": Buf(),
                 "acc": _ps(K, ph, "t1acc%d" % i, [128, 2, TW], F32), "b_acc": PB(),
                 "N": alloc_norm(K, ph, TW), "s_so": S.new_slot(), "s_so2": S.new_slot()}
            LT.append(d)

        def job(li, t):
            d = LT[li]
            xt, ot, ht, acc = d["xt"], d["ot"], d["ht"], d["acc"]
            S.dma("sp", d["s_x"], xt[:], K.xT_d[:, :, t * TW:(t + 1) * TW], reads=[K.b_xT[t]], writes=[d["b_xt"]])
            S.dma("sp", d["s_o"], ot[:], K.oT_d[:, :, t * TW:(t + 1) * TW], reads=[K.b_oT[t]], writes=[d["b_ot"]])
            yield
            for m in range(KC):
                S.group("pe", [lambda e, kc=kc, m=m: e.matmul(
                    acc[:, m % 2, :], lhsT=wo[:, kc, m * 128:(m + 1) * 128], rhs=ot[:, kc, :],
                    start=(kc == 0), stop=(kc == KC - 1)) for kc in range(KC)], reads=[b_wo, d["b_ot"]], writes=[d["b_acc"]])
                yield
                S.op("dve", lambda e, m=m: e.scalar_tensor_tensor(
                    out=xt[:, m, :], in0=acc[:, m % 2, :], scalar=V["g1"][:, l, m:m + 1], in1=xt[:, m, :],
                    op0=ALU.mult, op1=ALU.add), reads=[d["b_acc"], d["b_xt"], K.b_vec], writes=[d["b_xt"]])
                yield
            yield from norm_gen(K, d["N"], xt, d["b_xt"], ht, d["b_ht"], V["gs2"][:, l, :], V["sh2"][:, l, :], TW)
            S.dma("sp", d["s_so"], K.xT_d[:, :, t * TW:(t + 1) * TW], xt[:], reads=[d["b_xt"]], writes=[K.b_xT[t]])
            S.dma("sp", d["s_so2"], K.h2T_d[:, :, t * TW:(t + 1) * TW], ht[:], reads=[d["b_ht"]], writes=[K.b_h2T[t]])
            yield

        run_rolling([(lambda li, t=t: job(li, t)) for t in range(NT)], NLT, stagger=6)
        S.barrier()


def phase_t2(K, l, w_gu_ap, w_dn_ap, last):
    nc, S = K.nc, K.S
    TW2 = 512
    NT2 = T // TW2
    with contextlib.ExitStack() as ph:
        wgu = _sb(K, ph, "wgu", [128, KC, 2 * FH], BF16)
        wdn = _sb(K, ph, "wdn", [128, NJ, D], BF16)
        b_wgu = [Buf() for _ in range(KC)]
        b_wdn = Buf()
        xt = _sb(K, ph, "t2x", [128, KC, TW2], F32)
        b_xt = Buf()
        h2 = [_sb(K, ph, "t2h%d" % i, [128, KC, TW2], BF16) for i in range(2)]
        b_h2 = [Buf(), Buf()]
        act = _sb(K, ph, "t2act", [128, NJ, TW2], BF16)
        b_act = [Buf() for _ in range(NJ)]
        sg = [_sb(K, ph, "t2sg", [128, TW2], F32)] * 2
        b_sg = [Buf()] * 2
        pg = [_ps(K, ph, "t2pg%d" % i, [128, TW2], F32) for i in range(4)]
        b_pg = [PB() for _ in range(4)]
        pacc = [_ps(K, ph, "t2pa%d" % i, [128, TW2], F32) for i in range(4)]
        b_pacc = [PB() for _ in range(4)]
        slx = S.new_slot()
        slh = [S.new_slot(), S.new_slot()]
        so = S.new_slot()
        so2 = S.new_slot()
        gv = w_gu_ap.rearrange("(kc p) n -> p kc n", p=128)
        for kc in range(KC):
            S.dma("pool", S.new_slot(sw=True), wgu[:, kc, :], gv[:, kc, :], writes=[b_wgu[kc]])
        S.dma("pool", S.new_slot(sw=True), wdn[:], w_dn_ap.rearrange("(j p) n -> p j n", p=128), writes=[b_wdn])
        V = K.vec
        if last:
            osb = [_sb(K, ph, "t2os%d" % i, [128, D], F32) for i in range(2)]
            b_osb = [Buf(), Buf()]
            s_os = [S.new_slot(), S.new_slot()]
        else:
            N = {"sq": _sb(K, ph, "t2nsq", [128, 2, TW2], BF16), "tmp": _sb(K, ph, "t2ntmp", [128, 2, TW2], F32),
                 "rstd": _sb(K, ph, "t2nrstd", [128, TW2], F32), "ss": pacc[0], "b_ss": b_pacc[0],
                 "b_sqk": [Buf(), Buf()], "b_rstd": Buf(), "b_tmpk": [Buf(), Buf()]}
            hn = _sb(K, ph, "t2hn", [128, 4, TW2], BF16)
            b_hn = Buf()

        def load_h2(t):
            S.dma("sp", slh[t % 2], h2[t % 2][:], K.h2T_d[:, :, t * TW2:(t + 1) * TW2],
                  reads=[K.b_h2T[2 * t], K.b_h2T[2 * t + 1]], writes=[b_h2[t % 2]])

        def epilogue(t):
            tb = [K.b_xT[2 * t], K.b_xT[2 * t + 1]]
            if last:
                for sub in range(TW2 // 128):
                    o_ = osb[sub % 2]
                    bo = b_osb[sub % 2]
                    for hf in range(2):
                        p, bp = pacc[hf], b_pacc[hf]
                        S.group("pe", [lambda e, hf=hf, q=q, p=p, sub=sub: e.matmul(
                            p[:, q * 128:(q + 1) * 128], lhsT=xt[:, hf * 4 + q, sub * 128:(sub + 1) * 128], rhs=K.ident_f,
                            start=True, stop=True) for q in range(4)], reads=[b_xt, K.b_const], writes=[bp])
                        yield
                        if hf == 0:
                            S.op("act", lambda e, p=p, o_=o_: e.copy(out=o_[:, 0:512], in_=p[:]), reads=[bp], writes=[bo])
                        else:
                            S.op("dve", lambda e, p=p, o_=o_: e.tensor_copy(out=o_[:, 512:1024], in_=p[:]), reads=[bp], writes=[bo])
                        yield
                    r0 = t * TW2 + sub * 128
                    S.dma("sp", s_os[sub % 2], K.out["y"][r0:r0 + 128, :], o_[:], reads=[bo], writes=[K.b_ys[sub % 2]])
                    yield
                S.dma("sp", so, K.xT_d[:, :, t * TW2:(t + 1) * TW2], xt[:], reads=[b_xt], writes=tb)
                yield
            else:
                for kc in range(KC):
                    S.op("act", lambda e, kc=kc: e.activation(out=N["sq"][:, kc % 2, :], in_=xt[:, kc, :], func=AF.Square),
                         reads=[b_xt], writes=[N["b_sqk"][kc % 2]])
                    yield
                    S.op("pe", lambda e, kc=kc: e.matmul(N["ss"][:], lhsT=K.ones_bf, rhs=N["sq"][:, kc % 2, :],
                                                        start=(kc == 0), stop=(kc == KC - 1)),
                         reads=[N["b_sqk"][kc % 2], K.b_const], writes=[N["b_ss"]])
                    yield
                S.op("act", lambda e: e.activation(out=N["rstd"][:], in_=N["ss"][:], func=AF.Ln, bias=K.eps1024[:, 0:1], scale=1.0),
                     reads=[N["b_ss"], K.b_const], writes=[N["b_rstd"]])
                yield
                S.op("act", lambda e: e.activation(out=N["rstd"][:], in_=N["rstd"][:], func=AF.Exp, scale=-0.5),
                     reads=[N["b_rstd"]], writes=[N["b_rstd"]])
                yield
                gs, sh = V["gs1"][:, l + 1, :], V["sh1"][:, l + 1, :]
                for kc in range(KC):
                    S.op("dve", lambda e, kc=kc: e.tensor_tensor(out=N["tmp"][:, kc % 2, :], in0=xt[:, kc, :], in1=N["rstd"][:], op=ALU.mult),
                         reads=[b_xt, N["b_rstd"]], writes=[N["b_tmpk"][kc % 2]])
                    yield
                    S.op("act", lambda e, kc=kc: e.activation(out=hn[:, kc % 4, :], in_=N["tmp"][:, kc % 2, :], func=AF.Identity,
                                                             bias=sh[:, kc:kc + 1], scale=gs[:, kc:kc + 1]),
                         reads=[N["b_tmpk"][kc % 2], K.b_vec], writes=[b_hn])
                    yield
                    if kc % 4 == 3:
                        k0 = kc - 3
                        S.dma("sp", so2, K.hT_d[:, k0:k0 + 4, t * TW2:(t + 1) * TW2], hn[:], reads=[b_hn],
                              writes=[K.b_hT[2 * t], K.b_hT[2 * t + 1]])
                S.dma("sp", so, K.xT_d[:, :, t * TW2:(t + 1) * TW2], xt[:], reads=[b_xt], writes=tb)
                yield

        load_h2(0)
        pend = None
        for t in range(NT2):
            i = t % 2
            if t + 1 < NT2:
                load_h2(t + 1)
            for j in range(NJ):
                g_, u_ = pg[2 * (j % 2)], pg[2 * (j % 2) + 1]
                bg_, bu_ = b_pg[2 * (j % 2)], b_pg[2 * (j % 2) + 1]
                S.group("pe", [lambda e, kc=kc, j=j, g_=g_, i=i: e.matmul(
                    g_[:], lhsT=wgu[:, kc, j * 128:(j + 1) * 128], rhs=h2[i][:, kc, :],
                    start=(kc == 0), stop=(kc == KC - 1)) for kc in range(KC)], reads=b_wgu + [b_h2[i]], writes=[bg_])
                S.group("pe", [lambda e, kc=kc, j=j, u_=u_, i=i: e.matmul(
                    u_[:], lhsT=wgu[:, kc, FH + j * 128:FH + (j + 1) * 128], rhs=h2[i][:, kc, :],
                    start=(kc == 0), stop=(kc == KC - 1)) for kc in range(KC)], reads=b_wgu + [b_h2[i]], writes=[bu_])
                s_ = sg[j % 2]
                S.op("act", lambda e, g_=g_, s_=s_: e.activation(out=s_[:], in_=g_[:], func=AF.Silu), reads=[bg_], writes=[b_sg[j % 2]])
                S.op("dve", lambda e, u_=u_, s_=s_, j=j: e.tensor_tensor(out=act[:, j, :], in0=u_[:], in1=s_[:], op=ALU.mult),
                     reads=[bu_, b_sg[j % 2]], writes=[b_act[j]])
                if pend is not None:
                    for _ in range(2):
                        try:
                            next(pend)
                        except StopIteration:
                            pend = None
                            break
            while pend is not None:
                try:
                    next(pend)
                except StopIteration:
                    pend = None
            S.dma("sp", slx, xt[:], K.xT_d[:, :, t * TW2:(t + 1) * TW2], reads=[K.b_xT[2 * t], K.b_xT[2 * t + 1]], writes=[b_xt])
            for half in range(2):
                banks = pacc if half == 0 else pg
                bbanks = b_pacc if half == 0 else b_pg
                for mm in range(4):
                    m = half * 4 + mm
                    S.group("pe", [lambda e, j=j, m=m, mm=mm, banks=banks: e.matmul(
                        banks[mm][:], lhsT=wdn[:, j, m * 128:(m + 1) * 128], rhs=act[:, j, :],
                        start=(j == 0), stop=(j == NJ - 1)) for j in range(NJ)], reads=[b_wdn] + b_act, writes=[bbanks[mm]])
                    S.op("dve", lambda e, m=m, mm=mm, banks=banks: e.scalar_tensor_tensor(
                        out=xt[:, m, :], in0=banks[mm][:], scalar=V["g2"][:, l, m:m + 1], in1=xt[:, m, :],
                        op0=ALU.mult, op1=ALU.add), reads=[bbanks[mm], b_xt, K.b_vec], writes=[b_xt])
            pend = epilogue(t)
        while pend is not None:
            try:
                next(pend)
            except StopIteration:
                pend = None
        S.barrier()


MIXERS = {}

def phase_gdn(K, l, j):
    nc, S = K.nc, K.S
    NB = T // 128
    wv = K.inp["gdn_w_in"][j].rearrange("(kc p) n -> p kc n", p=128)
    with contextlib.ExitStack() as gph:
        gsb = lambda n, shp, dt: _sb(K, gph, n, shp, dt)
        K.cst2 = gsb("cst2_sb", [128, 6, 2, 128], BF16)
        S.dma("pool", S.new_slot(sw=True), K.cst2[:], K.inp["cst2"], writes=[K.b_const])
        beta = gsb("g_beta", [128, NB, 8], F32)
        gc = gsb("g_gc", [128, NB, 8], F32)
        egc = gsb("g_egc", [128, NB, 8], F32)
        etail = gsb("g_etail", [128, NB, 8], F32)
        glast = gsb("g_glast", [128, NB, 8], F32)
        bgt = gsb("g_bg", [128, NB, 8], F32)
        cw = gsb("g_cw", [128, 2, 24, 4], F32)
        hc = gsb("g_hc", [128, 2, 2, 8], F32)
        ng = gsb("g_ng", [128, 2], F32)
        b_g = Buf()
        b_cst = Buf()
        scst = S.new_slot()
        S.dma("sp", scst, cw[:], K.inp["gdn_conv"], writes=[b_cst])
        S.dma("sp", scst, hc[:], K.inp["gdn_hc"], writes=[b_cst])
        S.dma("sp", scst, ng[:], K.inp["gdn_ng"], writes=[b_cst])
        with contextlib.ExitStack() as ph:
            sb = lambda n, shp, dt: _sb(K, ph, n, shp, dt)
            hT = sb("g_hT", [128, KC, T], BF16)
            b_hT = [Buf() for _ in range(KC)]
            for kc in range(KC):
                S.dma("sp", S.new_slot(), hT[:, kc, :], K.hT_d[:, kc, :], reads=K.b_hT, writes=[b_hT[kc]])
            pf = [_ps(K, ph, "g_pf%d" % i, [128, 512], F32) for i in range(4)]
            b_pf = [PB() for _ in range(4)]
            wba = sb("g_wba", [128, KC, 16], BF16)
            b_wba = Buf()
            S.dma("pool", S.new_slot(sw=True), wba[:], wv[:, :, 4096:4112], writes=[b_wba])
            ba = sb("g_ba", [128, NB, 16], F32)
            gg = sb("g_g", [128, NB, 8], F32)
            negA = sb("g_negA", [128, 8], F32)
            for blk in range(NB):
                S.group("pe", [lambda e, kc=kc, blk=blk: e.matmul(
                    pf[0][:, blk * 16:(blk + 1) * 16], lhsT=hT[:, kc, blk * 128:(blk + 1) * 128], rhs=wba[:, kc, :],
                    start=(kc == 0), stop=(kc == KC - 1)) for kc in range(KC)], reads=b_hT + [b_wba], writes=[b_pf[0]])
            S.op("act", lambda e: e.copy(out=ba[:].rearrange("p a b -> p (a b)"), in_=pf[0][:]), reads=[b_pf[0]], writes=[b_g])
            S.op("act", lambda e: e.activation(out=beta[:], in_=ba[:, :, 0:8], func=AF.Sigmoid), reads=[b_g], writes=[b_g])
            S.op("act", lambda e: e.activation(out=negA[:], in_=hc[:, j, 0, :], func=AF.Exp), reads=[b_cst], writes=[b_g])
            S.op("dve", lambda e: e.tensor_scalar(out=negA[:], in0=negA[:], scalar1=-1.0, scalar2=None, op0=ALU.mult),
                 reads=[b_g], writes=[b_g])
            S.op("dve", lambda e: e.tensor_tensor(out=gg[:], in0=ba[:, :, 8:16],
                                                 in1=hc[:, j, 1, :].unsqueeze(1).to_broadcast([128, NB, 8]), op=ALU.add),
                 reads=[b_g, b_cst], writes=[b_g])
            S.op("act", lambda e: e.activation(out=gg[:], in_=gg[:], func=AF.Exp), reads=[b_g], writes=[b_g])
            S.op("act", lambda e: e.activation(out=gg[:], in_=gg[:], func=AF.Ln, bias=K.eps1024[:, 2:3], scale=1.0),
                 reads=[b_g, K.b_const], writes=[b_g])
            S.op("dve", lambda e: e.tensor_tensor(out=gg[:], in0=gg[:], in1=negA[:].unsqueeze(1).to_broadcast([128, NB, 8]),
                                                 op=ALU.mult), reads=[b_g], writes=[b_g])
            ggf = gg[:].rearrange("p a b -> p (a b)")
            S.op("pe", lambda e: e.matmul(pf[1][:, 0:256], lhsT=K.triU_f, rhs=ggf, start=True, stop=True),
                 reads=[b_g, K.b_const], writes=[b_pf[1]])
            S.op("pe", lambda e: e.matmul(pf[2][:, 0:256], lhsT=K.ones_f, rhs=ggf, start=True, stop=True),
                 reads=[b_g, K.b_const], writes=[b_pf[2]])
            fl = lambda t: t[:].rearrange("p a b -> p (a b)")
            S.op("act", lambda e: e.copy(out=fl(gc), in_=pf[1][:, 0:256]), reads=[b_pf[1]], writes=[b_g])
            S.op("act", lambda e: e.activation(out=fl(egc), in_=pf[1][:, 0:256], func=AF.Exp), reads=[b_pf[1]], writes=[b_g])
            S.op("act", lambda e: e.activation(out=fl(glast), in_=pf[2][:, 0:256], func=AF.Exp), reads=[b_pf[2]], writes=[b_g])
            S.op("dve", lambda e: e.tensor_tensor(out=fl(etail), in0=pf[2][:, 0:256], in1=fl(gc), op=ALU.subtract),
                 reads=[b_pf[2], b_g], writes=[b_g])
            S.op("act", lambda e: e.activation(out=fl(etail), in_=fl(etail), func=AF.Exp), reads=[b_g], writes=[b_g])
            S.op("dve", lambda e: e.tensor_tensor(out=fl(bgt), in0=fl(beta), in1=fl(egc), op=ALU.mult), reads=[b_g], writes=[b_g])
            NW = 4
            w1 = [sb("g_w1%d" % i, [128, KC, 128], BF16) for i in range(NW)]
            b_w1 = [Buf() for _ in range(NW)]
            s_w1 = [S.new_slot(sw=True) for _ in range(NW)]
            stg = [sb("g_stg%d" % i, [128, 3 + T], F32) for i in range(2)]
            b_stg = [[Buf() for _ in range(9)] for _ in range(2)]
            for i in range(2):
                S.op("pool", lambda e, i=i: e.memset(stg[i][:, 0:3], 0.0), writes=[b_stg[i][0]])

            def load_w1(ci):
                S.dma("pool", s_w1[ci % NW], w1[ci % NW][:], wv[:, :, ci * 128:(ci + 1) * 128], writes=[b_w1[ci % NW]])

            for ci in range(NW - 1):
                load_w1(ci)
            NLA = 4
            LAa = [{"cacc": sb("g_caccL%d" % i, [128, 512], F32), "b_cacc": Buf(),
                    "fo": sb("g_foL%d" % i, [128, 512], BF16), "b_fo": Buf(), "s_fo": S.new_slot()} for i in range(NLA)]

            def ajob(li, ci, tt):
                d = LAa[li]
                if tt == 0 and ci + NW - 1 < 32:
                    load_w1(ci + NW - 1)
                W = w1[ci % NW]
                bW = b_w1[ci % NW]
                sg_ = stg[ci % 2]
                bsg = b_stg[ci % 2]
                pa, bpa = pf[li], b_pf[li]
                S.group("pe", [lambda e, kc=kc: e.matmul(
                    pa[:], lhsT=W[:, kc, :], rhs=hT[:, kc, tt * 512:(tt + 1) * 512],
                    start=(kc == 0), stop=(kc == KC - 1)) for kc in range(KC)], reads=b_hT + [bW], writes=[bpa])
                yield
                o = tt * 512
                f_, bf_ = d["fo"], d["b_fo"]
                if ci >= 24:
                    S.op("act", lambda e: e.activation(out=f_[:], in_=pa[:], func=AF.Silu), reads=[bpa], writes=[bf_])
                    yield
                else:
                    S.op("act", lambda e: e.copy(out=sg_[:, 3 + o:3 + o + 512], in_=pa[:]), reads=[bpa], writes=[bsg[1 + tt]])
                    yield
                    ca, bca = d["cacc"], d["b_cacc"]
                    rd = [bsg[tt], bsg[1 + tt], b_cst]
                    S.op("dve", lambda e: e.tensor_scalar(
                        out=ca[:], in0=sg_[:, 3 + o:3 + o + 512], scalar1=cw[:, j, ci, 3:4], scalar2=None, op0=ALU.mult),
                        reads=rd, writes=[bca])
                    yield
                    for k in (2, 1, 0):
                        S.op("dve", lambda e, k=k: e.scalar_tensor_tensor(
                            out=ca[:], in0=sg_[:, k + o:k + o + 512], scalar=cw[:, j, ci, k:k + 1], in1=ca[:],
                            op0=ALU.mult, op1=ALU.add), reads=rd + [bca], writes=[bca])
                        yield
                    S.op("act", lambda e: e.activation(out=f_[:], in_=ca[:], func=AF.Silu), reads=[bca], writes=[bf_])
                    yield
                S.dma("sp", d["s_fo"], K.F_d[ci, :, o:o + 512], f_[:], reads=[bf_], writes=[])
                yield

            run_rolling([(lambda li, ci=ci, tt=tt: ajob(li, ci, tt)) for ci in range(32) for tt in range(8)], NLA, stagger=2)
            S.barrier()
        with contextlib.ExitStack() as ph:
            sb = lambda n, shp, dt: _sb(K, ph, n, shp, dt)
            NL = 8
            junk_sh = sb("g_junk_sh", [128, 128], F32)
            pl = [_ps(K, ph, "g_pl%d" % i, [128, 512], F32) for i in range(NL)]
            L = []
            for h in range(NL):
                d = {}
                d["pb"] = PB()
                d["ps"] = pl[h]
                d["F"] = [[sb("g_F%d_%d_%d" % (h, c, i), [128, 512], BF16) for i in range(2)] for c in range(4)]
                d["bF"] = [[Buf() for i in range(2)] for c in range(4)]
                d["sF"] = [[S.new_slot() for i in range(2)] for c in range(4)]
                d["oT"] = [sb("g_oT%d_%d" % (h, i), [128, 512], BF16) for i in range(1)] * 2
                d["boT"] = [Buf()] * 2
                d["soT"] = [S.new_slot()] * 2
                d["junk"] = junk_sh
                d["b_junk"] = None
                for nm, shp, dt in (("tm", [128, 3, 128], BF16), ("sc", [128, 16], F32),
                                    ("var", [128, 7, 128], BF16), ("fmT", [128, 4, 128], BF16), ("diagG", [128, 128], F32),
                                    ("dTs", [128, 128], F32), ("dTf", [128, 128], F32),
                                    ("attnT", [128, 128], BF16), ("N0", [128, 2, 128], BF16), ("N1", [128, 2, 128], BF16),
                                    ("N2", [128, 2, 128], BF16), ("X0", [128, 2, 128], BF16), ("X1", [128, 2, 128], BF16),
                                    ("MA", [128, 5, 2, 128], BF16), ("PP", [128, 2, 128], BF16), ("u", [128, 128], F32),
                                    ("wT", [128, 128], BF16), ("vnew", [128, 128], BF16), ("S", [128, 128], F32),
                                    ("Sbf", [128, 128], BF16), ("os", [128, 128], BF16)):
                    d[nm] = sb("g_%s%d" % (nm, h), shp, dt)
                    d["b_" + nm] = Buf()
                L.append(d)

            def load_F(h, tt):
                d = L[h]
                i = tt % 2
                for c in range(4):
                    S.dma("sp", d["sF"][c][i], d["F"][c][i][:], K.F_d[c * 8 + h, :, tt * 512:(tt + 1) * 512],
                          reads=[K.b_F], writes=[d["bF"][c][i]])

            pv2 = lambda t: t[:].rearrange("p a b -> p (a b)")

            def chunk(h, n):
                d = L[h]
                ps, pb = d["ps"], d["pb"]
                tt = n // 4
                fi = tt % 2
                o4 = (n % 4) * 128
                Fq, Fk, Fv, Fz = [d["F"][c][fi] for c in range(4)]
                bFq, bFk, bFv, bFz = [d["bF"][c][fi] for c in range(4)]
                gcol = lambda t: t[:, n, h:h + 1]
                tm, sc, var, fmT, junk = d["tm"], d["sc"], d["var"], d["fmT"], d["junk"]
                b_tm, b_sc, b_var, b_fmT, b_junk = d["b_tm"], d["b_sc"], d["b_var"], d["b_fmT"], d["b_junk"]
                S.group("pe", [lambda e, c=c, Fc=Fc: e.matmul(ps[:, c * 128:(c + 1) * 128], lhsT=Fc[:, o4:o4 + 128], rhs=K.ident_bf,
                                                            start=True, stop=True) for c, Fc in enumerate((Fq, Fk, Fv))],
                        reads=[bFq, bFk, bFv, K.b_const], writes=[pb])
                yield
                S.op("act", lambda e: e.copy(out=tm[:].rearrange("p a b -> p (a b)"), in_=ps[:, 0:384]), reads=[pb], writes=[b_tm])
                yield
                S.op("act", lambda e: e.activation(out=junk[:], in_=tm[:, 0, :], func=AF.Square, scale=128.0 ** 0.5, accum_out=sc[:, 0:1]),
                     reads=[b_tm], writes=[b_sc])
                S.op("act", lambda e: e.activation(out=junk[:], in_=tm[:, 1, :], func=AF.Square, accum_out=sc[:, 1:2]),
                     reads=[b_tm], writes=[b_sc])
                yield
                S.op("act", lambda e: e.activation(out=sc[:, 2:4], in_=sc[:, 0:2], func=AF.Ln, bias=K.eps1024[:, 1:2], scale=1.0),
                     reads=[b_sc, K.b_const], writes=[b_sc])
                S.op("act", lambda e: e.activation(out=sc[:, 2:4], in_=sc[:, 2:4], func=AF.Exp, scale=-0.5),
                     reads=[b_sc], writes=[b_sc])
                yield
                rq, rk = sc[:, 2:3], sc[:, 3:4]
                zb = K.eps1024[:, 3:4]
                S.op("act", lambda e: e.activation(out=var[:, 0, :], in_=tm[:, 0, :], func=AF.Identity, bias=zb, scale=rq),
                     reads=[b_tm, b_sc, K.b_const], writes=[b_var])
                S.op("dve", lambda e: e.tensor_scalar(out=var[:, 1, :], in0=tm[:, 0, :], scalar1=rq, scalar2=gcol(egc), op0=ALU.mult, op1=ALU.mult),
                     reads=[b_tm, b_sc, b_g], writes=[b_var])
                yield
                S.op("act", lambda e: e.activation(out=var[:, 2, :], in_=tm[:, 1, :], func=AF.Identity, bias=zb, scale=rk),
                     reads=[b_tm, b_sc, K.b_const], writes=[b_var])
                S.op("dve", lambda e: e.tensor_scalar(out=var[:, 3, :], in0=tm[:, 1, :], scalar1=rk, scalar2=gcol(beta), op0=ALU.mult, op1=ALU.mult),
                     reads=[b_tm, b_sc, b_g], writes=[b_var])
                yield
                S.op("act", lambda e: e.activation(out=var[:, 6, :], in_=tm[:, 2, :], func=AF.Identity, bias=zb, scale=gcol(beta)),
                     reads=[b_tm, b_g, K.b_const], writes=[b_var])
                S.op("dve", lambda e: e.tensor_scalar(out=var[:, 4, :], in0=tm[:, 1, :], scalar1=rk, scalar2=gcol(bgt), op0=ALU.mult, op1=ALU.mult),
                     reads=[b_tm, b_sc, b_g], writes=[b_var])
                yield
                S.op("dve", lambda e: e.tensor_scalar(out=var[:, 5, :], in0=tm[:, 1, :], scalar1=rk, scalar2=gcol(etail), op0=ALU.mult, op1=ALU.mult),
                     reads=[b_tm, b_sc, b_g], writes=[b_var])
                yield
                S.group("pe", [lambda e, vi=vi: e.matmul(ps[:, vi * 128:(vi + 1) * 128], lhsT=var[:, vi, :], rhs=K.ident_bf,
                                                        start=True, stop=True) for vi in range(4)],
                        reads=[b_var, K.b_const], writes=[pb])
                yield
                S.op("dve", lambda e: e.tensor_copy(out=fmT[:].rearrange("p a b -> p (a b)"), in_=ps[:, 0:512]),
                     reads=[pb], writes=[b_fmT])
                qhT, qdT, khT, kbT = fmT[:, 0, :], fmT[:, 1, :], fmT[:, 2, :], fmT[:, 3, :]
                S.op("act", lambda e: e.activation(out=d["diagG"][:], in_=K.ident_f, func=AF.Identity, bias=K.eps1024[:, 3:4], scale=gcol(gc)),
                     reads=[K.b_const, b_g], writes=[d["b_diagG"]])
                yield
                S.op("pe", lambda e: e.matmul(ps[:, 0:128], lhsT=K.ones_f, rhs=d["diagG"][:], start=True, stop=True),
                     reads=[d["b_diagG"], K.b_const], writes=[pb])
                yield
                S.op("dve", lambda e: e.scalar_tensor_tensor(out=d["dTs"][:], in0=ps[:, 0:128], scalar=gcol(gc), in1=K.cst_f[:, 5, :],
                                                            op0=ALU.subtract, op1=ALU.min), reads=[pb, b_g, K.b_const], writes=[d["b_dTs"]])
                yield
                S.op("act", lambda e: e.activation(out=d["dTs"][:], in_=d["dTs"][:], func=AF.Exp), reads=[d["b_dTs"]], writes=[d["b_dTs"]])
                S.group("pe", [lambda e: e.matmul(ps[:, 128:256], lhsT=khT, rhs=kbT, start=True, stop=True),
                               lambda e: e.matmul(ps[:, 256:384], lhsT=khT, rhs=qhT, start=True, stop=True)],
                        reads=[b_fmT], writes=[pb])
                yield
                S.op("pool", lambda e: e.tensor_tensor(out=d["dTf"][:], in0=d["dTs"][:], in1=K.ident_f, op=ALU.add),
                     reads=[d["b_dTs"], K.b_const], writes=[d["b_dTf"]])
                NP = d["N0"]
                S.op("dve", lambda e: e.tensor_tensor(out=NP[:, 1, :], in0=ps[:, 128:256], in1=d["dTs"][:], op=ALU.mult),
                     reads=[pb, d["b_dTs"]], writes=[d["b_N0"]])
                yield
                S.op("dve", lambda e: e.tensor_tensor(out=d["attnT"][:], in0=ps[:, 256:384], in1=d["dTf"][:], op=ALU.mult),
                     reads=[pb, d["b_dTf"]], writes=[d["b_attnT"]])
                yield
                S.op("pe", lambda e: e.matmul(ps[:, 384:512], lhsT=NP[:, 1, :], rhs=K.ident_bf, start=True, stop=True),
                     reads=[d["b_N0"], K.b_const], writes=[pb])
                yield
                S.op("act", lambda e: e.copy(out=NP[:, 0, :], in_=ps[:, 384:512]), reads=[pb], writes=[d["b_N0"]])
                yield
                Nn = [d["N0"], d["N1"], d["N2"]]
                b_N = [d["b_N0"], d["b_N1"], d["b_N2"]]
                XX = [d["X0"], d["X1"]]
                b_XX = [d["b_X0"], d["b_X1"]]
                PP = d["PP"]
                MA = d["MA"]
                S.op("pool", lambda e: e.tensor_tensor(out=MA[:], in0=NP[:].unsqueeze(1).to_broadcast([128, 5, 2, 128]),
                                                      in1=K.cst2[:, 0:5, :, :], op=ALU.mult),
                     reads=[b_N[0], K.b_const], writes=[d["b_MA"]])
                Nn[1] = MA[:, 0, :, :]
                b_N[1] = d["b_MA"]
                yield
                S.op("pool", lambda e: e.tensor_tensor(out=XX[0][:], in0=K.cst2[:, 5, :, :], in1=MA[:, 0, :, :], op=ALU.subtract),
                     reads=[K.b_const, b_N[1]], writes=[b_XX[0]])
                yield
                cx = 0
                for lv, (a, ba_, nx, bnx) in enumerate(((MA[:, 0, :, :], d["b_MA"], d["N2"], d["b_N2"]),
                                                        (d["N2"], d["b_N2"], d["N1"], d["b_N1"]))):
                    S.group("pe", [lambda e, a=a: e.matmul(ps[:, 0:128], lhsT=a[:, 1, :], rhs=a[:, 0, :], start=True, stop=True),
                                   lambda e, a=a: e.matmul(ps[:, 128:256], lhsT=a[:, 0, :], rhs=a[:, 1, :], start=True, stop=True)],
                            reads=[ba_], writes=[pb])
                    yield
                    S.op("act", lambda e, nx=nx: e.copy(out=pv2(nx), in_=ps[:, 0:256]), reads=[pb], writes=[bnx])
                    yield
                    xs, xd = XX[cx], XX[1 - cx]
                    S.group("pe", [lambda e, nx=nx, xs=xs: e.matmul(ps[:, 256:384], lhsT=nx[:, 1, :], rhs=xs[:, 0, :], start=True, stop=True),
                                   lambda e, nx=nx, xs=xs: e.matmul(ps[:, 384:512], lhsT=nx[:, 0, :], rhs=xs[:, 1, :], start=True, stop=True)],
                            reads=[bnx, b_XX[cx]], writes=[pb])
                    yield
                    S.op("dve", lambda e, xs=xs, xd=xd: e.tensor_tensor(out=pv2(xd), in0=ps[:, 256:512], in1=pv2(xs), op=ALU.add),
                         reads=[pb, b_XX[cx]], writes=[b_XX[1 - cx]])
                    yield
                    cx = 1 - cx
                for li in range(4):
                    xs, xd = XX[cx], XX[1 - cx]
                    S.group("pe", [lambda e, xs=xs, li=li: e.matmul(ps[:, 0:128], lhsT=MA[:, 1 + li, 0, :], rhs=xs[:, 1, :], start=True, stop=True),
                                   lambda e, xs=xs, li=li: e.matmul(ps[:, 128:256], lhsT=MA[:, 1 + li, 1, :], rhs=xs[:, 0, :], start=True, stop=True)],
                            reads=[d["b_MA"], b_XX[cx]], writes=[pb])
                    yield
                    S.op("act", lambda e: e.copy(out=pv2(PP), in_=ps[:, 0:256]), reads=[pb], writes=[d["b_PP"]])
                    yield
                    S.group("pe", [lambda e, xs=xs: e.matmul(ps[:, 256:384], lhsT=xs[:, 1, :], rhs=PP[:, 1, :], start=True, stop=True),
                                   lambda e, xs=xs: e.matmul(ps[:, 384:512], lhsT=xs[:, 0, :], rhs=PP[:, 0, :], start=True, stop=True)],
                            reads=[d["b_PP"], b_XX[cx]], writes=[pb])
                    yield
                    S.op("dve", lambda e, xs=xs, xd=xd: e.tensor_tensor(out=pv2(xd), in0=pv2(xs), in1=ps[:, 256:512], op=ALU.subtract),
                         reads=[pb, b_XX[cx]], writes=[b_XX[1 - cx]])
                    yield
                    cx = 1 - cx
                XTf = XX[cx][:, 1, :]
                bXTf = b_XX[cx]
                S.group("pe", [lambda e: e.matmul(ps[:, 0:128], lhsT=XTf, rhs=var[:, 6, :], start=True, stop=True),
                               lambda e: e.matmul(ps[:, 128:256], lhsT=var[:, 4, :], rhs=XTf, start=True, stop=True)],
                        reads=[bXTf, b_var], writes=[pb])
                yield
                wT, vnew, Sst, Sbf, u_sb = d["wT"], d["vnew"], d["S"], d["Sbf"], d["u"]
                S.op("act", lambda e: e.copy(out=wT[:], in_=ps[:, 128:256]), reads=[pb], writes=[d["b_wT"]])
                if n == 0:
                    S.op("dve", lambda e: e.tensor_copy(out=vnew[:], in_=ps[:, 0:128]), reads=[pb], writes=[d["b_vnew"]])
                    yield
                    S.group("pe", [lambda e: e.matmul(ps[:, 128:256], lhsT=d["attnT"][:], rhs=vnew[:], start=True, stop=True),
                                   lambda e: e.matmul(ps[:, 256:384], lhsT=var[:, 5, :], rhs=vnew[:], start=True, stop=True)],
                            reads=[d["b_attnT"], d["b_vnew"], b_var], writes=[pb])
                    yield
                    S.op("dve", lambda e: e.tensor_copy(out=Sst[:], in_=ps[:, 256:384]), reads=[pb], writes=[d["b_S"]])
                else:
                    S.op("dve", lambda e: e.tensor_copy(out=u_sb[:], in_=ps[:, 0:128]), reads=[pb], writes=[d["b_u"]])
                    yield
                    S.op("pe", lambda e: e.matmul(ps[:, 0:128], lhsT=wT[:], rhs=Sbf[:], start=True, stop=True),
                         reads=[d["b_wT"], d["b_S"]], writes=[pb])
                    yield
                    S.op("dve", lambda e: e.tensor_tensor(out=vnew[:], in0=u_sb[:], in1=ps[:, 0:128], op=ALU.subtract),
                         reads=[d["b_u"], pb], writes=[d["b_vnew"]])
                    yield
                    S.group("pe", [lambda e: e.matmul(ps[:, 128:256], lhsT=qdT, rhs=Sbf[:], start=True, stop=False),
                                   lambda e: e.matmul(ps[:, 128:256], lhsT=d["attnT"][:], rhs=vnew[:], start=False, stop=True),
                                   lambda e: e.matmul(ps[:, 256:384], lhsT=var[:, 5, :], rhs=vnew[:], start=True, stop=True)],
                            reads=[b_fmT, d["b_S"], d["b_attnT"], d["b_vnew"], b_var], writes=[pb])
                    yield
                    S.op("dve", lambda e: e.scalar_tensor_tensor(out=Sst[:], in0=Sst[:], scalar=gcol(glast), in1=ps[:, 256:384],
                                                                op0=ALU.mult, op1=ALU.add), reads=[d["b_S"], b_g, pb], writes=[d["b_S"]])
                yield
                if n < NB - 1:
                    S.op("pool", lambda e: e.tensor_copy(out=Sbf[:], in_=Sst[:]), reads=[d["b_S"]], writes=[d["b_S"]])
                S.op("act", lambda e: e.activation(out=junk[:], in_=ps[:, 128:256], func=AF.Square, accum_out=sc[:, 9:10]),
                     reads=[pb], writes=[b_sc])
                yield
                S.op("act", lambda e: e.activation(out=sc[:, 10:11], in_=sc[:, 9:10], func=AF.Ln, bias=K.eps1024[:, 1:2], scale=1.0 / 128.0),
                     reads=[b_sc, K.b_const], writes=[b_sc])
                S.op("act", lambda e: e.activation(out=sc[:, 10:11], in_=sc[:, 10:11], func=AF.Exp, scale=-0.5),
                     reads=[b_sc], writes=[b_sc])
                yield
                S.op("dve", lambda e: e.tensor_scalar(out=d["os"][:], in0=ps[:, 128:256], scalar1=sc[:, 10:11], scalar2=None, op0=ALU.mult),
                     reads=[pb, b_sc], writes=[d["b_os"]])
                yield
                S.op("pe", lambda e: e.matmul(ps[:, 384:512], lhsT=d["os"][:], rhs=K.ident_bf, start=True, stop=True),
                     reads=[d["b_os"], K.b_const], writes=[pb])
                yield
                oi = tt % 2
                S.op("dve", lambda e: e.scalar_tensor_tensor(out=d["oT"][oi][:, o4:o4 + 128], in0=ps[:, 384:512], scalar=ng[:, j:j + 1],
                                                            in1=Fz[:, o4:o4 + 128], op0=ALU.mult, op1=ALU.mult),
                     reads=[pb, b_cst, bFz], writes=[d["boT"][oi]])
                if n % 4 == 3:
                    S.dma("sp", d["soT"][oi], K.oT_d[:, h, tt * 512:(tt + 1) * 512], d["oT"][oi][:], reads=[d["boT"][oi]],
                          writes=[K.b_oT[2 * tt], K.b_oT[2 * tt + 1]])
                yield

            def head_chain(h):
                load_F(h, 0)
                for n in range(NB):
                    if n % 4 == 0 and n // 4 + 1 < 8:
                        load_F(h, n // 4 + 1)
                    yield from chunk(h, n)

            run_rolling([(lambda li: head_chain(li)) for _ in range(NL)], NL, stagger=5)
            S.barrier()


MIXERS["gdn"] = phase_gdn

def run_lanes(gens):
    live = list(gens)
    while live:
        nxt = []
        for g_ in live:
            try:
                next(g_)
                nxt.append(g_)
            except StopIteration:
                pass
        live = nxt


def run_rolling(jobs, nl, stagger=1):
    jobs = list(jobs)
    active = {}
    nxt = 0
    step = 0
    while nxt < len(jobs) or active:
        for li in range(nl):
            if li not in active and nxt < len(jobs) and step >= li * stagger:
                active[li] = jobs[nxt](li)
                nxt += 1
        for li in list(active.keys()):
            try:
                next(active[li])
            except StopIteration:
                del active[li]
        step += 1


def phase_dsw(K, l, j):
    nc, S = K.nc, K.S
    NB = 32
    SC = 128.0 ** -0.5
    NL = 7
    with contextlib.ExitStack() as ph:
        sb = lambda n, shp, dt: _sb(K, ph, n, shp, dt)
        hT = sb("d_hT", [128, KC, T], BF16)
        b_hT = [Buf() for _ in range(KC)]
        for kc in range(KC):
            S.dma("sp", S.new_slot(), hT[:, kc, :], K.hT_d[:, kc, :], reads=K.b_hT, writes=[b_hT[kc]])
        gain = sb("d_gain", [128, 6, 128], F32)
        rope = sb("d_rope", [128, 2, 32, 16], F32)
        b_rope = Buf()
        s_rope = S.new_slot()
        mask2 = sb("d_mask2", [128, 2, 256], BF16)
        b_cst = Buf()
        scst = S.new_slot()
        S.dma("sp", scst, gain[:], K.inp["dsw_gain"][:, j, :, :], writes=[b_cst])
        for hd in range(2):
            S.op("pool", lambda e, hd=hd: e.tensor_copy(out=mask2[:, hd, 0:128], in_=K.cst_f[:, 2, :]), reads=[K.b_const], writes=[b_cst])
            S.op("pool", lambda e, hd=hd: e.tensor_copy(out=mask2[:, hd, 128:256], in_=K.cst_f[:, 4, :]), reads=[K.b_const], writes=[b_cst])
        pA = [_ps(K, ph, "d_pA%d" % i, [128, 512], F32) for i in range(NL)]
        b_pA = [PB() for _ in range(NL)]
        pB = [pA[i][:].rearrange("p (a c) -> p a c", c=256) for i in range(NL)]
        b_pB = b_pA
        wsl = [sb("d_wsl%d" % i, [128, KC, 512], BF16) for i in range(2)]
        b_wsl = [Buf(), Buf()]
        s_wsl = [S.new_slot(sw=True), S.new_slot(sw=True)]
        qT = sb("d_qT", [128, 4, NB, 128], BF16)
        kT = sb("d_kT", [128, 4, NB, 128], BF16)
        b_qT = [Buf() for _ in range(NB)]
        b_kT = [Buf() for _ in range(NB)]
        Va = sb("d_Va", [128, NB, 4, 130], BF16)
        b_Va = [Buf() for _ in range(NB)]
        S.op("pool", lambda e: e.memset(Va[:].rearrange("p a b c -> p (a b c)"), 1.0), writes=b_Va)
        junk = sb("d_junk", [128, 128], F32)
        LA = []
        for i in range(NL):
            ypt = sb("d_ypt%d" % i, [128, 512], BF16)
            rst = sb("d_rst%d" % i, [128, 264], F32)
            b_ypt, b_rst = Buf(), Buf()
            d = {"ss": sb("d_ss%d" % i, [128, 8], F32), "b_ss": Buf(),
                 "y": ypt[:].rearrange("p (a c) -> p a c", c=128), "b_y": b_ypt,
                 "rt": rst[:, 0:256].rearrange("p (t a c) -> p t a c", t=2, a=4), "b_rt": b_rst,
                 "pT": ypt[:].rearrange("p (a c) -> p a c", c=256), "b_pT": b_ypt,
                 "st": rst[:].rearrange("p (a c) -> p a c", c=132), "b_st": b_rst, "s_st": S.new_slot()}
            LA.append(d)
        wv = K.inp["dsw_w_in"][j].rearrange("(kc p) n -> p kc n", p=128)
        slabs = [(g, hh, t3) for g in range(3) for hh in range(2) for t3 in range(3)]

        def load_slab(si):
            g, hh, t3 = slabs[si]
            col0 = ((g * 3 + t3) * 8 + hh * 4) * 128
            S.dma("pool", s_wsl[si % 2], wsl[si % 2][:], wv[:, :, col0:col0 + 512], writes=[b_wsl[si % 2]])

        def proj_blk(li, si, b):
            g, hh, t3 = slabs[si]
            dl = LA[li]
            dd = DSW_DIL[g]
            nsb = NB // dd
            r, sbk = b // nsb, b % nsb
            start = r + dd * sbk * 128
            W = wsl[si % 2]
            pa, bpa = pA[li], b_pA[li]
            S.group("pe", [lambda e, kc=kc: e.matmul(pa[:], lhsT=hT[:, kc, start:start + 127 * dd + 1:dd], rhs=W[:, kc, :],
                                                    start=(kc == 0), stop=(kc == KC - 1)) for kc in range(KC)],
                    reads=b_hT + [b_wsl[si % 2]], writes=[bpa])
            yield
            if t3 == 2:
                S.op("act", lambda e: e.copy(out=Va[:, b, :, 0:128], in_=pa[:].rearrange("p (a c) -> p a c", c=128)),
                     reads=[bpa], writes=[b_Va[b]])
                return
            ss, y, rt = dl["ss"], dl["y"], dl["rt"]
            for hd in range(4):
                S.op("act", lambda e, hd=hd: e.activation(out=junk[:], in_=pa[:, hd * 128:(hd + 1) * 128], func=AF.Square,
                                                         accum_out=ss[:, hd:hd + 1]), reads=[bpa], writes=[dl["b_ss"]])
                if hd == 1:
                    yield
            yield
            S.op("act", lambda e: e.activation(out=ss[:, 4:8], in_=ss[:, 0:4], func=AF.Ln, bias=K.eps1024[:, 1:2], scale=1.0 / 128.0),
                 reads=[dl["b_ss"], K.b_const], writes=[dl["b_ss"]])
            S.op("act", lambda e: e.activation(out=ss[:, 4:8], in_=ss[:, 4:8], func=AF.Exp, scale=-0.5), reads=[dl["b_ss"]], writes=[dl["b_ss"]])
            yield
            for hd in range(4):
                S.op("dve", lambda e, hd=hd: e.scalar_tensor_tensor(
                    out=y[:, hd, :], in0=pa[:, hd * 128:(hd + 1) * 128], scalar=ss[:, 4 + hd:5 + hd], in1=gain[:, g * 2 + t3, :],
                    op0=ALU.mult, op1=ALU.mult), reads=[bpa, dl["b_ss"], b_cst], writes=[dl["b_y"]])
                if hd == 1:
                    yield
            yield
            cs2 = rope[:, 0, b, :].unsqueeze(1).unsqueeze(1).to_broadcast([128, 4, 2, 16])
            sn2 = rope[:, 1, b, :].unsqueeze(1).unsqueeze(1).to_broadcast([128, 4, 2, 16])
            y32 = y[:, :, 0:32].rearrange("p a (t c) -> p a t c", t=2)
            S.op("dve", lambda e: e.tensor_tensor(out=rt[:, 0, :, :].rearrange("p a (t c) -> p a t c", t=2), in0=y32, in1=cs2, op=ALU.mult),
                 reads=[dl["b_y"], b_rope], writes=[dl["b_rt"]])
            S.op("dve", lambda e: e.tensor_tensor(out=rt[:, 1, :, :].rearrange("p a (t c) -> p a t c", t=2), in0=y32, in1=sn2, op=ALU.mult),
                 reads=[dl["b_y"], b_rope], writes=[dl["b_rt"]])
            yield
            S.op("dve", lambda e: e.tensor_tensor(out=y[:, :, 0:16], in0=rt[:, 0, :, 0:16], in1=rt[:, 1, :, 16:32], op=ALU.subtract),
                 reads=[dl["b_rt"]], writes=[dl["b_y"]])
            S.op("dve", lambda e: e.tensor_tensor(out=y[:, :, 16:32], in0=rt[:, 0, :, 16:32], in1=rt[:, 1, :, 0:16], op=ALU.add),
                 reads=[dl["b_rt"]], writes=[dl["b_y"]])
            yield
            S.group("pe", [lambda e, hd=hd: e.matmul(pa[:, hd * 128:(hd + 1) * 128], lhsT=y[:, hd, :], rhs=K.ident_bf, start=True, stop=True)
                           for hd in range(4)], reads=[dl["b_y"], K.b_const], writes=[bpa])
            yield
            dst = qT if t3 == 0 else kT
            bd = b_qT[b] if t3 == 0 else b_kT[b]
            S.op("act", lambda e: e.copy(out=dst[:, :, b, :], in_=pa[:].rearrange("p (a c) -> p a c", c=128)), reads=[bpa], writes=[bd])
            yield

        def att_unit(li, g, hh, b, pr):
            dl = LA[li]
            dd = DSW_DIL[g]
            nsb = NB // dd
            r, sbk = b // nsb, b % nsb
            has_prev = sbk > 0
            ps_, bps = pB[li], b_pB[li]
            fns = []
            for h2 in range(2):
                hd = pr * 2 + h2
                fns.append(lambda e, hd=hd, h2=h2: e.matmul(ps_[:, h2, 0:128], lhsT=kT[:, hd, b, :], rhs=qT[:, hd, b, :], start=True, stop=True))
                if has_prev:
                    fns.append(lambda e, hd=hd, h2=h2: e.matmul(ps_[:, h2, 128:256], lhsT=kT[:, hd, b - 1, :], rhs=qT[:, hd, b, :],
                                                               start=True, stop=True))
            S.group("pe", fns, reads=[b_qT[b], b_kT[b]] + ([b_kT[b - 1]] if has_prev else []), writes=[bps])
            yield
            p_, bp = dl["pT"], dl["b_pT"]
            wd = 256 if has_prev else 128
            S.op("act", lambda e: e.activation(out=p_[:, :, 0:wd], in_=ps_[:, :, 0:wd], func=AF.Exp, scale=SC), reads=[bps], writes=[bp])
            yield
            S.op("dve", lambda e: e.tensor_tensor(out=p_[:, :, 0:wd], in0=p_[:, :, 0:wd], in1=mask2[:, :, 0:wd], op=ALU.mult),
                 reads=[bp, b_cst], writes=[bp])
            yield
            fns = []
            for h2 in range(2):
                hd = pr * 2 + h2
                fns.append(lambda e, hd=hd, h2=h2: e.matmul(ps_[:, h2, 0:129], lhsT=p_[:, h2, 0:128], rhs=Va[:, b, hd, 0:129],
                                                           start=True, stop=not has_prev, skip_group_check=True))
                if has_prev:
                    fns.append(lambda e, hd=hd, h2=h2: e.matmul(ps_[:, h2, 0:129], lhsT=p_[:, h2, 128:256], rhs=Va[:, b - 1, hd, 0:129],
                                                               start=False, stop=True, skip_group_check=True))
            S.group("pe", fns, reads=[bp, b_Va[b]] + ([b_Va[b - 1]] if has_prev else []), writes=[bps])
            yield
            st_, bst = dl["st"], dl["b_st"]
            S.op("dve", lambda e: e.tensor_copy(out=st_[:, :, 0:129], in_=ps_[:, :, 0:129]), reads=[bps], writes=[bst])
            tok0 = r + dd * sbk * 128
            h0 = hh * 4 + pr * 2
            S.dma("sp", dl["s_st"], K.num_d[g, tok0:tok0 + 127 * dd + 1:dd, h0:h0 + 2, :], st_, reads=[bst], writes=[])
            yield

        load_slab(0)
        for si in range(len(slabs)):
            g, hh, t3 = slabs[si]
            if si + 1 < len(slabs):
                load_slab(si + 1)
            if hh == 0 and t3 == 0:
                S.dma("sp", s_rope, rope[:], K.inp["rope"][:, g, :, :, :], writes=[b_rope])
            run_rolling([(lambda li, b=b, si=si: proj_blk(li, si, b)) for b in range(NB)], NL, stagger=3)
            if t3 == 2:
                units = [(b, pr) for b in range(NB) for pr in range(2)]
                run_rolling([(lambda li, b=b, pr=pr, g=g, hh=hh: att_unit(li, g, hh, b, pr)) for (b, pr) in units], NL, stagger=1)
        S.barrier()
    with contextlib.ExitStack() as ph:
        sb = lambda n, shp, dt: _sb(K, ph, n, shp, dt)
        NC_ = 4
        LB = []
        for i in range(NC_):
            d = {"nin": [sb("d_nin%d_%d" % (i, g), [128, 8, 132], F32) for g in range(3)], "b_nin": [Buf() for g in range(3)],
                 "s_nin": [S.new_slot() for g in range(3)], "rden": sb("d_rden%d" % i, [128, 8], F32), "b_rden": Buf(),
                 "otm": sb("d_otm%d" % i, [128, 8, 128], BF16), "b_otm": Buf(),
                 "ps": _ps(K, ph, "d_pc%d_a" % i, [128, 512], F32), "ps2": _ps(K, ph, "d_pc%d_b" % i, [128, 512], F32),
                 "b_ps": PB(), "b_ps2": PB(),
                 "ot": sb("d_ot%d" % i, [128, KC, 128], BF16), "b_ot": Buf(), "s_o": S.new_slot()}
            LB.append(d)

        def comb(li, blk):
            d = LB[li]
            nin, bn = d["nin"], d["b_nin"]
            for g in range(3):
                S.dma("sp", d["s_nin"][g], nin[g][:], K.num_d[g, blk * 128:(blk + 1) * 128, :, :], reads=[K.b_num], writes=[bn[g]])
            yield
            S.op("dve", lambda e: e.tensor_tensor(out=nin[0][:], in0=nin[0][:], in1=nin[1][:], op=ALU.add), reads=[bn[0], bn[1]], writes=[bn[0]])
            yield
            S.op("dve", lambda e: e.tensor_tensor(out=nin[0][:], in0=nin[0][:], in1=nin[2][:], op=ALU.add), reads=[bn[0], bn[2]], writes=[bn[0]])
            yield
            S.op("dve", lambda e: e.reciprocal(out=d["rden"][:].unsqueeze(2), in_=nin[0][:, :, 128:129]), reads=[bn[0]], writes=[d["b_rden"]])
            yield
            S.op("dve", lambda e: e.tensor_tensor(out=d["otm"][:], in0=nin[0][:, :, 0:128],
                                                 in1=d["rden"][:].unsqueeze(2).to_broadcast([128, 8, 128]), op=ALU.mult),
                 reads=[bn[0], d["b_rden"]], writes=[d["b_otm"]])
            yield
            S.group("pe", [lambda e, hd=hd: e.matmul(d["ps"][:, hd * 128:(hd + 1) * 128], lhsT=d["otm"][:, hd, :], rhs=K.ident_bf,
                                                    start=True, stop=True) for hd in range(4)], reads=[d["b_otm"], K.b_const], writes=[d["b_ps"]])
            S.group("pe", [lambda e, hd=hd: e.matmul(d["ps2"][:, hd * 128:(hd + 1) * 128], lhsT=d["otm"][:, 4 + hd, :], rhs=K.ident_bf,
                                                    start=True, stop=True) for hd in range(4)], reads=[d["b_otm"], K.b_const], writes=[d["b_ps2"]])
            yield
            S.op("act", lambda e: e.copy(out=d["ot"][:, 0:4, :], in_=d["ps"][:].rearrange("p (a c) -> p a c", c=128)), reads=[d["b_ps"]], writes=[d["b_ot"]])
            yield
            S.op("act", lambda e: e.copy(out=d["ot"][:, 4:8, :], in_=d["ps2"][:].rearrange("p (a c) -> p a c", c=128)), reads=[d["b_ps2"]], writes=[d["b_ot"]])
            S.dma("sp", d["s_o"], K.oT_d[:, :, blk * 128:(blk + 1) * 128], d["ot"][:], reads=[d["b_ot"]], writes=[K.b_oT[blk // 2]])
            yield

        run_rolling([(lambda li, blk=blk: comb(li, blk)) for blk in range(T // 128)], NC_, stagger=2)
        S.barrier()


MIXERS["dsw"] = phase_dsw


def build(n_layers=DEPTH, mixers=True, dbg=None, mixer_seq=None):
    nc = bass.Bass("TRN2", target_bir_lowering=False)
    K = Ctx()
    K.nc = nc
    K.uid = 0
    inp = {}

    def di(name, shape, dt=F32):
        inp[name] = nc.dram_tensor(name, shape, dt, kind="ExternalInput").ap()

    di("x", [T, D])
    di("cT", [128, KC])
    di("mod_w", [DEPTH, D, 6 * D])
    di("modb", [128, DEPTH, 48])
    di("mixg", [128, DEPTH, KC])
    di("ffng", [128, DEPTH, KC])
    di("gdn_w_in", [2, D, GDN_IN])
    di("gdn_conv", [128, 2, 24, 4])
    di("gdn_hc", [128, 2, 2, 8])
    di("gdn_ng", [128, 2])
    di("gdn_w_out", [2, D, D])
    di("dsw_w_in", [2, D, DSW_IN])
    di("dsw_gain", [128, 2, 6, 128])
    di("dsw_w_out", [2, D, D])
    di("ffn_w_gate_up", [DEPTH, D, 2 * FH])
    di("ffn_w_down", [DEPTH, FH, D])
    di("cst", [128, 6, 128])
    di("rope", [128, 3, 2, 32, 16])
    di("cst2", [128, 6, 2, 128])
    K.inp = inp
    K.out = {"y": nc.dram_tensor("y", [T, D], F32, kind="ExternalOutput").ap()}
    sk = "ExternalOutput" if dbg is not None else "Internal"
    K.xT_d = nc.dram_tensor("xT_d", [128, KC, T], F32, kind=sk).ap()
    K.hT_d = nc.dram_tensor("hT_d", [128, KC, T], BF16, kind=sk).ap()
    K.oT_d = nc.dram_tensor("oT_d", [128, KC, T], BF16, kind=sk).ap()
    K.h2T_d = nc.dram_tensor("h2T_d", [128, KC, T], BF16, kind=sk).ap()
    K.num_d = nc.dram_tensor("num_d", [3, T, 8, 132], F32, kind="Internal").ap()
    K.F_d = nc.dram_tensor("F_d", [32, 128, T], BF16, kind="Internal").ap()
    K.b_F = Buf()
    K.b_xT = [Buf() for _ in range(NT)]
    K.b_hT = [Buf() for _ in range(NT)]
    K.b_oT = [Buf() for _ in range(NT)]
    K.b_h2T = [Buf() for _ in range(NT)]
    K.b_num = Buf()
    K.b_y = Buf()
    K.b_ys = [Buf(), Buf()]
    K.dbg = dbg
    if dbg:
        for nm, (shape, dt) in dbg.items():
            K.out[nm] = nc.dram_tensor(nm, shape, dt, kind="ExternalOutput").ap()

    with contextlib.ExitStack() as st:
        S = Sched(nc, st)
        K.S = S
        K.st = st
        K.cst_f = st.enter_context(nc.sbuf_tensor("cst_f", [128, 6, 128], F32))
        K.cst_b = st.enter_context(nc.sbuf_tensor("cst_b", [128, 6, 128], BF16))
        K.eps1024 = st.enter_context(nc.sbuf_tensor("eps1024", [128, 4], F32))
        K.b_const = Buf()
        s0 = S.new_slot()
        s0w = S.new_slot(sw=True)
        K.b_const2 = Buf()
        S.dma("pool", s0w, K.cst_b[:], inp["cst"], writes=[K.b_const2])
        S.dma("sp", s0, K.cst_f[:], inp["cst"], writes=[K.b_const])
        S.op("dve", lambda e: e.memset(K.eps1024[:, 0:1], 1024.0 * EPS), writes=[K.b_const])
        S.op("dve", lambda e: e.memset(K.eps1024[:, 1:2], EPS), writes=[K.b_const])
        S.op("dve", lambda e: e.memset(K.eps1024[:, 2:3], 1.0), writes=[K.b_const])
        S.op("dve", lambda e: e.memset(K.eps1024[:, 3:4], 0.0), writes=[K.b_const])
        S.barrier()
        K.ident_f = K.cst_f[:, 0, :]
        K.ones_f = K.cst_f[:, 1, :]
        K.triU_f = K.cst_f[:, 2, :]
        K.triUs_f = K.cst_f[:, 3, :]
        K.ident_bf = K.cst_b[:, 0, :]
        K.ones_bf = K.cst_b[:, 1, :]
        K.vec = {nm: st.enter_context(nc.sbuf_tensor("v_" + nm, [128, DEPTH, KC], F32))
                 for nm in ("gs1", "sh1", "g1", "gs2", "sh2", "g2")}
        K.b_vec = Buf()
        K.b_vecl = [Buf() for _ in range(DEPTH)]

        phase_x0(K)
        for l in range(n_layers):
            j = l // 2
            mix = mixer_seq[l] if mixer_seq else ("gdn" if l % 2 == 0 else "dsw")
            if mixers:
                MIXERS[mix](K, l, j)
            else:
                stub_mixer(K)
            phase_t1(K, l, inp["gdn_w_out"][j] if mix == "gdn" else inp["dsw_w_out"][j])
            phase_t2(K, l, inp["ffn_w_gate_up"][l], inp["ffn_w_down"][l], last=(l == n_layers - 1))
        deps = [K.b_ys[0].w, K.b_ys[1].w]
        S._wait("sp", deps)
        K.ninst = S.ninst
    return nc, K


def stub_mixer(K):
    S = K.S
    with contextlib.ExitStack() as ph:
        tb = [_sb(K, ph, "stub%d" % i, [128, KC, TW], BF16) for i in range(2)]
        bt = [Buf(), Buf()]
        sl = [S.new_slot(), S.new_slot()]
        so2_ = [S.new_slot(), S.new_slot()]
        for t in range(NT):
            i = t % 2
            S.dma("sp", sl[i], tb[i][:], K.hT_d[:, :, t * TW:(t + 1) * TW], reads=[K.b_hT[t]], writes=[bt[i]])
            S.dma("sp", so2_[i], K.oT_d[:, :, t * TW:(t + 1) * TW], tb[i][:], reads=[bt[i]], writes=[K.b_oT[t]])
        S.barrier()


def _fm(v):
    v = np.asarray(v, np.float32)
    lead = v.shape[:-1]
    r = v.reshape(lead + (KC, 128))
    r = np.moveaxis(r, -1, 0)
    return np.ascontiguousarray(r)


def host_consts():
    cst = np.zeros((128, 6, 128), np.float32)
    p = np.arange(128)[:, None]
    f = np.arange(128)[None, :]
    cst[:, 0] = (p == f)
    cst[:, 1] = 1.0
    cst[:, 2] = (f >= p)
    cst[:, 3] = (f > p)
    cst[:, 4] = (f <= p)
    cst[:, 5] = np.where(f > p, 0.0, -30000.0)
    half = 8 * 2
    inv = np.exp(-math.log(ROPE_THETA) * (2.0 * np.arange(16, dtype=np.float32) / 32.0)).astype(np.float32)
    rope = np.zeros((128, 3, 2, 32, 16), np.float32)
    for g, d in enumerate(DSW_DIL):
        nsb = (T // d) // 128
        for b in range(32):
            r = b // nsb
            sb = b % nsb
            pos = (r + d * (sb * 128 + np.arange(128))).astype(np.float32)
            ang = (pos[:, None] * inv[None, :]).astype(np.float32)
            rope[:, g, 0, b, :] = np.cos(ang)
            rope[:, g, 1, b, :] = np.sin(ang)
    cst2 = np.zeros((128, 6, 2, 128), np.float32)
    bd = (p // 8 == f // 8).astype(np.float32)
    cst2[:, 0, 0] = bd
    cst2[:, 0, 1] = bd
    for li, sz in enumerate((16, 32, 64, 128)):
        em = ((p // sz == f // sz) & (p % sz >= sz // 2) & (f % sz < sz // 2)).astype(np.float32)
        cst2[:, 1 + li, 0] = em
        cst2[:, 1 + li, 1] = em.T
    cst2[:, 5, 0] = (p == f)
    cst2[:, 5, 1] = (p == f)
    return cst, rope, cst2


def make_in_maps(inputs, n_cores=8):
    f = lambda a: np.ascontiguousarray(np.asarray(a, np.float32))
    cst, rope, cst2 = host_consts()
    mod_b = f(inputs["mod_b"])
    modb = np.ascontiguousarray(np.moveaxis(mod_b.reshape(DEPTH, 48, 128), -1, 0))
    mixg = np.ascontiguousarray(np.moveaxis(f(inputs["mix_norm_g"]).reshape(DEPTH, KC, 128), -1, 0))
    ffng = np.ascontiguousarray(np.moveaxis(f(inputs["ffn_norm_g"]).reshape(DEPTH, KC, 128), -1, 0))
    conv = f(inputs["gdn_conv_w"])
    gdn_conv = np.ascontiguousarray(np.transpose(conv.reshape(2, 4, 24, 128), (3, 0, 2, 1)))
    hc = np.stack([f(inputs["gdn_A_log"]), f(inputs["gdn_dt_bias"])], axis=1)
    gdn_hc = np.ascontiguousarray(np.broadcast_to(hc[None], (128, 2, 2, 8)))
    gdn_ng = np.ascontiguousarray(f(inputs["gdn_norm_g"]).T)
    qg = f(inputs["dsw_q_norm_g"])
    kg = f(inputs["dsw_k_norm_g"])
    gain = np.stack([qg, kg], axis=2).reshape(2, 6, 128)
    dsw_gain = np.ascontiguousarray(np.broadcast_to(gain[None], (128, 2, 6, 128)))
    shared = {
        "mod_w": f(inputs["mod_w"]), "modb": modb, "mixg": mixg, "ffng": ffng,
        "gdn_w_in": f(inputs["gdn_w_in"]), "gdn_conv": gdn_conv, "gdn_hc": gdn_hc, "gdn_ng": gdn_ng,
        "gdn_w_out": f(inputs["gdn_w_out"]), "dsw_w_in": f(inputs["dsw_w_in"]), "dsw_gain": dsw_gain,
        "dsw_w_out": f(inputs["dsw_w_out"]), "ffn_w_gate_up": f(inputs["ffn_w_gate_up"]),
        "ffn_w_down": f(inputs["ffn_w_down"]), "cst": cst, "rope": rope, "cst2": cst2,
    }
    x = f(inputs["x"])
    c = f(inputs["c"])
    maps = []
    for i in range(n_cores):
        b = i % 4
        m = dict(shared)
        m["x"] = np.ascontiguousarray(x[b])
        m["cT"] = np.ascontiguousarray(c[b].reshape(KC, 128).T)
        maps.append(m)
    return maps


def kernel(**inputs):
    nc, K = build()
    maps = make_in_maps(inputs, 8)
    res = run_bass_kernel_spmd(nc, maps, core_ids=list(range(8)))
    out = np.stack([np.asarray(res.results[b]["y"], np.float32) for b in range(4)], axis=0)
    return out
```

```python
import contextlib
import math
import numpy as np
import concourse.bass as bass
import concourse.mybir as mybir
from concourse.alu_op_type import AluOpType as ALU
from concourse.bass_utils import run_bass_kernel_spmd

AF = mybir.ActivationFunctionType
F32 = mybir.dt.float32
BF16 = mybir.dt.bfloat16

D = 1024
T = 4096
DEPTH = 4
KC = 8
FH = 2816
NJ = FH // 128
EPS = 1e-6
TW = 256
NT = T // TW
GDN_IN = 4112
DSW_IN = 9216
DSW_DIL = (1, 4, 16)
ROPE_THETA = 500000.0


class Buf:
    __slots__ = ("w", "r", "name", "excl")

    def __init__(self, name="", excl=False):
        self.w = None
        self.r = []
        self.name = name
        self.excl = excl


def PB():
    return Buf(excl=True)


class Sched:
    def __init__(self, nc, stack):
        self.nc = nc
        self.eng = {"pe": nc.tensor, "dve": nc.vector, "act": nc.scalar,
                    "pool": nc.gpsimd, "sp": nc.sync}
        self.sems = {}
        self.cnt = {}
        self.stack = stack
        for e in self.eng:
            self.sems[e] = stack.enter_context(nc.semaphore("s_" + e))
            self.cnt[e] = 0
        self.waited = {e: {} for e in self.eng}
        self.nslot = 0
        self.ninst = 0
        self.free_slots = []
        self.free_sw = []
        self.live_slots = []

    def new_slot(self, sw=False):
        fl = self.free_sw if sw else self.free_slots
        if fl:
            k = fl.pop()
        else:
            k = ("w%d" if sw else "d%d") % self.nslot
            self.nslot += 1
            self.sems[k] = self.stack.enter_context(self.nc.semaphore("s_" + k))
            self.cnt[k] = 0
        self.live_slots.append(k)
        return k

    def _wait(self, e, deps):
        best = {}
        for d in deps:
            if d is None:
                continue
            k, v = d
            if best.get(k, 0) < v:
                best[k] = v
        w = self.waited[e]
        for k, v in best.items():
            if k == e and e == "pe":
                continue
            if w.get(k, 0) < v:
                self.eng[e].wait_ge(self.sems[k], v)
                w[k] = v
                self.ninst += 1

    @staticmethod
    def _deps(reads, writes):
        deps = []
        for b in reads:
            deps.append(b.w)
            if b.excl:
                deps.extend(b.r)
        for b in writes:
            deps.append(b.w)
            deps.extend(b.r)
        return deps

    @staticmethod
    def _stamp(st, reads, writes):
        for b in reads:
            if b.excl:
                b.w = st
                b.r = []
                continue
            b.r.append(st)
            if len(b.r) > 64:
                best = {}
                for k, v in b.r:
                    if best.get(k, 0) < v:
                        best[k] = v
                b.r = list(best.items())
        for b in writes:
            b.w = st
            b.r = []

    def op(self, e, fn, reads=(), writes=()):
        self._wait(e, self._deps(reads, writes))
        inst = fn(self.eng[e])
        self.cnt[e] += 1
        inst.then_inc(self.sems[e], 1)
        self.ninst += 1
        self._stamp((e, self.cnt[e]), reads, writes)
        return inst

    def group(self, e, fns, reads=(), writes=()):
        self._wait(e, self._deps(reads, writes))
        inst = None
        for fn in fns:
            inst = fn(self.eng[e])
            self.ninst += 1
        self.cnt[e] += 1
        inst.then_inc(self.sems[e], 1)
        self._stamp((e, self.cnt[e]), reads, writes)
        return inst

    def dma(self, q, slot, out, in_, reads=(), writes=()):
        assert (slot[0] == "w") == (q == "pool"), (q, slot)
        self._wait(q, self._deps(reads, writes))
        inst = self.eng[q].dma_start(out=out, in_=in_)
        self.cnt[slot] += 16
        inst.then_inc(self.sems[slot], 16)
        self.ninst += 1
        self._stamp((slot, self.cnt[slot]), reads, writes)
        return inst

    def barrier(self):
        deps = [(k, v) for k, v in self.cnt.items() if v > 0]
        for e in self.eng:
            w = self.waited[e]
            for k, v in deps:
                if k != e and w.get(k, 0) < v:
                    self.eng[e].wait_ge(self.sems[k], v)
                    w[k] = v
        for k in self.live_slots:
            (self.free_sw if k[0] == "w" else self.free_slots).append(k)
        self.live_slots = []


class Ctx:
    pass


def _sb(K, ph, name, shape, dt):
    K.uid += 1
    return ph.enter_context(K.nc.sbuf_tensor("%s_%d" % (name, K.uid), shape, dt))


def _ps(K, ph, name, shape, dt):
    K.uid += 1
    return ph.enter_context(K.nc.psum_tensor("%s_%d" % (name, K.uid), shape, dt))


def mod_setup(K, ph):
    nc, S = K.nc, K.S
    M = {}
    M["cT"] = _sb(K, ph, "cT", [128, KC], F32)
    M["cond"] = _sb(K, ph, "cond", [128, KC], BF16)
    M["mixg"] = _sb(K, ph, "mixg", [128, DEPTH, KC], F32)
    M["ffng"] = _sb(K, ph, "ffng", [128, DEPTH, KC], F32)
    M["modb"] = _sb(K, ph, "modb", [128, DEPTH, 48], F32)
    M["modv"] = _sb(K, ph, "modv", [128, 48], F32)
    M["wsl"] = [_sb(K, ph, "mwsl%d" % i, [128, KC, 512], BF16) for i in range(2)]
    M["mps"] = _ps(K, ph, "modps", [128, 512], F32)
    for k in ("b_c", "b_cond", "b_mixg", "b_ffng", "b_modb", "b_modv"):
        M[k] = Buf()
    M["b_mps"] = PB()
    M["b_w"] = [Buf(), Buf()]
    M["sl"] = [S.new_slot(sw=True), S.new_slot(sw=True)]
    S.dma("sp", S.new_slot(), M["cT"][:], K.inp["cT"], writes=[M["b_c"]])
    S.dma("sp", S.new_slot(), M["mixg"][:], K.inp["mixg"], writes=[M["b_mixg"]])
    S.dma("sp", S.new_slot(), M["ffng"][:], K.inp["ffng"], writes=[M["b_ffng"]])
    S.dma("sp", S.new_slot(), M["modb"][:], K.inp["modb"], writes=[M["b_modb"]])
    S.op("act", lambda e: e.activation(out=M["cond"][:], in_=M["cT"][:], func=AF.Silu), reads=[M["b_c"]], writes=[M["b_cond"]])
    return M


def mod_layer_gen(K, M, l):
    nc, S = K.nc, K.S
    wsl, mps, cond, modv, modb, mixg, ffng = M["wsl"], M["mps"], M["cond"], M["modv"], M["modb"], M["mixg"], M["ffng"]
    wv = K.inp["mod_w"][l].rearrange("(kc p) n -> p kc n", p=128)
    for s_ in range(12):
        w = wsl[s_ % 2]
        bw = M["b_w"][s_ % 2]
        S.dma("pool", M["sl"][s_ % 2], w[:], wv[:, :, s_ * 512:(s_ + 1) * 512], writes=[bw])
        yield
        for mm in range(4):
            m = s_ * 4 + mm
            S.group("pe", [lambda e, kc=kc, mm=mm, m=m, w=w: e.matmul(
                mps[:, m:m + 1], lhsT=w[:, kc, mm * 128:(mm + 1) * 128], rhs=cond[:, kc:kc + 1],
                start=(kc == 0), stop=(kc == KC - 1)) for kc in range(KC)], reads=[bw, M["b_cond"]], writes=[M["b_mps"]])
            yield
    S.op("dve", lambda e: e.tensor_tensor(out=modv[:], in0=mps[:, 0:48], in1=modb[:, l, :], op=ALU.add),
         reads=[M["b_mps"], M["b_modb"]], writes=[M["b_modv"]])
    V = K.vec
    bv = K.b_vecl[l]
    b_modv = M["b_modv"]
    S.op("dve", lambda e: e.scalar_tensor_tensor(out=V["gs1"][:, l, :], in0=modv[:, 8:16], scalar=1.0,
                                                in1=mixg[:, l, :], op0=ALU.add, op1=ALU.mult),
         reads=[b_modv, M["b_mixg"]], writes=[bv])
    S.op("dve", lambda e: e.tensor_scalar(out=V["gs1"][:, l, :], in0=V["gs1"][:, l, :], scalar1=32.0,
                                         scalar2=None, op0=ALU.mult), reads=[bv], writes=[bv])
    S.op("dve", lambda e: e.scalar_tensor_tensor(out=V["gs2"][:, l, :], in0=modv[:, 32:40], scalar=1.0,
                                                in1=ffng[:, l, :], op0=ALU.add, op1=ALU.mult),
         reads=[b_modv, M["b_ffng"]], writes=[bv])
    S.op("dve", lambda e: e.tensor_scalar(out=V["gs2"][:, l, :], in0=V["gs2"][:, l, :], scalar1=32.0,
                                         scalar2=None, op0=ALU.mult), reads=[bv], writes=[bv])
    for nm, off in (("sh1", 0), ("g1", 16), ("sh2", 24), ("g2", 40)):
        S.op("dve", lambda e, nm=nm, off=off: e.tensor_copy(out=V[nm][:, l, :], in_=modv[:, off:off + 8]),
             reads=[b_modv], writes=[bv])
    yield


def emit_norm(K, N, x, b_x, h, b_h, gs, sh, W):
    S = K.S
    S.op("act", lambda e: e.activation(out=N["sq"][:, :, 0:W], in_=x[:, :, 0:W], func=AF.Square),
         reads=[b_x], writes=[N["b_sq"]])
    S.group("pe", [lambda e, kc=kc: e.matmul(N["ss"][:, 0:W], lhsT=K.ones_bf, rhs=N["sq"][:, kc, 0:W],
                                             start=(kc == 0), stop=(kc == KC - 1)) for kc in range(KC)],
            reads=[N["b_sq"], K.b_const], writes=[N["b_ss"]])
    S.op("act", lambda e: e.activation(out=N["rstd"][:, 0:W], in_=N["ss"][:, 0:W], func=AF.Ln,
                                       bias=K.eps1024[:, 0:1], scale=1.0),
         reads=[N["b_ss"], K.b_const], writes=[N["b_rstd"]])
    S.op("act", lambda e: e.activation(out=N["rstd"][:, 0:W], in_=N["rstd"][:, 0:W], func=AF.Exp, scale=-0.5),
         reads=[N["b_rstd"]], writes=[N["b_rstd"]])
    for kc in range(KC):
        S.op("dve", lambda e, kc=kc: e.tensor_tensor(out=N["tmp"][:, kc, 0:W], in0=x[:, kc, 0:W],
                                                    in1=N["rstd"][:, 0:W], op=ALU.mult),
             reads=[b_x, N["b_rstd"]], writes=[N["b_tmp"]])
        S.op("act", lambda e, kc=kc: e.activation(out=h[:, kc, 0:W], in_=N["tmp"][:, kc, 0:W], func=AF.Identity,
                                                 bias=sh[:, kc:kc + 1], scale=gs[:, kc:kc + 1]),
             reads=[N["b_tmp"], K.b_vec], writes=[b_h])


def norm_gen(K, N, x, b_x, h, b_h, gs, sh, W, bvec=None):
    S = K.S
    bvec = K.b_vec if bvec is None else bvec
    S.op("act", lambda e: e.activation(out=N["sq"][:, :, 0:W], in_=x[:, :, 0:W], func=AF.Square),
         reads=[b_x], writes=[N["b_sq"]])
    yield
    S.group("pe", [lambda e, kc=kc: e.matmul(N["ss"][:, 0:W], lhsT=K.ones_bf, rhs=N["sq"][:, kc, 0:W],
                                             start=(kc == 0), stop=(kc == KC - 1)) for kc in range(KC)],
            reads=[N["b_sq"], K.b_const], writes=[N["b_ss"]])
    yield
    S.op("act", lambda e: e.activation(out=N["rstd"][:, 0:W], in_=N["ss"][:, 0:W], func=AF.Ln,
                                       bias=K.eps1024[:, 0:1], scale=1.0),
         reads=[N["b_ss"], K.b_const], writes=[N["b_rstd"]])
    yield
    S.op("act", lambda e: e.activation(out=N["rstd"][:, 0:W], in_=N["rstd"][:, 0:W], func=AF.Exp, scale=-0.5),
         reads=[N["b_rstd"]], writes=[N["b_rstd"]])
    yield
    for kc in range(KC):
        S.op("dve", lambda e, kc=kc: e.tensor_tensor(out=N["tmp"][:, kc, 0:W], in0=x[:, kc, 0:W],
                                                    in1=N["rstd"][:, 0:W], op=ALU.mult),
             reads=[b_x, N["b_rstd"]], writes=[N["b_tmpk"][kc]])
        yield
        S.op("act", lambda e, kc=kc: e.activation(out=h[:, kc, 0:W], in_=N["tmp"][:, kc, 0:W], func=AF.Identity,
                                                 bias=sh[:, kc:kc + 1], scale=gs[:, kc:kc + 1]),
             reads=[N["b_tmpk"][kc], bvec], writes=[b_h])
        yield


def alloc_norm(K, ph, W):
    N = {}
    N["sq"] = _sb(K, ph, "nsq", [128, KC, W], BF16)
    N["tmp"] = _sb(K, ph, "ntmp", [128, KC, W], F32)
    N["rstd"] = _sb(K, ph, "nrstd", [128, W], F32)
    N["ss"] = _ps(K, ph, "nss", [128, 512], F32)
    for k in ("b_sq", "b_tmp", "b_rstd"):
        N[k] = Buf()
    N["b_ss"] = PB()
    N["b_tmpk"] = [Buf() for _ in range(KC)]
    return N


def phase_x0(K):
    nc, S = K.nc, K.S
    NLX = 3
    with contextlib.ExitStack() as ph:
        M = mod_setup(K, ph)
        for _ in mod_layer_gen(K, M, 0):
            pass
        LX = []
        for i in range(NLX):
            d = {"xin": _sb(K, ph, "xin%d" % i, [128, 2, D], F32), "b_xin": Buf(), "s_in": S.new_slot(),
                 "xt": _sb(K, ph, "xt%d" % i, [128, KC, TW], F32), "b_xt": Buf(),
                 "ht": _sb(K, ph, "ht%d" % i, [128, KC, TW], BF16), "b_ht": Buf(),
                 "pt": _ps(K, ph, "x0pt%d" % i, [128, 4, 128], F32), "b_pt": PB(),
                 "N": alloc_norm(K, ph, TW), "s_o": S.new_slot(), "s_o2": S.new_slot()}
            LX.append(d)

        def job(li, t):
            d = LX[li]
            xin, xt, ht, pt = d["xin"], d["xt"], d["ht"], d["pt"]
            S.dma("sp", d["s_in"], xin[:], K.inp["x"][t * TW:(t + 1) * TW, :].rearrange("(a p) n -> p a n", p=128), writes=[d["b_xin"]])
            yield
            for half in range(2):
                for hf in range(2):
                    S.group("pe", [lambda e, q=q, hf=hf, half=half: e.matmul(
                        pt[:, q, :], lhsT=xin[:, half, (hf * 4 + q) * 128:(hf * 4 + q + 1) * 128], rhs=K.ident_f, start=True, stop=True)
                        for q in range(4)], reads=[d["b_xin"], K.b_const], writes=[d["b_pt"]])
                    yield
                    if hf == 0:
                        S.op("act", lambda e, hf=hf, half=half: e.copy(out=xt[:, hf * 4:(hf + 1) * 4, half * 128:(half + 1) * 128], in_=pt[:]),
                             reads=[d["b_pt"]], writes=[d["b_xt"]])
                    else:
                        S.op("dve", lambda e, hf=hf, half=half: e.tensor_copy(out=xt[:, hf * 4:(hf + 1) * 4, half * 128:(half + 1) * 128], in_=pt[:]),
                             reads=[d["b_pt"]], writes=[d["b_xt"]])
                    yield
            yield from norm_gen(K, d["N"], xt, d["b_xt"], ht, d["b_ht"], K.vec["gs1"][:, 0, :], K.vec["sh1"][:, 0, :], TW, bvec=K.b_vecl[0])
            S.dma("sp", d["s_o"], K.xT_d[:, :, t * TW:(t + 1) * TW], xt[:], reads=[d["b_xt"]], writes=[K.b_xT[t]])
            S.dma("sp", d["s_o2"], K.hT_d[:, :, t * TW:(t + 1) * TW], ht[:], reads=[d["b_ht"]], writes=[K.b_hT[t]])
            yield

        def mod_rest(li):
            for l in range(1, DEPTH):
                yield from mod_layer_gen(K, M, l)

        jobs = [(lambda li, t=t: job(li, t)) for t in range(NT)]
        active = {NLX: mod_rest(NLX)}
        nxt = 0
        step = 0
        while nxt < len(jobs) or active:
            for li in range(NLX):
                if li not in active and nxt < len(jobs) and step >= li * 5:
                    active[li] = jobs[nxt](li)
                    nxt += 1
            for li in list(active.keys()):
                try:
                    next(active[li])
                except StopIteration:
                    del active[li]
            step += 1
        S.barrier()


def phase_t1(K, l, w_out_ap):
    nc, S = K.nc, K.S
    NLT = 3
    with contextlib.ExitStack() as ph:
        wo = _sb(K, ph, "wo", [128, KC, D], BF16)
        b_wo = Buf()
        S.dma("pool", S.new_slot(sw=True), wo[:], w_out_ap.rearrange("(kc p) n -> p kc n", p=128), writes=[b_wo])
        V = K.vec
        LT = []
        for i in range(NLT):
            d = {"xt": _sb(K, ph, "t1x%d" % i, [128, KC, TW], F32), "b_xt": Buf(), "s_x": S.new_slot(),
                 "ot": _sb(K, ph, "t1o%d" % i, [128, KC, TW], BF16), "b_ot": Buf(), "s_o": S.new_slot(),
                 "ht": _sb(K, ph, "t1h%d" % i, [128, KC, TW], BF16), "b_ht": Buf(),
                 "acc": _ps(K, ph, "t1acc%d" % i, [128, 2, TW], F32), "b_acc": PB(),
                 "N": alloc_norm(K, ph, TW), "s_so": S.new_slot(), "s_so2": S.new_slot()}
            LT.append(d)

        def job(li, t):
            d = LT[li]
            xt, ot, ht, acc = d["xt"], d["ot"], d["ht"], d["acc"]
            S.dma("sp", d["s_x"], xt[:], K.xT_d[:, :, t * TW:(t + 1) * TW], reads=[K.b_xT[t]], writes=[d["b_xt"]])
            S.dma("sp", d["s_o"], ot[:], K.oT_d[:, :, t * TW:(t + 1) * TW], reads=[K.b_oT[t]], writes=[d["b_ot"]])
            yield
            for m in range(KC):
                S.group("pe", [lambda e, kc=kc, m=m: e.matmul(
                    acc[:, m % 2, :], lhsT=wo[:, kc, m * 128:(m + 1) * 128], rhs=ot[:, kc, :],
                    start=(kc == 0), stop=(kc == KC - 1)) for kc in range(KC)], reads=[b_wo, d["b_ot"]], writes=[d["b_acc"]])
                yield
                S.op("dve", lambda e, m=m: e.scalar_tensor_tensor(
                    out=xt[:, m, :], in0=acc[:, m % 2, :], scalar=V["g1"][:, l, m:m + 1], in1=xt[:, m, :],
                    op0=ALU.mult, op1=ALU.add), reads=[d["b_acc"], d["b_xt"], K.b_vec], writes=[d["b_xt"]])
                yield
            yield from norm_gen(K, d["N"], xt, d["b_xt"], ht, d["b_ht"], V["gs2"][:, l, :], V["sh2"][:, l, :], TW)
            S.dma("sp", d["s_so"], K.xT_d[:, :, t * TW:(t + 1) * TW], xt[:], reads=[d["b_xt"]], writes=[K.b_xT[t]])
            S.dma("sp", d["s_so2"], K.h2T_d[:, :, t * TW:(t + 1) * TW], ht[:], reads=[d["b_ht"]], writes=[K.b_h2T[t]])
            yield

        run_rolling([(lambda li, t=t: job(li, t)) for t in range(NT)], NLT, stagger=6)
        S.barrier()


def phase_t2(K, l, w_gu_ap, w_dn_ap, last, pre=None):
    nc, S = K.nc, K.S
    TW2 = 512
    NT2 = T // TW2
    with contextlib.ExitStack() as ph:
        wgu, b_wgu = pre
        wdn = _sb(K, ph, "wdn", [128, NJ, D], BF16)
        b_wdn = Buf()
        xt = _sb(K, ph, "t2x", [128, KC, TW2], F32)
        b_xt = Buf()
        h2 = [_sb(K, ph, "t2h%d" % i, [128, KC, TW2], BF16) for i in range(2)]
        b_h2 = [Buf(), Buf()]
        act = _sb(K, ph, "t2act", [128, NJ, TW2], BF16)
        b_act = [Buf() for _ in range(NJ)]
        sg = [_sb(K, ph, "t2sg", [128, TW2], F32)] * 2
        b_sg = [Buf()] * 2
        pg = [_ps(K, ph, "t2pg%d" % i, [128, TW2], F32) for i in range(4)]
        b_pg = [PB() for _ in range(4)]
        pacc = [_ps(K, ph, "t2pa%d" % i, [128, TW2], F32) for i in range(4)]
        b_pacc = [PB() for _ in range(4)]
        slx = S.new_slot()
        slh = [S.new_slot(), S.new_slot()]
        so = S.new_slot()
        so2 = S.new_slot()
        S.dma("pool", S.new_slot(sw=True), wdn[:], w_dn_ap.rearrange("(j p) n -> p j n", p=128), writes=[b_wdn])
        V = K.vec
        if last:
            osb = [_sb(K, ph, "t2os%d" % i, [128, D], F32) for i in range(2)]
            b_osb = [Buf(), Buf()]
            s_os = [S.new_slot(), S.new_slot()]
        else:
            N = {"sq": _sb(K, ph, "t2nsq", [128, 2, TW2], BF16), "tmp": _sb(K, ph, "t2ntmp", [128, 2, TW2], F32),
                 "rstd": _sb(K, ph, "t2nrstd", [128, TW2], F32), "ss": pacc[0], "b_ss": b_pacc[0],
                 "b_sqk": [Buf(), Buf()], "b_rstd": Buf(), "b_tmpk": [Buf(), Buf()]}
            hn = _sb(K, ph, "t2hn", [128, 4, TW2], BF16)
            b_hn = Buf()

        def load_h2(t):
            S.dma("sp", slh[t % 2], h2[t % 2][:], K.h2T_d[:, :, t * TW2:(t + 1) * TW2],
                  reads=[K.b_h2T[2 * t], K.b_h2T[2 * t + 1]], writes=[b_h2[t % 2]])

        def epilogue(t):
            tb = [K.b_xT[2 * t], K.b_xT[2 * t + 1]]
            if last:
                for sub in range(TW2 // 128):
                    o_ = osb[sub % 2]
                    bo = b_osb[sub % 2]
                    for hf in range(2):
                        p, bp = pacc[hf], b_pacc[hf]
                        S.group("pe", [lambda e, hf=hf, q=q, p=p, sub=sub: e.matmul(
                            p[:, q * 128:(q + 1) * 128], lhsT=xt[:, hf * 4 + q, sub * 128:(sub + 1) * 128], rhs=K.ident_f,
                            start=True, stop=True) for q in range(4)], reads=[b_xt, K.b_const], writes=[bp])
                        yield
                        if hf == 0:
                            S.op("act", lambda e, p=p, o_=o_: e.copy(out=o_[:, 0:512], in_=p[:]), reads=[bp], writes=[bo])
                        else:
                            S.op("dve", lambda e, p=p, o_=o_: e.tensor_copy(out=o_[:, 512:1024], in_=p[:]), reads=[bp], writes=[bo])
                        yield
                    r0 = t * TW2 + sub * 128
                    S.dma("sp", s_os[sub % 2], K.out["y"][r0:r0 + 128, :], o_[:], reads=[bo], writes=[K.b_ys[sub % 2]])
                    yield
                S.dma("sp", so, K.xT_d[:, :, t * TW2:(t + 1) * TW2], xt[:], reads=[b_xt], writes=tb)
                yield
            else:
                for kc in range(KC):
                    S.op("act", lambda e, kc=kc: e.activation(out=N["sq"][:, kc % 2, :], in_=xt[:, kc, :], func=AF.Square),
                         reads=[b_xt], writes=[N["b_sqk"][kc % 2]])
                    yield
                    S.op("pe", lambda e, kc=kc: e.matmul(N["ss"][:], lhsT=K.ones_bf, rhs=N["sq"][:, kc % 2, :],
                                                        start=(kc == 0), stop=(kc == KC - 1)),
                         reads=[N["b_sqk"][kc % 2], K.b_const], writes=[N["b_ss"]])
                    yield
                S.op("act", lambda e: e.activation(out=N["rstd"][:], in_=N["ss"][:], func=AF.Ln, bias=K.eps1024[:, 0:1], scale=1.0),
                     reads=[N["b_ss"], K.b_const], writes=[N["b_rstd"]])
                yield
                S.op("act", lambda e: e.activation(out=N["rstd"][:], in_=N["rstd"][:], func=AF.Exp, scale=-0.5),
                     reads=[N["b_rstd"]], writes=[N["b_rstd"]])
                yield
                gs, sh = V["gs1"][:, l + 1, :], V["sh1"][:, l + 1, :]
                for kc in range(KC):
                    S.op("dve", lambda e, kc=kc: e.tensor_tensor(out=N["tmp"][:, kc % 2, :], in0=xt[:, kc, :], in1=N["rstd"][:], op=ALU.mult),
                         reads=[b_xt, N["b_rstd"]], writes=[N["b_tmpk"][kc % 2]])
                    yield
                    S.op("act", lambda e, kc=kc: e.activation(out=hn[:, kc % 4, :], in_=N["tmp"][:, kc % 2, :], func=AF.Identity,
                                                             bias=sh[:, kc:kc + 1], scale=gs[:, kc:kc + 1]),
                         reads=[N["b_tmpk"][kc % 2], K.b_vec], writes=[b_hn])
                    yield
                    if kc % 4 == 3:
                        k0 = kc - 3
                        S.dma("sp", so2, K.hT_d[:, k0:k0 + 4, t * TW2:(t + 1) * TW2], hn[:], reads=[b_hn],
                              writes=[K.b_hT[2 * t], K.b_hT[2 * t + 1]])
                S.dma("sp", so, K.xT_d[:, :, t * TW2:(t + 1) * TW2], xt[:], reads=[b_xt], writes=tb)
                yield

        load_h2(0)
        pend = None
        for t in range(NT2):
            i = t % 2
            if t + 1 < NT2:
                load_h2(t + 1)
            for j in range(NJ):
                g_, u_ = pg[2 * (j % 2)], pg[2 * (j % 2) + 1]
                bg_, bu_ = b_pg[2 * (j % 2)], b_pg[2 * (j % 2) + 1]
                S.group("pe", [lambda e, kc=kc, j=j, g_=g_, i=i: e.matmul(
                    g_[:], lhsT=wgu[:, kc, j * 128:(j + 1) * 128], rhs=h2[i][:, kc, :],
                    start=(kc == 0), stop=(kc == KC - 1)) for kc in range(KC)], reads=b_wgu + [b_h2[i]], writes=[bg_])
                S.group("pe", [lambda e, kc=kc, j=j, u_=u_, i=i: e.matmul(
                    u_[:], lhsT=wgu[:, kc, FH + j * 128:FH + (j + 1) * 128], rhs=h2[i][:, kc, :],
                    start=(kc == 0), stop=(kc == KC - 1)) for kc in range(KC)], reads=b_wgu + [b_h2[i]], writes=[bu_])
                s_ = sg[j % 2]
                S.op("act", lambda e, g_=g_, s_=s_: e.activation(out=s_[:], in_=g_[:], func=AF.Silu), reads=[bg_], writes=[b_sg[j % 2]])
                S.op("dve", lambda e, u_=u_, s_=s_, j=j: e.tensor_tensor(out=act[:, j, :], in0=u_[:], in1=s_[:], op=ALU.mult),
                     reads=[bu_, b_sg[j % 2]], writes=[b_act[j]])
                if pend is not None:
                    for _ in range(2):
                        try:
                            next(pend)
                        except StopIteration:
                            pend = None
                            break
            while pend is not None:
                try:
                    next(pend)
                except StopIteration:
                    pend = None
            S.dma("sp", slx, xt[:], K.xT_d[:, :, t * TW2:(t + 1) * TW2], reads=[K.b_xT[2 * t], K.b_xT[2 * t + 1]], writes=[b_xt])
            for half in range(2):
                banks = pacc if half == 0 else pg
                bbanks = b_pacc if half == 0 else b_pg
                for mm in range(4):
                    m = half * 4 + mm
                    S.group("pe", [lambda e, j=j, m=m, mm=mm, banks=banks: e.matmul(
                        banks[mm][:], lhsT=wdn[:, j, m * 128:(m + 1) * 128], rhs=act[:, j, :],
                        start=(j == 0), stop=(j == NJ - 1)) for j in range(NJ)], reads=[b_wdn] + b_act, writes=[bbanks[mm]])
                    S.op("dve", lambda e, m=m, mm=mm, banks=banks: e.scalar_tensor_tensor(
                        out=xt[:, m, :], in0=banks[mm][:], scalar=V["g2"][:, l, m:m + 1], in1=xt[:, m, :],
                        op0=ALU.mult, op1=ALU.add), reads=[bbanks[mm], b_xt, K.b_vec], writes=[b_xt])
            pend = epilogue(t)
        while pend is not None:
            try:
                next(pend)
            except StopIteration:
                pend = None
        S.barrier()


MIXERS = {}

def phase_gdn(K, l, j):
    nc, S = K.nc, K.S
    NB = T // 128
    wv = K.inp["gdn_w_in"][j].rearrange("(kc p) n -> p kc n", p=128)
    with contextlib.ExitStack() as gph:
        gsb = lambda n, shp, dt: _sb(K, gph, n, shp, dt)
        K.cst2 = gsb("cst2_sb", [128, 6, 2, 128], BF16)
        S.dma("pool", S.new_slot(sw=True), K.cst2[:], K.inp["cst2"], writes=[K.b_const])
        beta = gsb("g_beta", [128, NB, 8], F32)
        gc = gsb("g_gc", [128, NB, 8], F32)
        egc = gsb("g_egc", [128, NB, 8], F32)
        etail = gsb("g_etail", [128, NB, 8], F32)
        glast = gsb("g_glast", [128, NB, 8], F32)
        bgt = gsb("g_bg", [128, NB, 8], F32)
        cw = gsb("g_cw", [128, 2, 24, 4], F32)
        hc = gsb("g_hc", [128, 2, 2, 8], F32)
        ng = gsb("g_ng", [128, 2], F32)
        b_g = Buf()
        b_cst = Buf()
        scst = S.new_slot()
        S.dma("sp", scst, cw[:], K.inp["gdn_conv"], writes=[b_cst])
        S.dma("sp", scst, hc[:], K.inp["gdn_hc"], writes=[b_cst])
        S.dma("sp", scst, ng[:], K.inp["gdn_ng"], writes=[b_cst])
        with contextlib.ExitStack() as ph:
            sb = lambda n, shp, dt: _sb(K, ph, n, shp, dt)
            hT = sb("g_hT", [128, KC, T], BF16)
            b_hT = [Buf() for _ in range(KC)]
            for kc in range(KC):
                S.dma("sp", S.new_slot(), hT[:, kc, :], K.hT_d[:, kc, :], reads=K.b_hT, writes=[b_hT[kc]])
            pf = [_ps(K, ph, "g_pf%d" % i, [128, 512], F32) for i in range(4)]
            b_pf = [PB() for _ in range(4)]
            wba = sb("g_wba", [128, KC, 16], BF16)
            b_wba = Buf()
            S.dma("pool", S.new_slot(sw=True), wba[:], wv[:, :, 4096:4112], writes=[b_wba])
            ba = sb("g_ba", [128, NB, 16], F32)
            gg = sb("g_g", [128, NB, 8], F32)
            negA = sb("g_negA", [128, 8], F32)
            for blk in range(NB):
                S.group("pe", [lambda e, kc=kc, blk=blk: e.matmul(
                    pf[0][:, blk * 16:(blk + 1) * 16], lhsT=hT[:, kc, blk * 128:(blk + 1) * 128], rhs=wba[:, kc, :],
                    start=(kc == 0), stop=(kc == KC - 1)) for kc in range(KC)], reads=b_hT + [b_wba], writes=[b_pf[0]])
            S.op("act", lambda e: e.copy(out=ba[:].rearrange("p a b -> p (a b)"), in_=pf[0][:]), reads=[b_pf[0]], writes=[b_g])
            S.op("act", lambda e: e.activation(out=beta[:], in_=ba[:, :, 0:8], func=AF.Sigmoid), reads=[b_g], writes=[b_g])
            S.op("act", lambda e: e.activation(out=negA[:], in_=hc[:, j, 0, :], func=AF.Exp), reads=[b_cst], writes=[b_g])
            S.op("dve", lambda e: e.tensor_scalar(out=negA[:], in0=negA[:], scalar1=-1.0, scalar2=None, op0=ALU.mult),
                 reads=[b_g], writes=[b_g])
            S.op("dve", lambda e: e.tensor_tensor(out=gg[:], in0=ba[:, :, 8:16],
                                                 in1=hc[:, j, 1, :].unsqueeze(1).to_broadcast([128, NB, 8]), op=ALU.add),
                 reads=[b_g, b_cst], writes=[b_g])
            S.op("act", lambda e: e.activation(out=gg[:], in_=gg[:], func=AF.Exp), reads=[b_g], writes=[b_g])
            S.op("act", lambda e: e.activation(out=gg[:], in_=gg[:], func=AF.Ln, bias=K.eps1024[:, 2:3], scale=1.0),
                 reads=[b_g, K.b_const], writes=[b_g])
            S.op("dve", lambda e: e.tensor_tensor(out=gg[:], in0=gg[:], in1=negA[:].unsqueeze(1).to_broadcast([128, NB, 8]),
                                                 op=ALU.mult), reads=[b_g], writes=[b_g])
            ggf = gg[:].rearrange("p a b -> p (a b)")
            S.op("pe", lambda e: e.matmul(pf[1][:, 0:256], lhsT=K.triU_f, rhs=ggf, start=True, stop=True),
                 reads=[b_g, K.b_const], writes=[b_pf[1]])
            S.op("pe", lambda e: e.matmul(pf[2][:, 0:256], lhsT=K.ones_f, rhs=ggf, start=True, stop=True),
                 reads=[b_g, K.b_const], writes=[b_pf[2]])
            fl = lambda t: t[:].rearrange("p a b -> p (a b)")
            S.op("act", lambda e: e.copy(out=fl(gc), in_=pf[1][:, 0:256]), reads=[b_pf[1]], writes=[b_g])
            S.op("act", lambda e: e.activation(out=fl(egc), in_=pf[1][:, 0:256], func=AF.Exp), reads=[b_pf[1]], writes=[b_g])
            S.op("act", lambda e: e.activation(out=fl(glast), in_=pf[2][:, 0:256], func=AF.Exp), reads=[b_pf[2]], writes=[b_g])
            S.op("dve", lambda e: e.tensor_tensor(out=fl(etail), in0=pf[2][:, 0:256], in1=fl(gc), op=ALU.subtract),
                 reads=[b_pf[2], b_g], writes=[b_g])
            S.op("act", lambda e: e.activation(out=fl(etail), in_=fl(etail), func=AF.Exp), reads=[b_g], writes=[b_g])
            S.op("dve", lambda e: e.tensor_tensor(out=fl(bgt), in0=fl(beta), in1=fl(egc), op=ALU.mult), reads=[b_g], writes=[b_g])
            NW = 4
            w1 = [sb("g_w1%d" % i, [128, KC, 128], BF16) for i in range(NW)]
            b_w1 = [Buf() for _ in range(NW)]
            s_w1 = [S.new_slot(sw=True) for _ in range(NW)]
            stg = [sb("g_stg%d" % i, [128, 3 + T], F32) for i in range(2)]
            b_stg = [[Buf() for _ in range(9)] for _ in range(2)]
            for i in range(2):
                S.op("pool", lambda e, i=i: e.memset(stg[i][:, 0:3], 0.0), writes=[b_stg[i][0]])

            def load_w1(ci):
                S.dma("pool", s_w1[ci % NW], w1[ci % NW][:], wv[:, :, ci * 128:(ci + 1) * 128], writes=[b_w1[ci % NW]])

            for ci in range(NW - 1):
                load_w1(ci)
            NLA = 4
            LAa = [{"cacc": sb("g_caccL%d" % i, [128, 512], F32), "b_cacc": Buf(),
                    "fo": sb("g_foL%d" % i, [128, 512], BF16), "b_fo": Buf(), "s_fo": S.new_slot()} for i in range(NLA)]

            def ajob(li, ci, tt):
                d = LAa[li]
                if tt == 0 and ci + NW - 1 < 32:
                    load_w1(ci + NW - 1)
                W = w1[ci % NW]
                bW = b_w1[ci % NW]
                sg_ = stg[ci % 2]
                bsg = b_stg[ci % 2]
                pa, bpa = pf[li], b_pf[li]
                S.group("pe", [lambda e, kc=kc: e.matmul(
                    pa[:], lhsT=W[:, kc, :], rhs=hT[:, kc, tt * 512:(tt + 1) * 512],
                    start=(kc == 0), stop=(kc == KC - 1)) for kc in range(KC)], reads=b_hT + [bW], writes=[bpa])
                yield
                o = tt * 512
                f_, bf_ = d["fo"], d["b_fo"]
                if ci >= 24:
                    S.op("act", lambda e: e.activation(out=f_[:], in_=pa[:], func=AF.Silu), reads=[bpa], writes=[bf_])
                    yield
                else:
                    S.op("act", lambda e: e.copy(out=sg_[:, 3 + o:3 + o + 512], in_=pa[:]), reads=[bpa], writes=[bsg[1 + tt]])
                    yield
                    ca, bca = d["cacc"], d["b_cacc"]
                    rd = [bsg[tt], bsg[1 + tt], b_cst]
                    S.op("dve", lambda e: e.tensor_scalar(
                        out=ca[:], in0=sg_[:, 3 + o:3 + o + 512], scalar1=cw[:, j, ci, 3:4], scalar2=None, op0=ALU.mult),
                        reads=rd, writes=[bca])
                    yield
                    for k in (2, 1, 0):
                        S.op("dve", lambda e, k=k: e.scalar_tensor_tensor(
                            out=ca[:], in0=sg_[:, k + o:k + o + 512], scalar=cw[:, j, ci, k:k + 1], in1=ca[:],
                            op0=ALU.mult, op1=ALU.add), reads=rd + [bca], writes=[bca])
                        yield
                    S.op("act", lambda e: e.activation(out=f_[:], in_=ca[:], func=AF.Silu), reads=[bca], writes=[bf_])
                    yield
                S.dma("sp", d["s_fo"], K.F_d[ci, :, o:o + 512], f_[:], reads=[bf_], writes=[])
                yield

            run_rolling([(lambda li, ci=ci, tt=tt: ajob(li, ci, tt)) for ci in range(32) for tt in range(8)], NLA, stagger=2)
            S.barrier()
        with contextlib.ExitStack() as ph:
            sb = lambda n, shp, dt: _sb(K, ph, n, shp, dt)
            NL = 8
            junk_sh = sb("g_junk_sh", [128, 128], F32)
            pl = [_ps(K, ph, "g_pl%d" % i, [128, 512], F32) for i in range(NL)]
            L = []
            for h in range(NL):
                d = {}
                d["pb"] = PB()
                d["ps"] = pl[h]
                d["F"] = [[sb("g_F%d_%d_%d" % (h, c, i), [128, 512], BF16) for i in range(2)] for c in range(4)]
                d["bF"] = [[Buf() for i in range(2)] for c in range(4)]
                d["sF"] = [[S.new_slot() for i in range(2)] for c in range(4)]
                d["oT"] = [sb("g_oT%d_%d" % (h, i), [128, 512], BF16) for i in range(1)] * 2
                d["boT"] = [Buf()] * 2
                d["soT"] = [S.new_slot()] * 2
                d["junk"] = junk_sh
                d["b_junk"] = None
                for nm, shp, dt in (("tm", [128, 3, 128], BF16), ("sc", [128, 16], F32),
                                    ("var", [128, 7, 128], BF16), ("fmT", [128, 4, 128], BF16), ("diagG", [128, 128], F32),
                                    ("dTs", [128, 128], F32), ("dTf", [128, 128], F32),
                                    ("attnT", [128, 128], BF16), ("N0", [128, 2, 128], BF16), ("N1", [128, 2, 128], BF16),
                                    ("N2", [128, 2, 128], BF16), ("X0", [128, 2, 128], BF16), ("X1", [128, 2, 128], BF16),
                                    ("MA", [128, 5, 2, 128], BF16), ("PP", [128, 2, 128], BF16), ("u", [128, 128], F32),
                                    ("wT", [128, 128], BF16), ("vnew", [128, 128], BF16), ("S", [128, 128], F32),
                                    ("Sbf", [128, 128], BF16), ("os", [128, 128], BF16)):
                    d[nm] = sb("g_%s%d" % (nm, h), shp, dt)
                    d["b_" + nm] = Buf()
                d["b_varl"] = [Buf() for _ in range(7)]
                L.append(d)

            def load_F(h, tt):
                d = L[h]
                i = tt % 2
                for c in range(4):
                    S.dma("sp", d["sF"][c][i], d["F"][c][i][:], K.F_d[c * 8 + h, :, tt * 512:(tt + 1) * 512],
                          reads=[K.b_F], writes=[d["bF"][c][i]])

            pv2 = lambda t: t[:].rearrange("p a b -> p (a b)")

            def chunk(h, n):
                d = L[h]
                ps, pb = d["ps"], d["pb"]
                tt = n // 4
                fi = tt % 2
                o4 = (n % 4) * 128
                Fq, Fk, Fv, Fz = [d["F"][c][fi] for c in range(4)]
                bFq, bFk, bFv, bFz = [d["bF"][c][fi] for c in range(4)]
                gcol = lambda t: t[:, n, h:h + 1]
                tm, sc, var, fmT, junk = d["tm"], d["sc"], d["var"], d["fmT"], d["junk"]
                b_tm, b_sc, b_fmT, b_junk = d["b_tm"], d["b_sc"], d["b_fmT"], d["b_junk"]
                bv_ = d["b_varl"]
                S.group("pe", [lambda e, c=c, Fc=Fc: e.matmul(ps[:, c * 128:(c + 1) * 128], lhsT=Fc[:, o4:o4 + 128], rhs=K.ident_bf,
                                                            start=True, stop=True) for c, Fc in enumerate((Fq, Fk, Fv))],
                        reads=[bFq, bFk, bFv, K.b_const], writes=[pb])
                yield
                S.op("act", lambda e: e.copy(out=tm[:].rearrange("p a b -> p (a b)"), in_=ps[:, 0:384]), reads=[pb], writes=[b_tm])
                yield
                S.op("act", lambda e: e.activation(out=var[:, 0, :], in_=tm[:, 0, :], func=AF.Square, scale=128.0 ** 0.5, accum_out=sc[:, 0:1]),
                     reads=[b_tm], writes=[b_sc, bv_[0]])
                S.op("act", lambda e: e.activation(out=var[:, 1, :], in_=tm[:, 1, :], func=AF.Square, accum_out=sc[:, 1:2]),
                     reads=[b_tm], writes=[b_sc, bv_[1]])
                yield
                S.op("act", lambda e: e.activation(out=sc[:, 2:4], in_=sc[:, 0:2], func=AF.Ln, bias=K.eps1024[:, 1:2], scale=1.0),
                     reads=[b_sc, K.b_const], writes=[b_sc])
                S.op("act", lambda e: e.activation(out=sc[:, 2:4], in_=sc[:, 2:4], func=AF.Exp, scale=-0.5),
                     reads=[b_sc], writes=[b_sc])
                yield
                rq, rk = sc[:, 2:3], sc[:, 3:4]
                zb = K.eps1024[:, 3:4]
                S.op("act", lambda e: e.activation(out=var[:, 0, :], in_=tm[:, 0, :], func=AF.Identity, bias=zb, scale=rq),
                     reads=[b_tm, b_sc, K.b_const], writes=[bv_[0]])
                S.op("dve", lambda e: e.tensor_scalar(out=var[:, 1, :], in0=tm[:, 0, :], scalar1=rq, scalar2=gcol(egc), op0=ALU.mult, op1=ALU.mult),
                     reads=[b_tm, b_sc, b_g], writes=[bv_[1]])
                yield
                S.op("act", lambda e: e.activation(out=var[:, 2, :], in_=tm[:, 1, :], func=AF.Identity, bias=zb, scale=rk),
                     reads=[b_tm, b_sc, K.b_const], writes=[bv_[2]])
                S.op("dve", lambda e: e.tensor_scalar(out=var[:, 3, :], in0=tm[:, 1, :], scalar1=rk, scalar2=gcol(beta), op0=ALU.mult, op1=ALU.mult),
                     reads=[b_tm, b_sc, b_g], writes=[bv_[3]])
                yield
                S.op("act", lambda e: e.activation(out=var[:, 6, :], in_=tm[:, 2, :], func=AF.Identity, bias=zb, scale=gcol(beta)),
                     reads=[b_tm, b_g, K.b_const], writes=[bv_[6]])
                S.op("pool", lambda e: e.tensor_scalar(out=var[:, 4, :], in0=tm[:, 1, :], scalar1=rk, scalar2=gcol(bgt), op0=ALU.mult, op1=ALU.mult),
                     reads=[b_tm, b_sc, b_g], writes=[bv_[4]])
                yield
                S.op("pool", lambda e: e.tensor_scalar(out=var[:, 5, :], in0=tm[:, 1, :], scalar1=rk, scalar2=gcol(etail), op0=ALU.mult, op1=ALU.mult),
                     reads=[b_tm, b_sc, b_g], writes=[bv_[5]])
                yield
                S.group("pe", [lambda e, vi=vi: e.matmul(ps[:, vi * 128:(vi + 1) * 128], lhsT=var[:, vi, :], rhs=K.ident_bf,
                                                        start=True, stop=True) for vi in range(4)],
                        reads=bv_[0:4] + [K.b_const], writes=[pb])
                yield
                S.op("dve", lambda e: e.tensor_copy(out=fmT[:].rearrange("p a b -> p (a b)"), in_=ps[:, 0:512]),
                     reads=[pb], writes=[b_fmT])
                qhT, qdT, khT, kbT = fmT[:, 0, :], fmT[:, 1, :], fmT[:, 2, :], fmT[:, 3, :]
                S.op("act", lambda e: e.activation(out=d["diagG"][:], in_=K.ident_f, func=AF.Identity, bias=K.eps1024[:, 3:4], scale=gcol(gc)),
                     reads=[K.b_const, b_g], writes=[d["b_diagG"]])
                yield
                S.op("pe", lambda e: e.matmul(ps[:, 0:128], lhsT=K.ones_f, rhs=d["diagG"][:], start=True, stop=True),
                     reads=[d["b_diagG"], K.b_const], writes=[pb])
                yield
                S.op("dve", lambda e: e.scalar_tensor_tensor(out=d["dTs"][:], in0=ps[:, 0:128], scalar=gcol(gc), in1=K.cst_f[:, 5, :],
                                                            op0=ALU.subtract, op1=ALU.min), reads=[pb, b_g, K.b_const], writes=[d["b_dTs"]])
                yield
                S.op("act", lambda e: e.activation(out=d["dTs"][:], in_=d["dTs"][:], func=AF.Exp), reads=[d["b_dTs"]], writes=[d["b_dTs"]])
                S.group("pe", [lambda e: e.matmul(ps[:, 128:256], lhsT=khT, rhs=kbT, start=True, stop=True),
                               lambda e: e.matmul(ps[:, 256:384], lhsT=khT, rhs=qhT, start=True, stop=True)],
                        reads=[b_fmT], writes=[pb])
                yield
                S.op("pool", lambda e: e.tensor_tensor(out=d["dTf"][:], in0=d["dTs"][:], in1=K.ident_f, op=ALU.add),
                     reads=[d["b_dTs"], K.b_const], writes=[d["b_dTf"]])
                NP = d["N0"]
                S.op("dve", lambda e: e.tensor_tensor(out=NP[:, 1, :], in0=ps[:, 128:256], in1=d["dTs"][:], op=ALU.mult),
                     reads=[pb, d["b_dTs"]], writes=[d["b_N0"]])
                yield
                S.op("dve", lambda e: e.tensor_tensor(out=d["attnT"][:], in0=ps[:, 256:384], in1=d["dTf"][:], op=ALU.mult),
                     reads=[pb, d["b_dTf"]], writes=[d["b_attnT"]])
                yield
                S.op("pe", lambda e: e.matmul(ps[:, 384:512], lhsT=NP[:, 1, :], rhs=K.ident_bf, start=True, stop=True),
                     reads=[d["b_N0"], K.b_const], writes=[pb])
                yield
                S.op("act", lambda e: e.copy(out=NP[:, 0, :], in_=ps[:, 384:512]), reads=[pb], writes=[d["b_N0"]])
                yield
                Nn = [d["N0"], d["N1"], d["N2"]]
                b_N = [d["b_N0"], d["b_N1"], d["b_N2"]]
                XX = [d["X0"], d["X1"]]
                b_XX = [d["b_X0"], d["b_X1"]]
                PP = d["PP"]
                MA = d["MA"]
                S.op("pool", lambda e: e.tensor_tensor(out=MA[:], in0=NP[:].unsqueeze(1).to_broadcast([128, 5, 2, 128]),
                                                      in1=K.cst2[:, 0:5, :, :], op=ALU.mult),
                     reads=[b_N[0], K.b_const], writes=[d["b_MA"]])
                Nn[1] = MA[:, 0, :, :]
                b_N[1] = d["b_MA"]
                yield
                S.op("pool", lambda e: e.tensor_tensor(out=XX[0][:], in0=K.cst2[:, 5, :, :], in1=MA[:, 0, :, :], op=ALU.subtract),
                     reads=[K.b_const, b_N[1]], writes=[b_XX[0]])
                yield
                cx = 0
                for lv, (a, ba_, nx, bnx) in enumerate(((MA[:, 0, :, :], d["b_MA"], d["N2"], d["b_N2"]),
                                                        (d["N2"], d["b_N2"], d["N1"], d["b_N1"]))):
                    S.group("pe", [lambda e, a=a: e.matmul(ps[:, 0:128], lhsT=a[:, 1, :], rhs=a[:, 0, :], start=True, stop=True),
                                   lambda e, a=a: e.matmul(ps[:, 128:256], lhsT=a[:, 0, :], rhs=a[:, 1, :], start=True, stop=True)],
                            reads=[ba_], writes=[pb])
                    yield
                    S.op("act", lambda e, nx=nx: e.copy(out=pv2(nx), in_=ps[:, 0:256]), reads=[pb], writes=[bnx])
                    yield
                    xs, xd = XX[cx], XX[1 - cx]
                    S.group("pe", [lambda e, nx=nx, xs=xs: e.matmul(ps[:, 256:384], lhsT=nx[:, 1, :], rhs=xs[:, 0, :], start=True, stop=True),
                                   lambda e, nx=nx, xs=xs: e.matmul(ps[:, 384:512], lhsT=nx[:, 0, :], rhs=xs[:, 1, :], start=True, stop=True)],
                            reads=[bnx, b_XX[cx]], writes=[pb])
                    yield
                    S.op("dve", lambda e, xs=xs, xd=xd: e.tensor_tensor(out=pv2(xd), in0=ps[:, 256:512], in1=pv2(xs), op=ALU.add),
                         reads=[pb, b_XX[cx]], writes=[b_XX[1 - cx]])
                    yield
                    cx = 1 - cx
                for li in range(4):
                    xs, xd = XX[cx], XX[1 - cx]
                    S.group("pe", [lambda e, xs=xs, li=li: e.matmul(ps[:, 0:128], lhsT=MA[:, 1 + li, 0, :], rhs=xs[:, 1, :], start=True, stop=True),
                                   lambda e, xs=xs, li=li: e.matmul(ps[:, 128:256], lhsT=MA[:, 1 + li, 1, :], rhs=xs[:, 0, :], start=True, stop=True)],
                            reads=[d["b_MA"], b_XX[cx]], writes=[pb])
                    yield
                    S.op("act", lambda e: e.copy(out=pv2(PP), in_=ps[:, 0:256]), reads=[pb], writes=[d["b_PP"]])
                    yield
                    S.group("pe", [lambda e, xs=xs: e.matmul(ps[:, 256:384], lhsT=xs[:, 1, :], rhs=PP[:, 1, :], start=True, stop=True),
                                   lambda e, xs=xs: e.matmul(ps[:, 384:512], lhsT=xs[:, 0, :], rhs=PP[:, 0, :], start=True, stop=True)],
                            reads=[d["b_PP"], b_XX[cx]], writes=[pb])
                    yield
                    S.op("dve", lambda e, xs=xs, xd=xd: e.tensor_tensor(out=pv2(xd), in0=pv2(xs), in1=ps[:, 256:512], op=ALU.subtract),
                         reads=[pb, b_XX[cx]], writes=[b_XX[1 - cx]])
                    yield
                    cx = 1 - cx
                XTf = XX[cx][:, 1, :]
                bXTf = b_XX[cx]
                S.group("pe", [lambda e: e.matmul(ps[:, 0:128], lhsT=XTf, rhs=var[:, 6, :], start=True, stop=True),
                               lambda e: e.matmul(ps[:, 128:256], lhsT=var[:, 4, :], rhs=XTf, start=True, stop=True)],
                        reads=[bXTf, bv_[6], bv_[4]], writes=[pb])
                yield
                wT, vnew, Sst, Sbf, u_sb = d["wT"], d["vnew"], d["S"], d["Sbf"], d["u"]
                S.op("act", lambda e: e.copy(out=wT[:], in_=ps[:, 128:256]), reads=[pb], writes=[d["b_wT"]])
                if n == 0:
                    S.op("dve", lambda e: e.tensor_copy(out=vnew[:], in_=ps[:, 0:128]), reads=[pb], writes=[d["b_vnew"]])
                    yield
                    S.group("pe", [lambda e: e.matmul(ps[:, 128:256], lhsT=d["attnT"][:], rhs=vnew[:], start=True, stop=True),
                                   lambda e: e.matmul(ps[:, 256:384], lhsT=var[:, 5, :], rhs=vnew[:], start=True, stop=True)],
                            reads=[d["b_attnT"], d["b_vnew"], bv_[5]], writes=[pb])
                    yield
                    S.op("dve", lambda e: e.tensor_copy(out=Sst[:], in_=ps[:, 256:384]), reads=[pb], writes=[d["b_S"]])
                else:
                    S.op("dve", lambda e: e.tensor_copy(out=u_sb[:], in_=ps[:, 0:128]), reads=[pb], writes=[d["b_u"]])
                    yield
                    S.op("pe", lambda e: e.matmul(ps[:, 0:128], lhsT=wT[:], rhs=Sbf[:], start=True, stop=True),
                         reads=[d["b_wT"], d["b_S"]], writes=[pb])
                    yield
                    S.op("dve", lambda e: e.tensor_tensor(out=vnew[:], in0=u_sb[:], in1=ps[:, 0:128], op=ALU.subtract),
                         reads=[d["b_u"], pb], writes=[d["b_vnew"]])
                    yield
                    S.group("pe", [lambda e: e.matmul(ps[:, 128:256], lhsT=qdT, rhs=Sbf[:], start=True, stop=False),
                                   lambda e: e.matmul(ps[:, 128:256], lhsT=d["attnT"][:], rhs=vnew[:], start=False, stop=True),
                                   lambda e: e.matmul(ps[:, 256:384], lhsT=var[:, 5, :], rhs=vnew[:], start=True, stop=True)],
                            reads=[b_fmT, d["b_S"], d["b_attnT"], d["b_vnew"], bv_[5]], writes=[pb])
                    yield
                    S.op("dve", lambda e: e.scalar_tensor_tensor(out=Sst[:], in0=Sst[:], scalar=gcol(glast), in1=ps[:, 256:384],
                                                                op0=ALU.mult, op1=ALU.add), reads=[d["b_S"], b_g, pb], writes=[d["b_S"]])
                yield
                if n < NB - 1:
                    S.op("pool", lambda e: e.tensor_copy(out=Sbf[:], in_=Sst[:]), reads=[d["b_S"]], writes=[d["b_S"]])
                S.op("act", lambda e: e.activation(out=d["os"][:], in_=ps[:, 128:256], func=AF.Square, accum_out=sc[:, 9:10]),
                     reads=[pb], writes=[b_sc, d["b_os"]])
                yield
                S.op("act", lambda e: e.activation(out=sc[:, 10:11], in_=sc[:, 9:10], func=AF.Ln, bias=K.eps1024[:, 1:2], scale=1.0 / 128.0),
                     reads=[b_sc, K.b_const], writes=[b_sc])
                S.op("act", lambda e: e.activation(out=sc[:, 10:11], in_=sc[:, 10:11], func=AF.Exp, scale=-0.5),
                     reads=[b_sc], writes=[b_sc])
                yield
                S.op("dve", lambda e: e.tensor_scalar(out=d["os"][:], in0=ps[:, 128:256], scalar1=sc[:, 10:11], scalar2=None, op0=ALU.mult),
                     reads=[pb, b_sc], writes=[d["b_os"]])
                yield
                S.op("pe", lambda e: e.matmul(ps[:, 384:512], lhsT=d["os"][:], rhs=K.ident_bf, start=True, stop=True),
                     reads=[d["b_os"], K.b_const], writes=[pb])
                yield
                oi = tt % 2
                S.op("dve", lambda e: e.scalar_tensor_tensor(out=d["oT"][oi][:, o4:o4 + 128], in0=ps[:, 384:512], scalar=ng[:, j:j + 1],
                                                            in1=Fz[:, o4:o4 + 128], op0=ALU.mult, op1=ALU.mult),
                     reads=[pb, b_cst, bFz], writes=[d["boT"][oi]])
                if n % 4 == 3:
                    S.dma("sp", d["soT"][oi], K.oT_d[:, h, tt * 512:(tt + 1) * 512], d["oT"][oi][:], reads=[d["boT"][oi]],
                          writes=[K.b_oT[2 * tt], K.b_oT[2 * tt + 1]])
                yield

            def head_chain(h):
                load_F(h, 0)
                for n in range(NB):
                    if n % 4 == 0 and n // 4 + 1 < 8:
                        load_F(h, n // 4 + 1)
                    yield from chunk(h, n)

            run_rolling([(lambda li: head_chain(li)) for _ in range(NL)], NL, stagger=5)
            S.barrier()


MIXERS["gdn"] = phase_gdn

def run_lanes(gens):
    live = list(gens)
    while live:
        nxt = []
        for g_ in live:
            try:
                next(g_)
                nxt.append(g_)
            except StopIteration:
                pass
        live = nxt


def run_rolling(jobs, nl, stagger=1):
    jobs = list(jobs)
    active = {}
    nxt = 0
    step = 0
    while nxt < len(jobs) or active:
        for li in range(nl):
            if li not in active and nxt < len(jobs) and step >= li * stagger:
                active[li] = jobs[nxt](li)
                nxt += 1
        for li in list(active.keys()):
            try:
                next(active[li])
            except StopIteration:
                del active[li]
        step += 1


def phase_dsw(K, l, j):
    nc, S = K.nc, K.S
    NB = 32
    SC = 128.0 ** -0.5
    NL = 7
    with contextlib.ExitStack() as ph:
        sb = lambda n, shp, dt: _sb(K, ph, n, shp, dt)
        hT = sb("d_hT", [128, KC, T], BF16)
        b_hT = [Buf() for _ in range(KC)]
        for kc in range(KC):
            S.dma("sp", S.new_slot(), hT[:, kc, :], K.hT_d[:, kc, :], reads=K.b_hT, writes=[b_hT[kc]])
        gain = sb("d_gain", [128, 6, 128], F32)
        rope = sb("d_rope", [128, 2, 32, 16], F32)
        b_rope = Buf()
        s_rope = S.new_slot()
        mask2 = sb("d_mask2", [128, 2, 256], BF16)
        b_cst = Buf()
        scst = S.new_slot()
        S.dma("sp", scst, gain[:], K.inp["dsw_gain"][:, j, :, :], writes=[b_cst])
        for hd in range(2):
            S.op("pool", lambda e, hd=hd: e.tensor_copy(out=mask2[:, hd, 0:128], in_=K.cst_f[:, 2, :]), reads=[K.b_const], writes=[b_cst])
            S.op("pool", lambda e, hd=hd: e.tensor_copy(out=mask2[:, hd, 128:256], in_=K.cst_f[:, 4, :]), reads=[K.b_const], writes=[b_cst])
        gabs = sb("d_gabs", [128, 6, 128], F32)
        gmx = sb("d_gmx", [128, 16], F32)
        b_gm = Buf()
        S.op("act", lambda e: e.activation(out=gabs[:], in_=gain[:], func=AF.Abs), reads=[b_cst], writes=[b_gm])
        S.op("dve", lambda e: e.tensor_reduce(out=gmx[:, 0:6], in_=gabs[:], axis=mybir.AxisListType.X, op=ALU.max),
             reads=[b_gm], writes=[b_gm])
        S.op("dve", lambda e: e.tensor_tensor(out=gmx[:, 8:11], in0=gmx[:, 0:6:2], in1=gmx[:, 1:6:2], op=ALU.mult),
             reads=[b_gm], writes=[b_gm])
        S.op("dve", lambda e: e.tensor_reduce(out=gmx[:, 12:13], in_=gmx[:, 8:11], axis=mybir.AxisListType.X, op=ALU.max),
             reads=[b_gm], writes=[b_gm])
        S.op("dve", lambda e: e.tensor_scalar(out=gmx[:, 13:14], in0=gmx[:, 12:13], scalar1=-(128.0 ** 0.5), scalar2=None, op0=ALU.mult),
             reads=[b_gm], writes=[b_gm])
        negM = gmx[:, 13:14]
        pA = [_ps(K, ph, "d_pA%d" % i, [128, 512], F32) for i in range(NL)]
        b_pA = [PB() for _ in range(NL)]
        pB = [pA[i][:].rearrange("p (a c) -> p a c", c=256) for i in range(NL)]
        b_pB = b_pA
        wsl = [sb("d_wsl%d" % i, [128, KC, 512], BF16) for i in range(2)]
        b_wsl = [Buf(), Buf()]
        s_wsl = [S.new_slot(sw=True), S.new_slot(sw=True)]
        qT = sb("d_qT", [128, 4, NB, 128], BF16)
        kT = sb("d_kT", [128, 4, NB, 128], BF16)
        b_qT = [Buf() for _ in range(NB)]
        b_kT = [Buf() for _ in range(NB)]
        Va = sb("d_Va", [128, NB, 4, 130], BF16)
        b_Va = [Buf() for _ in range(NB)]
        S.op("pool", lambda e: e.memset(Va[:].rearrange("p a b c -> p (a b c)"), 1.0), writes=b_Va)
        junk = sb("d_junk", [128, 128], F32)
        LA = []
        for i in range(NL):
            ypt = sb("d_ypt%d" % i, [128, 512], BF16)
            rst = sb("d_rst%d" % i, [128, 264], F32)
            b_ypt, b_rst = Buf(), Buf()
            d = {"ss": sb("d_ss%d" % i, [128, 8], F32), "b_ss": Buf(),
                 "y": ypt[:].rearrange("p (a c) -> p a c", c=128), "b_y": b_ypt,
                 "rt": rst[:, 0:256].rearrange("p (t a c) -> p t a c", t=2, a=4), "b_rt": b_rst,
                 "pT": ypt[:].rearrange("p (a c) -> p a c", c=256), "b_pT": b_ypt,
                 "st": rst[:].rearrange("p (a c) -> p a c", c=132), "b_st": b_rst, "s_st": S.new_slot()}
            LA.append(d)
        wv = K.inp["dsw_w_in"][j].rearrange("(kc p) n -> p kc n", p=128)
        slabs = [(g, hh, t3) for g in range(3) for hh in range(2) for t3 in range(3)]

        def load_slab(si):
            g, hh, t3 = slabs[si]
            col0 = ((g * 3 + t3) * 8 + hh * 4) * 128
            S.dma("pool", s_wsl[si % 2], wsl[si % 2][:], wv[:, :, col0:col0 + 512], writes=[b_wsl[si % 2]])

        def proj_blk(li, si, b):
            g, hh, t3 = slabs[si]
            dl = LA[li]
            dd = DSW_DIL[g]
            nsb = NB // dd
            r, sbk = b // nsb, b % nsb
            start = r + dd * sbk * 128
            W = wsl[si % 2]
            pa, bpa = pA[li], b_pA[li]
            S.group("pe", [lambda e, kc=kc: e.matmul(pa[:], lhsT=hT[:, kc, start:start + 127 * dd + 1:dd], rhs=W[:, kc, :],
                                                    start=(kc == 0), stop=(kc == KC - 1)) for kc in range(KC)],
                    reads=b_hT + [b_wsl[si % 2]], writes=[bpa])
            yield
            if t3 == 2:
                S.op("act", lambda e: e.copy(out=Va[:, b, :, 0:128], in_=pa[:].rearrange("p (a c) -> p a c", c=128)),
                     reads=[bpa], writes=[b_Va[b]])
                return
            ss, y, rt = dl["ss"], dl["y"], dl["rt"]
            for hd in range(4):
                S.op("act", lambda e, hd=hd: e.activation(out=y[:, hd, :], in_=pa[:, hd * 128:(hd + 1) * 128], func=AF.Square,
                                                         accum_out=ss[:, hd:hd + 1]), reads=[bpa], writes=[dl["b_ss"], dl["b_y"]])
                if hd == 1:
                    yield
            yield
            S.op("act", lambda e: e.activation(out=ss[:, 4:8], in_=ss[:, 0:4], func=AF.Ln, bias=K.eps1024[:, 1:2], scale=1.0 / 128.0),
                 reads=[dl["b_ss"], K.b_const], writes=[dl["b_ss"]])
            S.op("act", lambda e: e.activation(out=ss[:, 4:8], in_=ss[:, 4:8], func=AF.Exp, scale=-0.5), reads=[dl["b_ss"]], writes=[dl["b_ss"]])
            yield
            for hd in range(4):
                S.op("dve", lambda e, hd=hd: e.scalar_tensor_tensor(
                    out=y[:, hd, :], in0=pa[:, hd * 128:(hd + 1) * 128], scalar=ss[:, 4 + hd:5 + hd], in1=gain[:, g * 2 + t3, :],
                    op0=ALU.mult, op1=ALU.mult), reads=[bpa, dl["b_ss"], b_cst], writes=[dl["b_y"]])
                if hd == 1:
                    yield
            yield
            cs2 = rope[:, 0, b, :].unsqueeze(1).unsqueeze(1).to_broadcast([128, 4, 2, 16])
            sn2 = rope[:, 1, b, :].unsqueeze(1).unsqueeze(1).to_broadcast([128, 4, 2, 16])
            y32 = y[:, :, 0:32].rearrange("p a (t c) -> p a t c", t=2)
            S.op("dve", lambda e: e.tensor_tensor(out=rt[:, 0, :, :].rearrange("p a (t c) -> p a t c", t=2), in0=y32, in1=cs2, op=ALU.mult),
                 reads=[dl["b_y"], b_rope], writes=[dl["b_rt"]])
            S.op("dve", lambda e: e.tensor_tensor(out=rt[:, 1, :, :].rearrange("p a (t c) -> p a t c", t=2), in0=y32, in1=sn2, op=ALU.mult),
                 reads=[dl["b_y"], b_rope], writes=[dl["b_rt"]])
            yield
            S.op("dve", lambda e: e.tensor_tensor(out=y[:, :, 0:16], in0=rt[:, 0, :, 0:16], in1=rt[:, 1, :, 16:32], op=ALU.subtract),
                 reads=[dl["b_rt"]], writes=[dl["b_y"]])
            S.op("dve", lambda e: e.tensor_tensor(out=y[:, :, 16:32], in0=rt[:, 0, :, 16:32], in1=rt[:, 1, :, 0:16], op=ALU.add),
                 reads=[dl["b_rt"]], writes=[dl["b_y"]])
            yield
            S.group("pe", [lambda e, hd=hd: e.matmul(pa[:, hd * 128:(hd + 1) * 128], lhsT=y[:, hd, :], rhs=K.ident_bf, start=True, stop=True)
                           for hd in range(4)], reads=[dl["b_y"], K.b_const], writes=[bpa])
            yield
            dst = qT if t3 == 0 else kT
            bd = b_qT[b] if t3 == 0 else b_kT[b]
            S.op("act", lambda e: e.copy(out=dst[:, :, b, :], in_=pa[:].rearrange("p (a c) -> p a c", c=128)), reads=[bpa], writes=[bd])
            yield

        def att_unit(li, g, hh, b, pr):
            dl = LA[li]
            dd = DSW_DIL[g]
            nsb = NB // dd
            r, sbk = b // nsb, b % nsb
            has_prev = sbk > 0
            ps_, bps = pB[li], b_pB[li]
            fns = []
            for h2 in range(2):
                hd = pr * 2 + h2
                fns.append(lambda e, hd=hd, h2=h2: e.matmul(ps_[:, h2, 0:128], lhsT=kT[:, hd, b, :], rhs=qT[:, hd, b, :], start=True, stop=True))
                if has_prev:
                    fns.append(lambda e, hd=hd, h2=h2: e.matmul(ps_[:, h2, 128:256], lhsT=kT[:, hd, b - 1, :], rhs=qT[:, hd, b, :],
                                                               start=True, stop=True))
            S.group("pe", fns, reads=[b_qT[b], b_kT[b]] + ([b_kT[b - 1]] if has_prev else []), writes=[bps])
            yield
            p_, bp = dl["pT"], dl["b_pT"]
            wd = 256 if has_prev else 128
            S.op("act", lambda e: e.activation(out=p_[:, :, 0:wd], in_=ps_[:, :, 0:wd], func=AF.Exp, bias=negM, scale=SC),
                 reads=[bps, b_gm], writes=[bp])
            yield
            S.op("dve", lambda e: e.tensor_tensor(out=p_[:, :, 0:wd], in0=p_[:, :, 0:wd], in1=mask2[:, :, 0:wd], op=ALU.mult),
                 reads=[bp, b_cst], writes=[bp])
            yield
            fns = []
            for h2 in range(2):
                hd = pr * 2 + h2
                fns.append(lambda e, hd=hd, h2=h2: e.matmul(ps_[:, h2, 0:129], lhsT=p_[:, h2, 0:128], rhs=Va[:, b, hd, 0:129],
                                                           start=True, stop=not has_prev, skip_group_check=True))
                if has_prev:
                    fns.append(lambda e, hd=hd, h2=h2: e.matmul(ps_[:, h2, 0:129], lhsT=p_[:, h2, 128:256], rhs=Va[:, b - 1, hd, 0:129],
                                                               start=False, stop=True, skip_group_check=True))
            S.group("pe", fns, reads=[bp, b_Va[b]] + ([b_Va[b - 1]] if has_prev else []), writes=[bps])
            yield
            st_, bst = dl["st"], dl["b_st"]
            S.op("dve", lambda e: e.tensor_copy(out=st_[:, :, 0:129], in_=ps_[:, :, 0:129]), reads=[bps], writes=[bst])
            tok0 = r + dd * sbk * 128
            h0 = hh * 4 + pr * 2
            S.dma("sp", dl["s_st"], K.num_d[g, tok0:tok0 + 127 * dd + 1:dd, h0:h0 + 2, :], st_, reads=[bst], writes=[])
            yield

        load_slab(0)
        for si in range(len(slabs)):
            g, hh, t3 = slabs[si]
            if si + 1 < len(slabs):
                load_slab(si + 1)
            if hh == 0 and t3 == 0:
                S.dma("sp", s_rope, rope[:], K.inp["rope"][:, g, :, :, :], writes=[b_rope])
            run_rolling([(lambda li, b=b, si=si: proj_blk(li, si, b)) for b in range(NB)], NL, stagger=3)
            if t3 == 2:
                units = [(b, pr) for b in range(NB) for pr in range(2)]
                run_rolling([(lambda li, b=b, pr=pr, g=g, hh=hh: att_unit(li, g, hh, b, pr)) for (b, pr) in units], NL, stagger=1)
        S.barrier()
    with contextlib.ExitStack() as ph:
        sb = lambda n, shp, dt: _sb(K, ph, n, shp, dt)
        NC_ = 4
        LB = []
        for i in range(NC_):
            d = {"nin": [sb("d_nin%d_%d" % (i, g), [128, 8, 132], F32) for g in range(3)], "b_nin": [Buf() for g in range(3)],
                 "s_nin": [S.new_slot() for g in range(3)], "rden": sb("d_rden%d" % i, [128, 8], F32), "b_rden": Buf(),
                 "otm": sb("d_otm%d" % i, [128, 8, 128], BF16), "b_otm": Buf(),
                 "ps": _ps(K, ph, "d_pc%d_a" % i, [128, 512], F32), "ps2": _ps(K, ph, "d_pc%d_b" % i, [128, 512], F32),
                 "b_ps": PB(), "b_ps2": PB(),
                 "ot": sb("d_ot%d" % i, [128, KC, 128], BF16), "b_ot": Buf(), "s_o": S.new_slot()}
            LB.append(d)

        def comb(li, blk):
            d = LB[li]
            nin, bn = d["nin"], d["b_nin"]
            for g in range(3):
                S.dma("sp", d["s_nin"][g], nin[g][:], K.num_d[g, blk * 128:(blk + 1) * 128, :, :], reads=[K.b_num], writes=[bn[g]])
            yield
            S.op("dve", lambda e: e.tensor_tensor(out=nin[0][:], in0=nin[0][:], in1=nin[1][:], op=ALU.add), reads=[bn[0], bn[1]], writes=[bn[0]])
            yield
            S.op("dve", lambda e: e.tensor_tensor(out=nin[0][:], in0=nin[0][:], in1=nin[2][:], op=ALU.add), reads=[bn[0], bn[2]], writes=[bn[0]])
            yield
            S.op("dve", lambda e: e.reciprocal(out=d["rden"][:].unsqueeze(2), in_=nin[0][:, :, 128:129]), reads=[bn[0]], writes=[d["b_rden"]])
            yield
            S.op("dve", lambda e: e.tensor_tensor(out=d["otm"][:], in0=nin[0][:, :, 0:128],
                                                 in1=d["rden"][:].unsqueeze(2).to_broadcast([128, 8, 128]), op=ALU.mult),
                 reads=[bn[0], d["b_rden"]], writes=[d["b_otm"]])
            yield
            S.group("pe", [lambda e, hd=hd: e.matmul(d["ps"][:, hd * 128:(hd + 1) * 128], lhsT=d["otm"][:, hd, :], rhs=K.ident_bf,
                                                    start=True, stop=True) for hd in range(4)], reads=[d["b_otm"], K.b_const], writes=[d["b_ps"]])
            S.group("pe", [lambda e, hd=hd: e.matmul(d["ps2"][:, hd * 128:(hd + 1) * 128], lhsT=d["otm"][:, 4 + hd, :], rhs=K.ident_bf,
                                                    start=True, stop=True) for hd in range(4)], reads=[d["b_otm"], K.b_const], writes=[d["b_ps2"]])
            yield
            S.op("act", lambda e: e.copy(out=d["ot"][:, 0:4, :], in_=d["ps"][:].rearrange("p (a c) -> p a c", c=128)), reads=[d["b_ps"]], writes=[d["b_ot"]])
            yield
            S.op("act", lambda e: e.copy(out=d["ot"][:, 4:8, :], in_=d["ps2"][:].rearrange("p (a c) -> p a c", c=128)), reads=[d["b_ps2"]], writes=[d["b_ot"]])
            S.dma("sp", d["s_o"], K.oT_d[:, :, blk * 128:(blk + 1) * 128], d["ot"][:], reads=[d["b_ot"]], writes=[K.b_oT[blk // 2]])
            yield

        run_rolling([(lambda li, blk=blk: comb(li, blk)) for blk in range(T // 128)], NC_, stagger=2)
        S.barrier()


MIXERS["dsw"] = phase_dsw


def build(n_layers=DEPTH, mixers=True, dbg=None, mixer_seq=None):
    nc = bass.Bass("TRN2", target_bir_lowering=False)
    K = Ctx()
    K.nc = nc
    K.uid = 0
    inp = {}

    def di(name, shape, dt=F32):
        inp[name] = nc.dram_tensor(name, shape, dt, kind="ExternalInput").ap()

    di("x", [T, D])
    di("cT", [128, KC])
    di("mod_w", [DEPTH, D, 6 * D])
    di("modb", [128, DEPTH, 48])
    di("mixg", [128, DEPTH, KC])
    di("ffng", [128, DEPTH, KC])
    di("gdn_w_in", [2, D, GDN_IN])
    di("gdn_conv", [128, 2, 24, 4])
    di("gdn_hc", [128, 2, 2, 8])
    di("gdn_ng", [128, 2])
    di("gdn_w_out", [2, D, D])
    di("dsw_w_in", [2, D, DSW_IN])
    di("dsw_gain", [128, 2, 6, 128])
    di("dsw_w_out", [2, D, D])
    di("ffn_w_gate_up", [DEPTH, D, 2 * FH])
    di("ffn_w_down", [DEPTH, FH, D])
    di("cst", [128, 6, 128])
    di("rope", [128, 3, 2, 32, 16])
    di("cst2", [128, 6, 2, 128])
    K.inp = inp
    K.out = {"y": nc.dram_tensor("y", [T, D], F32, kind="ExternalOutput").ap()}
    sk = "ExternalOutput" if dbg is not None else "Internal"
    K.xT_d = nc.dram_tensor("xT_d", [128, KC, T], F32, kind=sk).ap()
    K.hT_d = nc.dram_tensor("hT_d", [128, KC, T], BF16, kind=sk).ap()
    K.oT_d = nc.dram_tensor("oT_d", [128, KC, T], BF16, kind=sk).ap()
    K.h2T_d = nc.dram_tensor("h2T_d", [128, KC, T], BF16, kind=sk).ap()
    K.num_d = nc.dram_tensor("num_d", [3, T, 8, 132], F32, kind="Internal").ap()
    K.F_d = nc.dram_tensor("F_d", [32, 128, T], BF16, kind="Internal").ap()
    K.b_F = Buf()
    K.b_xT = [Buf() for _ in range(NT)]
    K.b_hT = [Buf() for _ in range(NT)]
    K.b_oT = [Buf() for _ in range(NT)]
    K.b_h2T = [Buf() for _ in range(NT)]
    K.b_num = Buf()
    K.b_y = Buf()
    K.b_ys = [Buf(), Buf()]
    K.dbg = dbg
    if dbg:
        for nm, (shape, dt) in dbg.items():
            K.out[nm] = nc.dram_tensor(nm, shape, dt, kind="ExternalOutput").ap()

    with contextlib.ExitStack() as st:
        S = Sched(nc, st)
        K.S = S
        K.st = st
        K.cst_f = st.enter_context(nc.sbuf_tensor("cst_f", [128, 6, 128], F32))
        K.cst_b = st.enter_context(nc.sbuf_tensor("cst_b", [128, 6, 128], BF16))
        K.eps1024 = st.enter_context(nc.sbuf_tensor("eps1024", [128, 4], F32))
        K.b_const = Buf()
        s0 = S.new_slot()
        s0w = S.new_slot(sw=True)
        K.b_const2 = Buf()
        S.dma("pool", s0w, K.cst_b[:], inp["cst"], writes=[K.b_const2])
        S.dma("sp", s0, K.cst_f[:], inp["cst"], writes=[K.b_const])
        S.op("dve", lambda e: e.memset(K.eps1024[:, 0:1], 1024.0 * EPS), writes=[K.b_const])
        S.op("dve", lambda e: e.memset(K.eps1024[:, 1:2], EPS), writes=[K.b_const])
        S.op("dve", lambda e: e.memset(K.eps1024[:, 2:3], 1.0), writes=[K.b_const])
        S.op("dve", lambda e: e.memset(K.eps1024[:, 3:4], 0.0), writes=[K.b_const])
        S.barrier()
        K.ident_f = K.cst_f[:, 0, :]
        K.ones_f = K.cst_f[:, 1, :]
        K.triU_f = K.cst_f[:, 2, :]
        K.triUs_f = K.cst_f[:, 3, :]
        K.ident_bf = K.cst_b[:, 0, :]
        K.ones_bf = K.cst_b[:, 1, :]
        K.vec = {nm: st.enter_context(nc.sbuf_tensor("v_" + nm, [128, DEPTH, KC], F32))
                 for nm in ("gs1", "sh1", "g1", "gs2", "sh2", "g2")}
        K.b_vec = Buf()
        K.b_vecl = [Buf() for _ in range(DEPTH)]

        phase_x0(K)
        for l in range(n_layers):
            j = l // 2
            mix = mixer_seq[l] if mixer_seq else ("gdn" if l % 2 == 0 else "dsw")
            if mixers:
                MIXERS[mix](K, l, j)
            else:
                stub_mixer(K)
            with contextlib.ExitStack() as wsc:
                wgu = _sb(K, wsc, "wgu", [128, KC, 2 * FH], BF16)
                b_wgu = [Buf() for _ in range(KC)]
                gv = inp["ffn_w_gate_up"][l].rearrange("(kc p) n -> p kc n", p=128)
                for kc in range(KC):
                    S.dma("pool", S.new_slot(sw=True), wgu[:, kc, :], gv[:, kc, :], writes=[b_wgu[kc]])
                phase_t1(K, l, inp["gdn_w_out"][j] if mix == "gdn" else inp["dsw_w_out"][j])
                phase_t2(K, l, inp["ffn_w_gate_up"][l], inp["ffn_w_down"][l], last=(l == n_layers - 1), pre=(wgu, b_wgu))
        deps = [K.b_ys[0].w, K.b_ys[1].w]
        S._wait("sp", deps)
        K.ninst = S.ninst
    return nc, K


def stub_mixer(K):
    S = K.S
    with contextlib.ExitStack() as ph:
        tb = [_sb(K, ph, "stub%d" % i, [128, KC, TW], BF16) for i in range(2)]
        bt = [Buf(), Buf()]
        sl = [S.new_slot(), S.new_slot()]
        so2_ = [S.new_slot(), S.new_slot()]
        for t in range(NT):
            i = t % 2
            S.dma("sp", sl[i], tb[i][:], K.hT_d[:, :, t * TW:(t + 1) * TW], reads=[K.b_hT[t]], writes=[bt[i]])
            S.dma("sp", so2_[i], K.oT_d[:, :, t * TW:(t + 1) * TW], tb[i][:], reads=[bt[i]], writes=[K.b_oT[t]])
        S.barrier()


def _fm(v):
    v = np.asarray(v, np.float32)
    lead = v.shape[:-1]
    r = v.reshape(lead + (KC, 128))
    r = np.moveaxis(r, -1, 0)
    return np.ascontiguousarray(r)


def host_consts():
    cst = np.zeros((128, 6, 128), np.float32)
    p = np.arange(128)[:, None]
    f = np.arange(128)[None, :]
    cst[:, 0] = (p == f)
    cst[:, 1] = 1.0
    cst[:, 2] = (f >= p)
    cst[:, 3] = (f > p)
    cst[:, 4] = (f <= p)
    cst[:, 5] = np.where(f > p, 0.0, -30000.0)
    half = 8 * 2
    inv = np.exp(-math.log(ROPE_THETA) * (2.0 * np.arange(16, dtype=np.float32) / 32.0)).astype(np.float32)
    rope = np.zeros((128, 3, 2, 32, 16), np.float32)
    for g, d in enumerate(DSW_DIL):
        nsb = (T // d) // 128
        for b in range(32):
            r = b // nsb
            sb = b % nsb
            pos = (r + d * (sb * 128 + np.arange(128))).astype(np.float32)
            ang = (pos[:, None] * inv[None, :]).astype(np.float32)
            rope[:, g, 0, b, :] = np.cos(ang)
            rope[:, g, 1, b, :] = np.sin(ang)
    cst2 = np.zeros((128, 6, 2, 128), np.float32)
    bd = (p // 8 == f // 8).astype(np.float32)
    cst2[:, 0, 0] = bd
    cst2[:, 0, 1] = bd
    for li, sz in enumerate((16, 32, 64, 128)):
        em = ((p // sz == f // sz) & (p % sz >= sz // 2) & (f % sz < sz // 2)).astype(np.float32)
        cst2[:, 1 + li, 0] = em
        cst2[:, 1 + li, 1] = em.T
    cst2[:, 5, 0] = (p == f)
    cst2[:, 5, 1] = (p == f)
    return cst, rope, cst2


def make_in_maps(inputs, n_cores=8):
    f = lambda a: np.ascontiguousarray(np.asarray(a, np.float32))
    cst, rope, cst2 = host_consts()
    mod_b = f(inputs["mod_b"])
    modb = np.ascontiguousarray(np.moveaxis(mod_b.reshape(DEPTH, 48, 128), -1, 0))
    mixg = np.ascontiguousarray(np.moveaxis(f(inputs["mix_norm_g"]).reshape(DEPTH, KC, 128), -1, 0))
    ffng = np.ascontiguousarray(np.moveaxis(f(inputs["ffn_norm_g"]).reshape(DEPTH, KC, 128), -1, 0))
    conv = f(inputs["gdn_conv_w"])
    gdn_conv = np.ascontiguousarray(np.transpose(conv.reshape(2, 4, 24, 128), (3, 0, 2, 1)))
    hc = np.stack([f(inputs["gdn_A_log"]), f(inputs["gdn_dt_bias"])], axis=1)
    gdn_hc = np.ascontiguousarray(np.broadcast_to(hc[None], (128, 2, 2, 8)))
    gdn_ng = np.ascontiguousarray(f(inputs["gdn_norm_g"]).T)
    qg = f(inputs["dsw_q_norm_g"])
    kg = f(inputs["dsw_k_norm_g"])
    gain = np.stack([qg, kg], axis=2).reshape(2, 6, 128)
    dsw_gain = np.ascontiguousarray(np.broadcast_to(gain[None], (128, 2, 6, 128)))
    shared = {
        "mod_w": f(inputs["mod_w"]), "modb": modb, "mixg": mixg, "ffng": ffng,
        "gdn_w_in": f(inputs["gdn_w_in"]), "gdn_conv": gdn_conv, "gdn_hc": gdn_hc, "gdn_ng": gdn_ng,
        "gdn_w_out": f(inputs["gdn_w_out"]), "dsw_w_in": f(inputs["dsw_w_in"]), "dsw_gain": dsw_gain,
        "dsw_w_out": f(inputs["dsw_w_out"]), "ffn_w_gate_up": f(inputs["ffn_w_gate_up"]),
        "ffn_w_down": f(inputs["ffn_w_down"]), "cst": cst, "rope": rope, "cst2": cst2,
    }
    x = f(inputs["x"])
    c = f(inputs["c"])
    maps = []
    for i in range(n_cores):
        b = i % 4
        m = dict(shared)
        m["x"] = np.ascontiguousarray(x[b])
        m["cT"] = np.ascontiguousarray(c[b].reshape(KC, 128).T)
        maps.append(m)
    return maps


def kernel(**inputs):
    nc, K = build()
    maps = make_in_maps(inputs, 8)
    res = run_bass_kernel_spmd(nc, maps, core_ids=list(range(8)))
    out = np.stack([np.asarray(res.results[b]["y"], np.float32) for b in range(4)], axis=0)
    return out
```
